# Optimizing a Trainium2 kernel written in Bass

```python
import math
import jax
import jax.numpy as jnp
from jax import lax
import numpy as np

D_MODEL = 1024
BATCH = 1
SEQ = 16384
DEPTH = 1
DEC_BATCH = 8
DEC_SEQ = 2048
PAST_LEN = 128

DA_HEADS = 4
DA_HEAD_DIM = 64
DA_V_DIM = 2 * DA_HEAD_DIM
DA_WIDTH = DA_HEADS * DA_V_DIM
RW_WIDTH = D_MODEL - DA_WIDTH
RW_HEAD = 64
RW_HEADS = RW_WIDTH // RW_HEAD
DECAY_LORA = 64
ICLR_LORA = 64
GATE_LORA = 128
D_FF = 4 * D_MODEL
ROPE_THETA = 10000.0
Q_BLOCK = 128
NORM_EPS = 1e-6
LN_X_EPS = 64e-5
DA_QK_COLS = DA_HEADS * 2 * DA_HEAD_DIM
DA_COLS = 2 * DA_QK_COLS + DA_WIDTH
RW_COLS = 3 * RW_WIDTH + DECAY_LORA + ICLR_LORA + GATE_LORA
IN_COLS = DA_COLS + RW_COLS

kernel_name = "hymba_diffattn_birwkv7_encoder"


def rms_norm(x, g, eps=NORM_EPS):
    xf = x.astype(jnp.float32)
    y = xf * lax.rsqrt(jnp.mean(xf * xf, axis=-1, keepdims=True) + eps)
    return (y * g.astype(jnp.float32)).astype(x.dtype)


def rope_tables(seq_len):
    inv_freq = 1.0 / (ROPE_THETA ** (jnp.arange(0, DA_HEAD_DIM, 2, dtype=jnp.float32) / DA_HEAD_DIM))
    ang = jnp.arange(seq_len, dtype=jnp.float32)[:, None] * inv_freq[None, :]
    ang = jnp.concatenate([ang, ang], axis=-1)
    return jnp.cos(ang), jnp.sin(ang)


def apply_rope(x, cos, sin):
    half = DA_HEAD_DIM // 2
    xf = x.astype(jnp.float32)
    rot = jnp.concatenate([-xf[..., half:], xf[..., :half]], axis=-1)
    c = cos[None, :, None, None, :]
    s = sin[None, :, None, None, :]
    return (xf * c + rot * s).astype(x.dtype)


def diff_attention(q, k, v, lam):
    B, S = q.shape[0], q.shape[1]
    nb = S // Q_BLOCK
    scale = DA_HEAD_DIM ** -0.5
    qb = jnp.moveaxis(q.reshape(B, nb, Q_BLOCK, DA_HEADS, 2, DA_HEAD_DIM), 1, 0)

    def one_block(q_blk):
        s = jnp.einsum("bqhcd,bkhcd->bhcqk", q_blk, k, preferred_element_type=jnp.float32) * scale
        p = jax.nn.softmax(s, axis=-1)
        attn = p[:, :, 0] - lam * p[:, :, 1]
        return jnp.einsum("bhqk,bkhe->bqhe", attn.astype(v.dtype), v)

    o = lax.map(one_block, qb)
    return jnp.moveaxis(o, 0, 1).reshape(B, S, DA_HEADS, DA_V_DIM)


def diff_attn_mixer(z, cos, sin, lambda_init, q_norm_g, k_norm_g, lam_q1, lam_k1, lam_q2, lam_k2, subln_g):
    B, S = z.shape[0], z.shape[1]
    q = z[..., :DA_QK_COLS].reshape(B, S, DA_HEADS, 2, DA_HEAD_DIM)
    k = z[..., DA_QK_COLS:2 * DA_QK_COLS].reshape(B, S, DA_HEADS, 2, DA_HEAD_DIM)
    v = z[..., 2 * DA_QK_COLS:].reshape(B, S, DA_HEADS, DA_V_DIM)
    q = apply_rope(rms_norm(q, q_norm_g), cos, sin)
    k = apply_rope(rms_norm(k, k_norm_g), cos, sin)
    lam = (jnp.exp(jnp.sum(lam_q1.astype(jnp.float32) * lam_k1.astype(jnp.float32)))
           - jnp.exp(jnp.sum(lam_q2.astype(jnp.float32) * lam_k2.astype(jnp.float32)))
           + lambda_init)
    o = diff_attention(q, k, v, lam)
    o = rms_norm(o, subln_g) * (1.0 - lambda_init)
    return o.reshape(B, S, DA_WIDTH)


def rwkv7_scan(r, w, k, v, a, b):
    def step(state, inp):
        r_t, w_t, k_t, v_t, a_t, b_t = inp
        sa = jnp.einsum("dbhij,dbhj->dbhi", state, a_t)
        state = (state * w_t[..., None, :] + sa[..., :, None] * b_t[..., None, :]
                 + v_t[..., :, None] * k_t[..., None, :])
        y = jnp.einsum("dbhij,dbhj->dbhi", state, r_t)
        return state, y

    s0 = jnp.zeros(r.shape[1:] + (RW_HEAD,), jnp.float32)
    _, y = lax.scan(step, s0, (r, w, k, v, a, b))
    return y


def rwkv7_mixer(z, mu_prev, mu_next, w0, w_up, a0, a_up, g_up, k_k, k_a, r_k, ln_x_g, ln_x_b):
    B, S = z.shape[0], z.shape[1]
    zf = z.astype(jnp.float32)
    z_prev = jnp.pad(zf[:, :-1], ((0, 0), (1, 0), (0, 0)))
    z_next = jnp.pad(zf[:, 1:], ((0, 0), (0, 1), (0, 0)))
    zf = zf + mu_prev * (z_prev - zf) + mu_next * (z_next - zf)
    o1, o2, o3 = RW_WIDTH, 2 * RW_WIDTH, 3 * RW_WIDTH
    o4 = o3 + DECAY_LORA
    o5 = o4 + ICLR_LORA
    r, k, v = zf[..., :o1], zf[..., o1:o2], zf[..., o2:o3]
    w_dn, a_dn, g_dn = zf[..., o3:o4], zf[..., o4:o5], zf[..., o5:]
    w_log = -jax.nn.softplus(-(w0[:, None, None, :] + jnp.einsum("bsr,drc->dbsc", jnp.tanh(w_dn), w_up))) - 0.5
    decay = jnp.exp(-jnp.exp(w_log))
    a_rate = jax.nn.sigmoid(a0[:, None, None, :] + jnp.einsum("bsr,drc->dbsc", a_dn, a_up))
    g = jnp.einsum("bsr,rc->bsc", jax.nn.sigmoid(g_dn), g_up)

    def heads(t):
        return t.reshape(t.shape[:-1] + (RW_HEADS, RW_HEAD))

    kk = heads(k * k_k)
    kk = kk * lax.rsqrt(jnp.sum(kk * kk, axis=-1, keepdims=True) + 1e-12)
    k_dir = heads(k[None] * (1.0 + (a_rate - 1.0) * k_a))
    a_dir = heads(a_rate)
    decay = heads(decay)
    r_h, v_h = heads(r), heads(v)

    def both(t):
        return jnp.broadcast_to(t[None], (2,) + t.shape)

    def time_major(t):
        t = jnp.stack([t[0], jnp.flip(t[1], axis=1)])
        return jnp.moveaxis(t, 2, 0)

    y = rwkv7_scan(time_major(both(r_h)), time_major(decay), time_major(k_dir),
                   time_major(both(v_h)), time_major(both(-kk)), time_major(kk[None] * a_dir))
    y = jnp.moveaxis(y, 0, 2)
    y = y[0] + jnp.flip(y[1], axis=1)
    mean = jnp.mean(y, axis=-1, keepdims=True)
    var = jnp.mean(jnp.square(y - mean), axis=-1, keepdims=True)
    y = ((y - mean) * lax.rsqrt(var + LN_X_EPS)).reshape(B, S, RW_WIDTH) * ln_x_g + ln_x_b
    bonus = jnp.sum(r_h[None] * k_dir * r_k, axis=-1, keepdims=True) * v_h[None]
    y = y + jnp.sum(bonus, axis=0).reshape(B, S, RW_WIDTH)
    return (y * g).astype(z.dtype)


def encoder_layer(x, cos, sin, lambda_init, norm1_g, w_in, q_norm_g, k_norm_g, lam_q1, lam_k1,
                  lam_q2, lam_k2, subln_g, mu_prev, mu_next, w0, w_up, a0, a_up, g_up, k_k, k_a,
                  r_k, ln_x_g, ln_x_b, w_out, norm2_g, w_ff1, w_ff2):
    h = rms_norm(x, norm1_g)
    z = jnp.einsum("bsd,dc->bsc", h, w_in)
    o_da = diff_attn_mixer(z[..., :DA_COLS], cos, sin, lambda_init, q_norm_g, k_norm_g,
                           lam_q1, lam_k1, lam_q2, lam_k2, subln_g)
    o_rw = rwkv7_mixer(z[..., DA_COLS:], mu_prev, mu_next, w0, w_up, a0, a_up, g_up,
                       k_k, k_a, r_k, ln_x_g, ln_x_b)
    mixed = jnp.concatenate([o_da, o_rw.astype(o_da.dtype)], axis=-1)
    x = x + jnp.einsum("bsc,cd->bsd", mixed, w_out)
    h2 = rms_norm(x, norm2_g)
    u = jax.nn.relu(jnp.einsum("bsd,df->bsf", h2, w_ff1))
    return x + jnp.einsum("bsf,fd->bsd", u * u, w_ff2)


def setup_inputs(seed: int = 0) -> dict:
    key = jax.random.key(seed)
    ks = jax.random.split(key, 32)
    f32 = jnp.float32
    nrm = lambda i, shape: jax.random.normal(ks[i], shape, f32)
    L = DEPTH
    return {
        "x_prompt": nrm(0, (BATCH, SEQ, D_MODEL)),
        "x_sample": nrm(1, (DEC_BATCH, DEC_SEQ, D_MODEL)),
        "norm1_g": 1.0 + 0.02 * nrm(2, (L, D_MODEL)),
        "w_in": nrm(3, (L, D_MODEL, IN_COLS)) * D_MODEL ** -0.5,
        "q_norm_g": 1.0 + 0.02 * nrm(4, (L, DA_HEAD_DIM)),
        "k_norm_g": 1.0 + 0.02 * nrm(5, (L, DA_HEAD_DIM)),
        "lam_q1": 0.1 * nrm(6, (L, DA_HEAD_DIM)),
        "lam_k1": 0.1 * nrm(7, (L, DA_HEAD_DIM)),
        "lam_q2": 0.1 * nrm(8, (L, DA_HEAD_DIM)),
        "lam_k2": 0.1 * nrm(9, (L, DA_HEAD_DIM)),
        "subln_g": 1.0 + 0.02 * nrm(10, (L, DA_V_DIM)),
        "mu_prev": jax.random.uniform(ks[11], (L, RW_COLS), f32, 0.0, 0.5),
        "mu_next": jax.random.uniform(ks[12], (L, RW_COLS), f32, 0.0, 0.5),
        "w0": jax.random.uniform(ks[13], (L, 2, RW_WIDTH), f32, -4.0, 0.0),
        "w_up": 0.5 * nrm(14, (L, 2, DECAY_LORA, RW_WIDTH)) * DECAY_LORA ** -0.5,
        "a0": 0.1 * nrm(15, (L, 2, RW_WIDTH)),
        "a_up": 0.5 * nrm(16, (L, 2, ICLR_LORA, RW_WIDTH)) * ICLR_LORA ** -0.5,
        "g_up": nrm(17, (L, GATE_LORA, RW_WIDTH)) * GATE_LORA ** -0.5,
        "k_k": 0.85 + 0.05 * nrm(18, (L, RW_WIDTH)),
        "k_a": 1.0 + 0.05 * nrm(19, (L, RW_WIDTH)),
        "r_k": 0.1 * nrm(20, (L, RW_HEADS, RW_HEAD)),
        "ln_x_g": 1.0 + 0.02 * nrm(21, (L, RW_WIDTH)),
        "ln_x_b": 0.02 * nrm(22, (L, RW_WIDTH)),
        "w_out": nrm(23, (L, D_MODEL, D_MODEL)) * D_MODEL ** -0.5,
        "norm2_g": 1.0 + 0.02 * nrm(24, (L, D_MODEL)),
        "w_ff1": nrm(25, (L, D_MODEL, D_FF)) * D_MODEL ** -0.5,
        "w_ff2": nrm(26, (L, D_FF, D_MODEL)) * D_FF ** -0.5,
    }


def reference(x_prompt, x_sample, norm1_g, w_in, q_norm_g, k_norm_g, lam_q1, lam_k1, lam_q2,
              lam_k2, subln_g, mu_prev, mu_next, w0, w_up, a0, a_up, g_up, k_k, k_a, r_k,
              ln_x_g, ln_x_b, w_out, norm2_g, w_ff1, w_ff2):
    def run(x):
        cos, sin = rope_tables(x.shape[1])
        for l in range(DEPTH):
            lambda_init = 0.8 - 0.6 * math.exp(-0.3 * l)
            x = encoder_layer(x, cos, sin, lambda_init, norm1_g[l], w_in[l], q_norm_g[l],
                              k_norm_g[l], lam_q1[l], lam_k1[l], lam_q2[l], lam_k2[l],
                              subln_g[l], mu_prev[l], mu_next[l], w0[l], w_up[l], a0[l],
                              a_up[l], g_up[l], k_k[l], k_a[l], r_k[l], ln_x_g[l], ln_x_b[l],
                              w_out[l], norm2_g[l], w_ff1[l], w_ff2[l])
        return x

    y_prompt = run(x_prompt)
    y_sample = run(x_sample)
    return (y_prompt, y_sample)
```

```python
import math
import numpy as np
from contextlib import ExitStack
import concourse.bass as bass
import concourse.mybir as mybir
from concourse.bass_utils import run_bass_kernel_spmd

F32 = mybir.dt.float32
BF16 = mybir.dt.bfloat16
AF = mybir.ActivationFunctionType
ALU = mybir.AluOpType
AX = mybir.AxisListType

D = 1024
NCORES = 8
DA_COLS = 1536
RW_COLS = 1792
IN_COLS = 3328
DFF = 4096
CDEC = math.exp(-0.5)
LAMBDA_INIT = 0.8 - 0.6 * math.exp(0.0)


class Sched:
    def __init__(self, nc, es, n_dma_sems=24):
        self.nc = nc
        self.E = {'pe': nc.tensor, 'act': nc.scalar, 'dve': nc.vector, 'pool': nc.gpsimd, 'sp': nc.sync}
        self.sem = {k: es.enter_context(nc.semaphore("s_" + k)) for k in ['pe', 'act', 'dve', 'pool']}
        self.cnt = {k: 0 for k in self.sem}
        self.seen = {k: {} for k in self.E}
        self.lastw = {}
        self.readers = {}
        self.dsems = [es.enter_context(nc.semaphore("d%d" % i)) for i in range(n_dma_sems)]
        self.dcnt = [0] * n_dma_sems
        self.dnext = 0
        self.ninst = 0
        self.psum_names = set()
        self._rec = None

    def _key(self, a):
        if isinstance(a, tuple):
            if a[0].name in self.psum_names:
                return a[0].name
            return a[1]
        return a.name

    def _ap(self, a):
        return a[0] if isinstance(a, tuple) else a

    def _wait(self, eng, tok):
        sem, val, name = tok
        if self.seen[eng].get(name, 0) >= val:
            return
        self.E[eng].wait_ge(sem, val)
        self.seen[eng][name] = val

    def _deps(self, eng, ins, outs):
        toks = []
        for a in ins:
            k = self._key(a)
            if k in self.lastw:
                toks.append(self.lastw[k])
            if k in self.psum_names:
                toks.extend(t for e, t in self.readers.get(k, {}).items() if e != eng)
        for a in outs:
            k = self._key(a)
            if k in self.lastw:
                toks.append(self.lastw[k])
            toks.extend(self.readers.get(k, {}).values())
        for t in toks:
            if eng == 'pe' and t[2] == 'pe':
                continue
            self._wait(eng, t)

    def _commit(self, tok, ins, outs):
        for a in outs:
            k = self._key(a)
            self.lastw[k] = tok
            self.readers[k] = {}
        for a in ins:
            k = self._key(a)
            self.readers.setdefault(k, {})[tok[2]] = tok

    def rec_begin(self):
        self._rec = []

    def rec_end(self):
        r, self._rec = self._rec, None
        return r

    def replay(self, items):
        for it in items:
            if it[0] == 'op':
                self.op(*it[1:])
            else:
                self.dma(it[1], it[2], q=it[3])

    @staticmethod
    def merge(a, b):
        out, ia, ib = [], 0, 0
        while ia < len(a) or ib < len(b):
            if ib >= len(b) or (ia < len(a) and ia * len(b) <= ib * len(a)):
                out.append(a[ia]); ia += 1
            else:
                out.append(b[ib]); ib += 1
        return out

    def op(self, eng, fn, outs, ins):
        if _STOPPED[0]:
            return None
        if self._rec is not None:
            self._rec.append(('op', eng, fn, outs, ins))
            return None
        self._deps(eng, ins, outs)
        inst = fn()
        self.cnt[eng] += 1
        self.ninst += 1
        inst.then_inc(self.sem[eng], 1)
        tok = (self.sem[eng], self.cnt[eng], eng)
        self._commit(tok, ins, outs)
        return tok

    def dma(self, out, in_, q='sp'):
        if _STOPPED[0]:
            return None
        if self._rec is not None:
            self._rec.append(('dma', out, in_, q))
            return None
        i = self.dnext
        self.dnext = (self.dnext + 1) % len(self.dsems)
        name = 'dma%d' % i
        if self.dcnt[i] > 0:
            self._wait(q, (self.dsems[i], 16 * self.dcnt[i], name))
        self._deps(q, [in_], [out])
        self.E[q].dma_start(out=self._ap(out), in_=self._ap(in_)).then_inc(self.dsems[i], 16)
        self.dcnt[i] += 1
        self.ninst += 1
        tok = (self.dsems[i], 16 * self.dcnt[i], name)
        self._commit(tok, [in_], [out])
        return tok

    def barrier(self):
        if _STOPPED[0]:
            return
        for e in self.E:
            for k in self.sem:
                if self.cnt[k] > 0:
                    self._wait(e, (self.sem[k], self.cnt[k], k))
            for i in range(len(self.dsems)):
                if self.dcnt[i] > 0:
                    self._wait(e, (self.dsems[i], 16 * self.dcnt[i], 'dma%d' % i))
        self.lastw = {}
        self.readers = {}

    def finish(self, q='sp'):
        for i in range(len(self.dsems)):
            if self.dcnt[i] > 0:
                self._wait(q, (self.dsems[i], 16 * self.dcnt[i], 'dma%d' % i))


class _Stop(Exception):
    pass


import os as _os
_KSTOP = _os.environ.get('KSTOP', '')


_STOPPED = [False]


def _stop(tag):
    if _KSTOP == tag:
        _STOPPED[0] = True


def K(ap, key):
    return (ap, key)


def build_nc(NPT, NST, inv_dt=BF16):
    OWN = NPT // NCORES
    NOWN = OWN + NST
    NTOK = NOWN * 128
    nc = bass.Bass("TRN2", target_bir_lowering=False)
    dram = lambda name, shape, dt=F32, kind="ExternalInput": nc.dram_tensor(name, shape, dt, kind=kind).ap()
    NG = NCORES - 1
    xctx = dram("xctx", [NG, OWN + 2, 128, D])
    xown = dram("xown", [2, OWN + 2, 128, D])
    xP = dram("xP", [NPT, 128, D])
    xsmp = dram("xsmp", [2, NST + 2, 128, D])
    grp_mu = dram("grp_mu", [NG + 2, 2, RW_COLS])
    grp_w0 = dram("grp_w0", [NG + 2, 512])
    grp_a0 = dram("grp_a0", [NG + 2, 512])
    grp_wup = dram("grp_wup", [NG + 2, 64, 512])
    grp_aup = dram("grp_aup", [NG + 2, 64, 512])
    flags_d = dram("flags_d", [1, 8])
    J_d = dram("J_d", [128, 128])
    csP = dram("csP", [NPT, 128, 64])
    csQ = dram("csQ", [OWN, 128, 64])
    w_in = dram("w_in", [D, IN_COLS])
    w_out = dram("w_out", [D, D])
    w_ff1 = dram("w_ff1", [D, DFF])
    w_ff2 = dram("w_ff2", [DFF, D])
    norm1_g = dram("norm1_g", [128, 8])
    norm2_g = dram("norm2_g", [128, 8])
    vec = {}
    for nm, n in [("q_norm_g", 64), ("k_norm_g", 64), ("lam_q1", 64), ("lam_k1", 64), ("lam_q2", 64),
                  ("lam_k2", 64), ("subln_g", 128), ("mu_prev", RW_COLS), ("mu_next", RW_COLS),
                  ("k_k", 512), ("k_a", 512), ("r_k", 512), ("ln_x_g", 512), ("ln_x_b", 512)]:
        vec[nm] = dram(nm, [1, n])
    w0 = dram("w0", [2, 512])
    a0 = dram("a0", [2, 512])
    w_up = dram("w_up", [2, 64, 512])
    a_up = dram("a_up", [2, 64, 512])
    g_up = dram("g_up", [128, 512])
    ident_d = dram("ident_d", [128, 128])
    maskA_d = dram("maskA_d", [2, 128, 1536])
    ltri_d = dram("ltri_d", [2, 128, 256])
    out_p = dram("out_p", [OWN, 128, D], kind="ExternalOutput")
    out_s = dram("out_s", [NST, 128, D], kind="ExternalOutput")
    scr = lambda name, shape, dt=F32: nc.dram_tensor(name, shape, dt).ap()
    yF_s = scr("yF_s", [NOWN, 128, 512])
    bonF_s = scr("bonF_s", [NOWN, 128, 8])
    mixT_s = scr("mixT_s", [8, 128, NTOK], BF16)
    kT_sP = scr("kT_sP", [4, 128, NPT * 128], BF16)
    kT_sS = scr("kT_sS", [4, 128, NST * 128], BF16)
    V_sP = scr("V_sP", [4, 128, NPT, 129], BF16)
    V_sS = scr("V_sS", [4, 128, NST, 129], BF16)
    w1b_s = scr("w1b_s", [8, 128, DFF], BF16)

    with ExitStack() as es:
        S = Sched(nc, es)
        V, G, A, P = nc.vector, nc.gpsimd, nc.scalar, nc.tensor

        def sbuf(stack, name, shape, dt=F32):
            return stack.enter_context(nc.sbuf_tensor(name, shape, dt))

        def psum(stack, name, shape, dt=F32):
            S.psum_names.add(name)
            return stack.enter_context(nc.psum_tensor(name, shape, dt))

        def MM(out, lhsT, rhs, start=True, stop=True):
            S.op('pe', lambda: P.matmul(S._ap(out), S._ap(lhsT), S._ap(rhs), start=start, stop=stop), [out], [lhsT, rhs])

        def TR(out, in_, idt):
            S.op('pe', lambda: P.transpose(S._ap(out), S._ap(in_), S._ap(idt)), [out], [in_, idt])

        def ACT(out, in_, func, bias=None, scale=None, accum=None, extra_out=()):
            kw = {}
            ins = [in_]
            if bias is not None:
                kw['bias'] = S._ap(bias) if not isinstance(bias, float) else bias
                if not isinstance(bias, float):
                    ins.append(bias)
            if scale is not None:
                kw['scale'] = S._ap(scale) if not isinstance(scale, float) else scale
                if not isinstance(scale, float):
                    ins.append(scale)
            outs = [out] + list(extra_out)
            if accum is not None:
                kw['accum_out'] = S._ap(accum)
                outs.append(accum)
            S.op('act', lambda: A.activation(S._ap(out), S._ap(in_), func, **kw), outs, ins)

        def EW(eng):
            return {'dve': V, 'pool': G}[eng]

        def TT(eng, out, a, b, op):
            S.op(eng, lambda: EW(eng).tensor_tensor(S._ap(out), S._ap(a), S._ap(b), op), [out], [a, b])

        def TS(eng, out, a, s1, s2, op0, op1=None):
            ins = [a] + [s for s in (s1, s2) if s is not None and not isinstance(s, float)]
            g = lambda s: s if (s is None or isinstance(s, float)) else S._ap(s)
            if op1 is None:
                S.op(eng, lambda: EW(eng).tensor_scalar(S._ap(out), S._ap(a), g(s1), None, op0), [out], ins)
            else:
                S.op(eng, lambda: EW(eng).tensor_scalar(S._ap(out), S._ap(a), g(s1), g(s2), op0, op1), [out], ins)

        def STT(eng, out, a, s, b, op0, op1):
            ins = [a, b] + ([] if isinstance(s, float) else [s])
            sv = s if isinstance(s, float) else S._ap(s)
            S.op(eng, lambda: EW(eng).scalar_tensor_tensor(S._ap(out), S._ap(a), sv, S._ap(b), op0, op1), [out], ins)

        def CP(eng, out, in_):
            if eng == 'act':
                S.op('act', lambda: A.copy(S._ap(out), S._ap(in_)), [out], [in_])
            else:
                S.op(eng, lambda: EW(eng).tensor_copy(S._ap(out), S._ap(in_)), [out], [in_])

        def RED(eng, out, in_, op=ALU.add):
            S.op(eng, lambda: EW(eng).tensor_reduce(S._ap(out), S._ap(in_), AX.X, op), [out], [in_])

        def MSET(eng, out, val):
            S.op(eng, lambda: EW(eng).memset(S._ap(out), val), [out], [])

        def RECIP(out, in_):
            S.op('dve', lambda: V.reciprocal(S._ap(out), S._ap(in_)), [out], [in_])

        def rsqrt_small(out, in_, mul, add):
            TS('dve', out, in_, float(mul), float(add), ALU.mult, ALU.add)
            S.op('act', lambda: A.sqrt(S._ap(out), S._ap(out)), [out], [out])
            RECIP(out, out)

        def bc(ap, shape):
            return ap.to_broadcast(shape)

        idf = sbuf(es, "idf", [128, 128])
        idb = sbuf(es, "idb", [128, 128], BF16)
        g1T = sbuf(es, "g1T", [128, 8])
        g2T = sbuf(es, "g2T", [128, 8])
        S.dma(idf[:], ident_d[:, :])
        CP('dve', idb[:], idf[:])
        S.dma(g1T[:], norm1_g[:, :])
        S.dma(g2T[:], norm2_g[:, :])

        pT = psum(es, "pT", [128, 1024], BF16)
        pZ = psum(es, "pZ", [128, 512])
        pL0 = psum(es, "pL0", [128, 512])
        pL1 = psum(es, "pL1", [128, 512])
        pA = psum(es, "pA", [128, 512])
        pB = psum(es, "pB", [128, 512])
        pC = psum(es, "pC", [128, 512])
        pD = psum(es, "pD", [128, 512])

        xt = [sbuf(es, "xt%d" % i, [128, D]) for i in range(2)]
        xb = sbuf(es, "xb", [128, D], BF16)
        junk = xb
        xT = sbuf(es, "xT", [128, 8, 128], BF16)
        ssq = sbuf(es, "ssq", [128, 1])
        rstd = sbuf(es, "rstd", [128, 1])
        WS = 512
        wstage = [None, None]
        xcount = [0]

        def load_weight_bf16(dst, src_cols, gT):
            c0, c1 = src_cols
            n = c1 - c0
            for kc in range(8):
                for s0 in range(0, n, WS):
                    s1 = min(n, s0 + WS)
                    ws = wstage[xcount[0] % 2]
                    xcount[0] += 1
                    S.dma(ws[:, 0:s1 - s0], w_in[kc * 128:(kc + 1) * 128, c0 + s0:c0 + s1])
                    TS('pool', dst[:, kc, s0:s1], ws[:, 0:s1 - s0], gT[:, kc:kc + 1], None, ALU.mult)

        def project_tile(src_ap, w_b, col_groups, evac):
            xtile = xt[xcount[0] % 2]
            xcount[0] += 1
            S.dma(xtile[:], src_ap)
            ACT(junk[:], xtile[:], AF.Square, accum=ssq[:])
            rsqrt_small(rstd[:], ssq[:], 1.0 / D, 1e-6)
            CP('pool', xb[:], xtile[:])
            for hlf in range(2):
                for k4 in range(4):
                    kc = hlf * 4 + k4
                    TR(K(pT[:, hlf * 512 + k4 * 128: hlf * 512 + (k4 + 1) * 128], "pT%d" % hlf), xb[:, kc * 128:(kc + 1) * 128], idb[:])
                eng = 'act' if hlf == 0 else 'dve'
                CP(eng, K(xT[:, hlf * 4:(hlf + 1) * 4, :], "xT%d" % hlf),
                   K(pT[:, hlf * 512:(hlf + 1) * 512].rearrange("p (a b) -> p a b", a=4), "pT%d" % hlf))
            zb_ = [pZ, pL0, pL1]
            for gi, (c0, c1) in enumerate(col_groups):
                n = c1 - c0
                pz_ = zb_[gi % 3]
                for kc in range(8):
                    MM(pz_[:, 0:n], K(xT[:, kc, :], "xT%d" % (kc // 4)), w_b[:, kc, c0:c1], start=(kc == 0), stop=(kc == 7))
                evac(gi, pz_[:, 0:n], rstd)

        with ExitStack() as e1:
            wrw = sbuf(e1, "wrw", [128, 8, RW_COLS], BF16)
            Z = [sbuf(e1, "Zr%d" % i, [128, RW_COLS]) for i in range(3)]
            wstage[0], wstage[1] = Z[1], Z[2]
            load_weight_bf16(wrw, (DA_COLS, IN_COLS), g1T)
            ZP = [sbuf(e1, "zp%d" % j, [128, RW_COLS]) for j in range(2)]
            zn = sbuf(e1, "zn", [128, RW_COLS])
            mu_p = sbuf(e1, "mu_p", [128, RW_COLS])
            mu_n = sbuf(e1, "mu_n", [128, RW_COLS])
            mu_c = sbuf(e1, "mu_c", [128, RW_COLS])
            S.dma(mu_p[:], vec["mu_prev"].partition_broadcast(128))
            S.dma(mu_n[:], vec["mu_next"].partition_broadcast(128))
            TT('pool', mu_c[:], mu_p[:], mu_n[:], ALU.add)
            TS('pool', mu_c[:], mu_c[:], -1.0, 1.0, ALU.mult, ALU.add)
            bt = {}
            for nm in ["k_k", "k_a", "r_k", "ln_x_g", "ln_x_b"]:
                bt[nm] = sbuf(e1, "b_" + nm, [128, 512])
                S.dma(bt[nm][:], vec[nm].partition_broadcast(128))
            w0a0 = sbuf(e1, "w0a0", [128, 512])
            onesf = sbuf(e1, "onesf", [128, 128])
            MSET('pool', onesf[:], 1.0)
            MSET('pool', w0a0[:], 0.0)
            waup = sbuf(e1, "waup", [128, 512])
            gup = sbuf(e1, "gup", [128, 512])
            S.dma(gup[:], g_up[:, :])
            mS4 = sbuf(e1, "mS4", [128, 512])
            mI4 = sbuf(e1, "mI4", [128, 512])
            mB4 = sbuf(e1, "mB4", [128, 512])
            ltri = sbuf(e1, "ltri", [128, 256])
            negc = sbuf(e1, "negc", [128, 1])
            MSET('pool', negc[:], -CDEC)
            T = {}
            for nm in ["sgw", "arate", "Winv", "We", "kk", "kd", "t0", "t1", "ydir", "yF", "gte", "f1"]:
                T[nm] = sbuf(e1, "T_" + nm, [128, 512])
            T["Wt"] = T["kd"]
            LO = [sbuf(e1, "lo%d" % j, [128, 128]) for j in range(2)]
            LOT = [sbuf(e1, "loT%d" % j, [128, 128]) for j in range(2)]
            sgT = sbuf(e1, "sgT", [128, 128])
            s8 = [sbuf(e1, "s8_%d" % i, [128, 8]) for i in range(6)]
            bonF = sbuf(e1, "bonF", [128, 8])
            rtok = sbuf(e1, "rtok", [128, 512], BF16)
            orw = sbuf(e1, "orw", [128, 512], BF16)
            orwT = sbuf(e1, "orwT", [128, 4, 128], BF16)
            BS = []
            for j in range(2):
                o = {}
                o["arT"] = sbuf(e1, "arT%d" % j, [128, 4, 2, 128], BF16)
                o["bT"] = sbuf(e1, "bT%d" % j, [128, 4, 128], BF16)
                o["kTt"] = sbuf(e1, "kTt%d" % j, [128, 4, 128], BF16)
                o["atok"] = sbuf(e1, "atok%d" % j, [128, 512], BF16)
                o["btok"] = sbuf(e1, "btok%d" % j, [128, 512], BF16)
                o["ktok"] = sbuf(e1, "ktok%d" % j, [128, 512], BF16)
                o["vb"] = sbuf(e1, "vb%d" % j, [128, 512], BF16)
                o["WC"] = sbuf(e1, "WC%d" % j, [128, 4])
                o["vfin"] = sbuf(e1, "vfin%d" % j, [128, 512])
                o["sgs"] = sbuf(e1, "sgs%d" % j, [128, 128])
                o["bon"] = sbuf(e1, "bon%d" % j, [128, 8])
                BS.append(o)
            P1T = sbuf(e1, "P1T", [128, 4, 128], BF16)
            U0 = sbuf(e1, "U0", [128, 512])
            ArbT = sbuf(e1, "ArbT", [128, 8, 128], BF16)
            ArkT = sbuf(e1, "ArkT", [128, 8, 128], BF16)
            NN = [sbuf(e1, "NN%d" % j, [128, 8, 128], inv_dt) for j in range(2)]
            NTt = [sbuf(e1, "NTt%d" % j, [128, 8, 128], inv_dt) for j in range(2)]
            RR = [sbuf(e1, "RR%d" % j, [128, 8, 128], inv_dt) for j in range(2)]
            AakT = sbuf(e1, "AakT", [128, 8, 128], BF16)
            X0a = sbuf(e1, "X0a", [128, 512], BF16)
            idi = sbuf(e1, "idi", [128, 128], inv_dt)
            CP('pool', idi[:], idf[:])
            Hf = sbuf(e1, "Hf", [128, 4, 128])
            Hb = sbuf(e1, "Hb", [128, 4, 128], BF16)
            HfW = sbuf(e1, "HfW", [128, 4, 128])
            tmpH = T["f1"][:].rearrange("p (q t) -> p q t", q=4)
            Ub = sbuf(e1, "Ub", [128, 512], BF16)

            def compute_z(src_ap, zbuf):
                groups = [(0, 512), (512, 1024), (1024, 1536), (1536, 1792)]

                def evac(gi, pz, rs):
                    c0, c1 = groups[gi]
                    ACT(zbuf[:, c0:c1], pz, AF.Copy, scale=rs[:, 0:1])
                project_tile(src_ap, wrw, groups, evac)

            def blk(banks, h):
                return banks[h % 2][:, (h // 2) * 128:(h // 2 + 1) * 128]

            def hrows(h):
                return slice((h % 2) * 64, (h % 2) * 64 + 64)
            f2 = lambda t3, b: t3[:].rearrange("p (q b) t -> p q b t", b=2)[:, :, b, :]
            bv = lambda t2: t2[:].rearrange("p (q t) -> p q t", q=4)
            h8 = lambda ap: ap.rearrange("p (h n) -> p h n", h=8)

            Jf = sbuf(e1, "Jf", [128, 128])
            Jb = sbuf(e1, "Jb", [128, 128], BF16)
            S.dma(Jf[:], J_d[:, :])
            CP('dve', Jb[:], Jf[:])
            flg = sbuf(e1, "flg", [128, 8])
            nflg = sbuf(e1, "nflg", [128, 8])
            S.dma(flg[:], flags_d.partition_broadcast(128))
            TS('dve', nflg[:], flg[:], -1.0, 1.0, ALU.mult, ALU.add)
            Hsave = T["yF"]
            Hbk = T["gte"]
            S.dma(mS4[:], maskA_d[0, :, 0:512])
            S.dma(mI4[:], maskA_d[0, :, 512:1024])
            S.dma(mB4[:], maskA_d[0, :, 1024:1536])
            S.dma(ltri[:], ltri_d[0, :, :])
            Hf2 = Hf[:].rearrange("p q t -> p (q t)")
            Hb2 = Hb[:].rearrange("p q t -> p (q t)")

            def load_group_params(gi):
                S.dma(mu_p[:], grp_mu[gi, 0:1, :].partition_broadcast(128))
                S.dma(mu_n[:], grp_mu[gi, 1:2, :].partition_broadcast(128))
                S.dma(w0a0[0:1, :], grp_w0[gi:gi + 1, :])
                S.dma(w0a0[64:65, :], grp_a0[gi:gi + 1, :])
                S.dma(waup[0:64, :], grp_wup[gi, :, :])
                S.dma(waup[64:128, :], grp_aup[gi, :, :])

            def rwkv_pass(d, xsrc, n_slots, n_own, own_base):
                compute_z(xsrc[0, :, :], Z[0])
                compute_z(xsrc[1, :, :], Z[1])

                def stage12a(i):
                    zp, lo, loT = ZP[i % 2], LO[i % 2], LOT[i % 2]
                    compute_z(xsrc[i + 2, :, :], Z[(i + 2) % 3])
                    zc, za, zb = Z[(i + 1) % 3], Z[(i + 2) % 3], Z[i % 3]
                    zprev_src, znext_src = zb, za
                    S.dma(zp[1:128, :], zc[0:127, :])
                    S.dma(zp[0:1, :], zprev_src[127:128, :])
                    S.dma(zn[0:127, :], zc[1:128, :])
                    S.dma(zn[127:128, :], znext_src[0:1, :])
                    TT('pool', zp[:], zp[:], mu_p[:], ALU.mult)
                    TT('dve', zn[:], zn[:], mu_n[:], ALU.mult)
                    TT('dve', zp[:], zp[:], zn[:], ALU.add)
                    TT('pool', zn[:], zc[:], mu_c[:], ALU.mult)
                    TT('dve', zp[:], zp[:], zn[:], ALU.add)
                    zf = zp
                    r_, k_, v_ = zf[:, 0:512], zf[:, 512:1024], zf[:, 1024:1536]
                    _stop('S12a')
                    ACT(lo[:, 0:64], zf[:, 1536:1600], AF.Tanh)
                    CP('pool', lo[:, 64:128], zf[:, 1600:1664])
                    TR(pL1[:, 0:128], lo[:], idf[:])
                    CP('act', loT[:], pL1[:, 0:128])

                def stage12b(i):
                    own = i >= n_slots - n_own
                    o = BS[i % 2]
                    zf, loT = ZP[i % 2], LOT[i % 2]
                    r_, k_, v_ = zf[:, 0:512], zf[:, 512:1024], zf[:, 1024:1536]
                    MM(pL0[:], loT[0:64, :], waup[0:64, :], start=True, stop=False)
                    MM(pL0[:], onesf[0:1, :], w0a0[0:1, :], start=False, stop=True)
                    MM(pL1[:], loT[64:128, :], waup[64:128, :], start=True, stop=False)
                    MM(pL1[:], onesf[64:65, :], w0a0[64:65, :], start=False, stop=True)
                    ACT(T["sgw"][:], pL0[:], AF.Sigmoid)
                    ACT(T["arate"][:], pL1[:], AF.Sigmoid)
                    MM(pL0[:], ltri[:, 0:128], T["sgw"][:])
                    MM(pL1[:], ltri[:, 128:256], T["sgw"][:])
                    for p in range(4):
                        MM(pZ[:, p:p + 1], T["sgw"][:, p * 128:(p + 1) * 128], negc[:], start=True, stop=True)
                    ACT(o["WC"][:], pZ[:, 0:4], AF.Exp)
                    if own:
                        ACT(T["Wt"][:], pL0[:], AF.Exp)
                        TT('dve', rtok[:], r_, T["Wt"][:], ALU.mult)
                    ACT(T["Winv"][:], pL0[:], AF.Exp, scale=-1.0)
                    ACT(T["We"][:], pL1[:], AF.Exp)
                    _stop('S12b')
                    TT('pool', T["kk"][:], k_, bt["k_k"][:], ALU.mult)
                    TT('pool', T["t0"][:], T["kk"][:], T["kk"][:], ALU.mult)
                    RED('dve', s8[0][:], h8(T["t0"][:]))
                    rsqrt_small(s8[0][:], s8[0][:], 1.0, 1e-12)
                    TT('dve', h8(T["kk"][:]), h8(T["kk"][:]), bc(s8[0][:].unsqueeze(2), [128, 8, 64]), ALU.mult)
                    STT('dve', T["t0"][:], T["arate"][:], -1.0, bt["k_a"][:], ALU.add, ALU.mult)
                    STT('dve', T["kd"][:], T["t0"][:], 1.0, k_, ALU.add, ALU.mult)
                    STT('dve', o["atok"][:], T["kk"][:], -1.0, T["We"][:], ALU.mult, ALU.mult)
                    TT('pool', T["t1"][:], T["kk"][:], T["arate"][:], ALU.mult)
                    TT('pool', o["btok"][:], T["t1"][:], T["Winv"][:], ALU.mult)
                    TT('pool', o["ktok"][:], T["kd"][:], T["Winv"][:], ALU.mult)
                    CP('act', o["vb"][:], v_)
                    for p in range(4):
                        TR(pT[:, p * 128:(p + 1) * 128], o["atok"][:, p * 128:(p + 1) * 128], idb[:])
                    for p in range(4):
                        TR(pT[:, 512 + p * 128:512 + (p + 1) * 128], o["btok"][:, p * 128:(p + 1) * 128], idb[:])
                    CP('act', o["arT"][:, :, 0, :], pT[:, 0:512].rearrange("p (a b) -> p a b", a=4))
                    CP('dve', o["bT"][:], pT[:, 512:1024].rearrange("p (a b) -> p a b", a=4))
                    for p in range(4):
                        TR(pT[:, p * 128:(p + 1) * 128], o["ktok"][:, p * 128:(p + 1) * 128], idb[:])
                    if own:
                        for p in range(4):
                            TR(pT[:, 512 + p * 128:512 + (p + 1) * 128], rtok[:, p * 128:(p + 1) * 128], idb[:])
                    CP('act', o["kTt"][:], pT[:, 0:512].rearrange("p (a b) -> p a b", a=4))
                    if own:
                        CP('dve', o["arT"][:, :, 1, :], pT[:, 512:1024].rearrange("p (a b) -> p a b", a=4))
                        TT('pool', T["t0"][:], r_, bt["r_k"][:], ALU.mult)
                        TT('pool', T["t0"][:], T["t0"][:], T["kd"][:], ALU.mult)
                        RED('dve', o["bon"][:], h8(T["t0"][:]))
                        if d == 1:
                            CP('pool', o["vfin"][:], v_)
                            ACT(o["sgs"][:], zf[:, 1664:1792], AF.Sigmoid)

                def stage34(i):
                    own = i >= n_slots - n_own
                    o = BS[i % 2]
                    arT, bT, kTt, atok = o["arT"], o["bT"], o["kTt"], o["atok"]
                    for h in range(8):
                        MM(blk((pA, pB), h), bT[hrows(h), h // 2, :], arT[hrows(h), h // 2, 0, :])
                    for h in range(8):
                        MM(blk((pC, pD), h), kTt[hrows(h), h // 2, :], arT[hrows(h), h // 2, 0, :])
                    for b_, bank in enumerate((pA, pB)):
                        TT('dve', f2(NN[0], b_), bv(bank), bv(mS4), ALU.mult)
                    for b_, bank in enumerate((pC, pD)):
                        TT('dve', f2(AakT, b_), bv(bank), bv(mS4), ALU.mult)
                    for h in range(8):
                        MM(blk((pA, pB), h), arT[hrows(h), h // 2, 0, :], bT[hrows(h), h // 2, :])
                    for b_, bank in enumerate((pA, pB)):
                        TT('dve', f2(NTt[0], b_), bv(bank), bv(mB4), ALU.mult)
                    if own:
                        for h in range(8):
                            MM(blk((pC, pD), h), bT[hrows(h), h // 2, :], arT[hrows(h), h // 2, 1, :])
                        for b_, bank in enumerate((pC, pD)):
                            TT('dve', f2(ArbT, b_), bv(bank), bv(mI4), ALU.mult)
                        for h in range(8):
                            MM(blk((pA, pB), h), kTt[hrows(h), h // 2, :], arT[hrows(h), h // 2, 1, :])
                        for b_, bank in enumerate((pA, pB)):
                            TT('dve', f2(ArkT, b_), bv(bank), bv(mI4), ALU.mult)
                    _stop('A')
                    TT('dve', RR[0][:], NN[0][:], bc(idi[:].unsqueeze(1), [128, 8, 128]), ALU.add)
                    for lev in range(1, 7):
                        cur, nxt = (lev - 1) % 2, lev % 2
                        last = lev == 6
                        if not last:
                            for h in range(8):
                                MM(blk((pA, pB), h), NTt[cur][:, h, :], NN[cur][:, h, :])
                        for h in range(8):
                            MM(blk((pC, pD), h), NN[cur][:, h, :], NTt[cur][:, h, :])
                        if not last:
                            CP('act', f2(NN[nxt], 0), bv(pA))
                            CP('act', f2(NN[nxt], 1), bv(pB))
                        CP('act', f2(NTt[nxt], 0), bv(pC))
                        CP('dve', f2(NTt[nxt], 1), bv(pD))
                        for h in range(8):
                            MM(blk((pA, pB), h), NTt[nxt][:, h, :], RR[cur][:, h, :])
                        TT('dve', f2(RR[nxt], 0), bv(pA), f2(RR[cur], 0), ALU.add)
                        TT('dve', f2(RR[nxt], 1), bv(pB), f2(RR[cur], 1), ALU.add)
                    _stop('B')
                    Rf = RR[0]
                    for h in range(8):
                        MM(blk((pC, pD), h), atok[:, (h // 2) * 128:(h // 2 + 1) * 128], Rf[:, h, :])
                    for hh, bank in enumerate((pC, pD)):
                        src = bank[hh * 64:(hh + 1) * 64, :].rearrange("p (q t) -> p q t", q=4)
                        CP('act' if hh == 0 else 'dve', P1T[hh * 64:(hh + 1) * 64, :, :], src)
                    for h in range(8):
                        MM(pA[:, h * 64:(h + 1) * 64], AakT[:, h, :], o["vb"][:, h * 64:(h + 1) * 64])
                    CP('act', X0a[:], pA[:])
                    for h in range(8):
                        MM(pB[:, h * 64:(h + 1) * 64], Rf[:, h, :], X0a[:, h * 64:(h + 1) * 64])
                    CP('act', U0[:], pB[:])
                    _stop('C')
                    for p in range(4):
                        MM(pC[:, p * 128:(p + 1) * 128], P1T[:, p, :], Hb[:, p, :])
                    TT('dve', Ub[:], pC[:], U0[:], ALU.add)
                    if own:
                        for h in range(8):
                            p, hh = h // 2, h % 2
                            cs_ = slice(h * 64, h * 64 + 64)
                            MM(pD[:, cs_], arT[:, p, 1, :], Hb[:, p, hh * 64:(hh + 1) * 64], start=True, stop=False)
                            MM(pD[:, cs_], ArbT[:, h, :], Ub[:, cs_], start=False, stop=False)
                            MM(pD[:, cs_], ArkT[:, h, :], o["vb"][:, cs_], start=False, stop=True)
                        CP('act', T["ydir"][:], pD[:])
                    for p in range(4):
                        pc_ = slice(p * 128, (p + 1) * 128)
                        MM(pA[:, pc_], o["btok"][:, pc_], Ub[:, pc_], start=True, stop=False)
                        MM(pA[:, pc_], o["ktok"][:, pc_], o["vb"][:, pc_], start=False, stop=True)
                    for hh in range(2):
                        rows = slice(hh * 64, hh * 64 + 64)
                        cols = slice(hh * 64, hh * 64 + 64)
                        WCb = bc(o["WC"][rows, :].unsqueeze(2), [64, 4, 64])
                        TT('dve', HfW[rows, :, cols], Hf[rows, :, cols], WCb, ALU.mult)
                        pblk = pA[rows, :].rearrange("p (q h i) -> p q h i", q=4, h=2)[:, :, hh, :]
                        TT('dve', tmpH[rows, :, cols], pblk, WCb, ALU.mult)
                        TT('dve', Hf[rows, :, cols], tmpH[rows, :, cols], HfW[rows, :, cols], ALU.add)
                        CP('act', Hb[rows, :, cols], Hf[rows, :, cols])
                    _stop('D')
                    if own:
                        if d == 0:
                            ot = own_base + i
                            S.dma(yF_s[ot, :, :], T["ydir"][:])
                            S.dma(bonF_s[ot, :, :], o["bon"][:])
                        else:
                            ot = own_base + (n_slots - 1 - i)
                            S.dma(T["yF"][:], yF_s[ot, :, :])
                            S.dma(bonF[:], bonF_s[ot, :, :])
                            TR(pB[:, 0:128], o["sgs"][:], idf[:])
                            CP('act', sgT[:], pB[:, 0:128])
                            MM(pC[:], sgT[:], gup[:])
                            CP('act', T["gte"][:], pC[:])
                            MM(pC[:], Jf[:], T["yF"][:])
                            MM(pB[:, 0:8], Jf[:], bonF[:])
                            y = T["yF"]
                            y3 = h8(y[:])
                            TT('dve', y[:], pC[:], T["ydir"][:], ALU.add)
                            RED('dve', s8[2][:], y3)
                            TS('dve', s8[2][:], s8[2][:], 1.0 / 64, None, ALU.mult)
                            TT('dve', y3, y3, bc(s8[2][:].unsqueeze(2), [128, 8, 64]), ALU.subtract)
                            TT('pool', T["f1"][:], y[:], y[:], ALU.mult)
                            RED('dve', s8[3][:], h8(T["f1"][:]))
                            rsqrt_small(s8[3][:], s8[3][:], 1.0 / 64, 64e-5)
                            TT('dve', y3, y3, bc(s8[3][:].unsqueeze(2), [128, 8, 64]), ALU.mult)
                            TT('pool', y[:], y[:], bt["ln_x_g"][:], ALU.mult)
                            TT('pool', y[:], y[:], bt["ln_x_b"][:], ALU.add)
                            TT('dve', s8[4][:], pB[:, 0:8], o["bon"][:], ALU.add)
                            TT('dve', h8(T["f1"][:]), h8(o["vfin"][:]), bc(s8[4][:].unsqueeze(2), [128, 8, 64]), ALU.mult)
                            TT('pool', y[:], y[:], T["f1"][:], ALU.add)
                            TT('dve', orw[:], y[:], T["gte"][:], ALU.mult)
                            for p in range(4):
                                MM(pD[:, p * 128:(p + 1) * 128], orw[:, p * 128:(p + 1) * 128], Jb[:])
                            CP('act', orwT[:], pD[:].rearrange("p (a b) -> p a b", a=4))
                            S.dma(mixT_s[4:8, :, ot * 128:(ot + 1) * 128].rearrange("c p t -> p c t"), orwT[:])

                stage12a(0)
                stage12b(0)
                if n_slots > 1:
                    stage12a(1)
                for i in range(n_slots):
                    S.rec_begin()
                    if i + 2 < n_slots:
                        stage12a(i + 2)
                    if i + 1 < n_slots:
                        stage12b(i + 1)
                    A_ = S.rec_end()
                    S.rec_begin()
                    stage34(i)
                    B_ = S.rec_end()
                    S.replay(S.merge(A_, B_) if not _os.environ.get("NOMERGE") else (B_ + A_))

            def set_state(src2):
                if src2 is None:
                    MSET('pool', Hf[:], 0.0)
                else:
                    CP('pool', Hf2, src2)
                CP('act', Hb2, Hf2)

            def switch_step(g):
                STT('dve', Hsave[:], Hf2, flg[:, g:g + 1], Hsave[:], ALU.mult, ALU.add)
                TS('dve', Hf2, Hf2, nflg[:, g:g + 1], None, ALU.mult)
                CP('act', Hb2, Hf2)

            MSET('pool', Hb[:], 0.0)
            MSET('pool', HfW[:], 0.0)
            MSET('pool', Hsave[:], 0.0)
            set_state(None)
            for g in range(NG):
                switch_step(g)
                load_group_params(g)
                rwkv_pass(0, xctx[g], OWN, 0, 0)
            switch_step(NG)
            CP('pool', Hbk[:], Hf2)
            set_state(Hsave[:])
            load_group_params(NG)
            rwkv_pass(0, xown[0], OWN, OWN, 0)
            set_state(None)
            rwkv_pass(0, xsmp[0], NST, NST, OWN)
            set_state(Hbk[:])
            load_group_params(NG + 1)
            rwkv_pass(1, xown[1], OWN, OWN, 0)
            set_state(None)
            rwkv_pass(1, xsmp[1], NST, NST, OWN)

        S.barrier()
        with ExitStack() as e2:
            wda = sbuf(e2, "wda", [128, 8, DA_COLS], BF16)
            wstage[0], wstage[1] = [sbuf(e2, "wstg2_%d" % i, [128, WS]) for i in range(2)]
            load_weight_bf16(wda, (0, DA_COLS), g1T)
            gq = sbuf(e2, "gq", [128, 64])
            gk = sbuf(e2, "gk", [128, 64])
            S.dma(gq[:], vec["q_norm_g"].partition_broadcast(128))
            S.dma(gk[:], vec["k_norm_g"].partition_broadcast(128))
            gsub = sbuf(e2, "gsub", [128, 128])
            S.dma(gsub[:], vec["subln_g"].partition_broadcast(128))
            TS('pool', gsub[:], gsub[:], 1.0 - LAMBDA_INIT, None, ALU.mult)
            lv = [sbuf(e2, "lv%d" % i, [128, 64]) for i in range(4)]
            for i, nm in enumerate(["lam_q1", "lam_k1", "lam_q2", "lam_k2"]):
                S.dma(lv[i][:], vec[nm].partition_broadcast(128))
            l2 = sbuf(e2, "l2", [128, 2])
            neglam = sbuf(e2, "neglam", [128, 1])
            TT('dve', lv[0][:], lv[0][:], lv[1][:], ALU.mult)
            TT('dve', lv[2][:], lv[2][:], lv[3][:], ALU.mult)
            RED('dve', l2[:, 0:1], lv[0][:])
            RED('dve', l2[:, 1:2], lv[2][:])
            ACT(l2[:], l2[:], AF.Exp)
            TT('dve', neglam[:], l2[:, 1:2], l2[:, 0:1], ALU.subtract)
            TS('dve', neglam[:], neglam[:], -LAMBDA_INIT, None, ALU.add)
            cs = sbuf(e2, "cs", [128, 64])
            zq = sbuf(e2, "zq", [128, 512])
            zk = sbuf(e2, "zk", [128, 512])
            zv = sbuf(e2, "zv", [128, 512])
            qn = sbuf(e2, "qn", [128, 512])
            u1 = sbuf(e2, "u1", [128, 256])
            u2 = sbuf(e2, "u2", [128, 256])
            rb = sbuf(e2, "rbf", [128, 512], BF16)
            s8q = sbuf(e2, "s8q", [128, 8])
            kTst = sbuf(e2, "kTst", [128, 4, 128], BF16)
            Vst = sbuf(e2, "Vst", [128, 4, 129], BF16)
            MSET('pool', Vst[:], 1.0)
            qT_P = sbuf(e2, "qT_P", [128, 4, OWN * 128], BF16)
            qT_S = sbuf(e2, "qT_S", [128, 4, NST * 128], BF16)

            def norm_rope(zsrc, g64, cstile, out_bf):
                z3 = zsrc[:].rearrange("p (h n) -> p h n", h=8)
                TT('pool', qn[:], zsrc[:], zsrc[:], ALU.mult)
                RED('dve', s8q[:], qn[:].rearrange("p (h n) -> p h n", h=8))
                rsqrt_small(s8q[:], s8q[:], 1.0 / 64, 1e-6)
                TT('dve', qn[:].rearrange("p (h n) -> p h n", h=8), z3, bc(s8q[:].unsqueeze(2), [128, 8, 64]), ALU.mult)
                TT('pool', qn[:].rearrange("p (h n) -> p h n", h=8), qn[:].rearrange("p (h n) -> p h n", h=8),
                   bc(g64[:].unsqueeze(1), [128, 8, 64]), ALU.mult)
                q4 = qn[:].rearrange("p (h c n) -> p h c n", h=8, c=2)
                o4 = out_bf[:].rearrange("p (h c n) -> p h c n", h=8, c=2)
                x1, x2 = q4[:, :, 0, :], q4[:, :, 1, :]
                cosb = bc(cstile[:, 0:32].unsqueeze(1), [128, 8, 32])
                sinb = bc(cstile[:, 32:64].unsqueeze(1), [128, 8, 32])
                a1 = u1[:].rearrange("p (h n) -> p h n", h=8)
                a2 = u2[:].rearrange("p (h n) -> p h n", h=8)
                TT('dve', a1, x1, cosb, ALU.mult)
                TT('pool', a2, x2, sinb, ALU.mult)
                TT('dve', o4[:, :, 0, :], a1, a2, ALU.subtract)
                TT('pool', a1, x2, cosb, ALU.mult)
                TT('dve', a2, x1, sinb, ALU.mult)
                TT('pool', o4[:, :, 1, :], a1, a2, ALU.add)

            ZK = [zk, sbuf(e2, "zk1", [128, 512])]
            ZV = [zv, sbuf(e2, "zv1", [128, 512])]
            CS = [cs, sbuf(e2, "cs1", [128, 64])]

            def kv_pass(xsrc, n_tiles, cs_src, kT_s, V_s):
                def stage_a(t):
                    S.dma(CS[t % 2][:], cs_src[t, :, :])

                    def evac(gi, pz, rs):
                        ACT([ZK, ZV][gi][t % 2][:], pz, AF.Copy, scale=rs[:, 0:1])
                    project_tile(xsrc[t, :, :], wda, [(512, 1024), (1024, 1536)], evac)

                def stage_b(t):
                    norm_rope(ZK[t % 2], gk, CS[t % 2], rb)
                    for p in range(4):
                        TR(K(pT[:, p * 128:(p + 1) * 128], "pT0"), rb[:, p * 128:(p + 1) * 128], idb[:])
                    CP('act', kTst[:], K(pT[:, 0:512].rearrange("p (a b) -> p a b", a=4), "pT0"))
                    S.dma(kT_s[:, :, t * 128:(t + 1) * 128].rearrange("h p t -> p h t"), kTst[:])
                    CP('act', Vst[:, :, 0:128], ZV[t % 2][:].rearrange("p (h n) -> p h n", h=4))
                    S.dma(V_s[:, :, t, :].rearrange("h p n -> p h n"), Vst[:])
                stage_a(0)
                for t in range(n_tiles):
                    S.rec_begin()
                    if t + 1 < n_tiles:
                        stage_a(t + 1)
                    A_ = S.rec_end()
                    S.rec_begin()
                    stage_b(t)
                    B_ = S.rec_end()
                    S.replay(S.merge(A_, B_))

            def q_pass(xsrc, tile0, n_tiles, cs_src, qT):
                for t in range(n_tiles):
                    S.dma(cs[:], cs_src[t, :, :])

                    def evac(gi, pz, rs):
                        ACT(zq[:], pz, AF.Copy, scale=rs[:, 0:1])
                    project_tile(xsrc[tile0 + t, :, :], wda, [(0, 512)], evac)
                    norm_rope(zq, gq, cs, rb)
                    for p in range(4):
                        TR(K(pT[:, p * 128:(p + 1) * 128], "pT0"), rb[:, p * 128:(p + 1) * 128], idb[:])
                    CP('act', qT[:, :, t * 128:(t + 1) * 128], K(pT[:, 0:512].rearrange("p (a b) -> p a b", a=4), "pT0"))

            kv_pass(xP, NPT, csP, kT_sP, V_sP)
            kv_pass(xsmp[0, 1:NST + 1], NST, csP, kT_sS, V_sS)
            q_pass(xown[0], 1, OWN, csQ, qT_P)
            q_pass(xsmp[0], 1, NST, csP, qT_S)

            PTb = [sbuf(e2, "PTb%d" % i, [128, 512], BF16) for i in range(2)]
            o0 = sbuf(e2, "o0", [128, 128])
            o1 = sbuf(e2, "o1", [128, 128])
            rc = sbuf(e2, "rc", [128, 2])
            ssd = sbuf(e2, "ssd", [128, 1])
            ob = sbuf(e2, "ob", [128, 128], BF16)
            oT = sbuf(e2, "oT", [128, 128], BF16)
            pO = [[pA, pB], [pC, pD]]
            pS = [pL0, pL1]

            osv = [sbuf(e2, "osv%d" % j, [128, 128]) for j in range(4)]
            pOb = [pA, pB, pC, pD]

            def attention(qT, n_q_tiles, kT_s, V_s, n_kt, tok_base):
                kTh = sbuf(e2a, "kTh_%d" % tok_base, [128, n_kt * 128], BF16)
                Vh = sbuf(e2a, "Vh_%d" % tok_base, [128, n_kt, 129], BF16)
                cnt = 0
                for h in range(4):
                    S.dma(kTh[:], kT_s[h, :, :])
                    S.dma(Vh[:], V_s[h, :, :, :])
                    for qg in range(0, n_q_tiles, 4):
                        nq = min(4, n_q_tiles - qg)
                        for c in range(2):
                            rows = slice(c * 64, c * 64 + 64)
                            def qk(kt_, slot):
                                MM(pS[slot % 2][:, 0:nq * 128], kTh[rows, kt_ * 128:(kt_ + 1) * 128], qT[rows, h, qg * 128:(qg + nq) * 128])
                            qk(0, cnt)
                            for kt in range(n_kt):
                                ps_ = pS[cnt % 2]
                                pt_ = PTb[cnt % 2]
                                if kt + 1 < n_kt:
                                    qk(kt + 1, cnt + 1)
                                cnt += 1
                                ACT(pt_[:, 0:nq * 128], ps_[:, 0:nq * 128], AF.Exp, scale=0.125)
                                for j in range(nq):
                                    MM(pOb[j][:, 0:129], pt_[:, j * 128:(j + 1) * 128], Vh[:, kt, :], start=(kt == 0), stop=(kt == n_kt - 1))
                            for j in range(nq):
                                RECIP(rc[:, c:c + 1], pOb[j][:, 128:129])
                                if c == 0:
                                    ACT(osv[j][:], pOb[j][:, 0:128], AF.Copy, scale=rc[:, 0:1])
                                else:
                                    TT('dve', rc[:, 1:2], rc[:, 1:2], neglam[:], ALU.mult)
                                    STT('dve', o1[:], pOb[j][:, 0:128], rc[:, 1:2], osv[j][:], ALU.mult, ALU.add)
                                    ACT(o0[:], o1[:], AF.Square, accum=ssd[:])
                                    rsqrt_small(ssd[:], ssd[:], 1.0 / 128, 1e-6)
                                    STT('dve', ob[:], o1[:], ssd[:, 0:1], gsub[:], ALU.mult, ALU.mult)
                                    TR(pT[:, 0:128], ob[:], idb[:])
                                    CP('act', oT[:], pT[:, 0:128])
                                    tk = tok_base + (qg + j) * 128
                                    S.dma(mixT_s[h, :, tk:tk + 128], oT[:])

            S.barrier()
            with ExitStack() as e2a:
                attention(qT_P, OWN, kT_sP, V_sP, NPT, 0)
            S.barrier()
            with ExitStack() as e2a:
                attention(qT_S, NST, kT_sS, V_sS, NST, OWN * 128)

        S.barrier()
        with ExitStack() as e3:
            wo = sbuf(e3, "wo", [128, 8, D], BF16)
            w2 = sbuf(e3, "w2", [128, 32, D], BF16)
            w1blk = [sbuf(e3, "w1blk%d" % i, [128, 8, 512], BF16) for i in range(2)]
            wcast = [sbuf(e3, "wcast%d" % i, [128, WS], BF16) for i in range(2)]
            wstage[0], wstage[1] = [sbuf(e3, "wstg3_%d" % i, [128, WS]) for i in range(2)]
            wc = 0
            for kc in range(8):
                for s0 in range(0, D, WS):
                    ws = wstage[xcount[0] % 2]
                    xcount[0] += 1
                    S.dma(ws[:, 0:WS], w_out[kc * 128:(kc + 1) * 128, s0:s0 + WS])
                    CP('pool', wo[:, kc, s0:s0 + WS], ws[:, 0:WS])
            for fc in range(32):
                for s0 in range(0, D, WS):
                    ws = wstage[xcount[0] % 2]
                    xcount[0] += 1
                    S.dma(ws[:, 0:WS], w_ff2[fc * 128:(fc + 1) * 128, s0:s0 + WS])
                    CP('pool', w2[:, fc, s0:s0 + WS], ws[:, 0:WS])
            for kc in range(8):
                for s0 in range(0, DFF, WS):
                    ws = wstage[xcount[0] % 2]
                    xcount[0] += 1
                    wcb = wcast[wc % 2]
                    wc += 1
                    S.dma(ws[:, 0:WS], w_ff1[kc * 128:(kc + 1) * 128, s0:s0 + WS])
                    TS('pool', wcb[:], ws[:, 0:WS], g2T[:, kc:kc + 1], None, ALU.mult)
                    S.dma(w1b_s[kc, :, s0:s0 + WS], wcb[:])
            GT = 4
            mixg = sbuf(e3, "mixg", [128, 8, GT * 128], BF16)
            xmid = [sbuf(e3, "xmid%d" % j, [128, D]) for j in range(GT)]
            rs2 = sbuf(e3, "rs2", [128, GT])
            xmb = sbuf(e3, "xmb", [128, D], BF16)
            xmT = sbuf(e3, "xmT", [128, 8, GT * 128], BF16)
            uT = sbuf(e3, "uT", [128, 32, GT * 128], BF16)
            rl = [sbuf(e3, "rl%d" % i, [128, GT * 128]) for i in range(2)]
            yo = sbuf(e3, "yo", [128, D])
            own_src = [(xown[0], 1 + t, out_p, t) for t in range(OWN)] + [(xsmp[0], 1 + t, out_s, t) for t in range(NST)]
            wbc = 0
            for g0 in range(0, NOWN, GT):
                ng = min(GT, NOWN - g0)
                nt = ng * 128
                S.dma(mixg[:, :, 0:nt], mixT_s[:, :, g0 * 128:g0 * 128 + nt].rearrange("c p t -> p c t"))
                for j in range(ng):
                    src, sidx, _, _ = own_src[g0 + j]
                    xr = xt[xcount[0] % 2]
                    xcount[0] += 1
                    S.dma(xr[:], src[sidx, :, :])
                    for hf in range(2):
                        pz = [pZ, pL0][hf]
                        for ch in range(8):
                            MM(pz[:], mixg[:, ch, j * 128:(j + 1) * 128], wo[:, ch, hf * 512:(hf + 1) * 512], start=(ch == 0), stop=(ch == 7))
                        TT('dve', xmid[j][:, hf * 512:(hf + 1) * 512], pz[:], xr[:, hf * 512:(hf + 1) * 512], ALU.add)
                    ACT(junk[:], xmid[j][:], AF.Square, accum=ssq[:])
                    rsqrt_small(rs2[:, j:j + 1], ssq[:], 1.0 / D, 1e-6)
                    CP('pool', xmb[:], xmid[j][:])
                    for hlf in range(2):
                        for k4 in range(4):
                            kc = hlf * 4 + k4
                            TR(K(pT[:, hlf * 512 + k4 * 128: hlf * 512 + (k4 + 1) * 128], "pT%d" % hlf), xmb[:, kc * 128:(kc + 1) * 128], idb[:])
                        CP('act' if hlf == 0 else 'dve', xmT[:, hlf * 4:(hlf + 1) * 4, j * 128:(j + 1) * 128],
                           K(pT[:, hlf * 512:(hlf + 1) * 512].rearrange("p (a b) -> p a b", a=4), "pT%d" % hlf))
                    TT('dve', rs2[:, j:j + 1], rs2[:, j:j + 1], rs2[:, j:j + 1], ALU.mult)
                for fb in range(8):
                    wb_ = w1blk[wbc % 2]
                    wbc += 1
                    S.dma(wb_[:], w1b_s[:, :, fb * 512:(fb + 1) * 512].rearrange("k p f -> p k f"))
                    for f4 in range(4):
                        fc = fb * 4 + f4
                        pf = [pA, pB, pC, pD][fc % 4]
                        for kc in range(8):
                            MM(pf[:, 0:nt], wb_[:, kc, f4 * 128:(f4 + 1) * 128], xmT[:, kc, 0:nt], start=(kc == 0), stop=(kc == 7))
                        r_ = rl[fc % 2]
                        ACT(r_[:, 0:nt], pf[:, 0:nt], AF.Relu)
                        TT('pool' if fc % 2 else 'dve', uT[:, fc, 0:nt], r_[:, 0:nt], r_[:, 0:nt], ALU.mult)
                for j in range(ng):
                    _, _, dst, didx = own_src[g0 + j]
                    for hf in range(2):
                        pz = [pL1, pZ][hf]
                        for fc in range(32):
                            MM(pz[:], uT[:, fc, j * 128:(j + 1) * 128], w2[:, fc, hf * 512:(hf + 1) * 512], start=(fc == 0), stop=(fc == 31))
                        STT('dve', yo[:, hf * 512:(hf + 1) * 512], pz[:], rs2[:, j:j + 1], xmid[j][:, hf * 512:(hf + 1) * 512], ALU.mult, ALU.add)
                    S.dma(dst[didx, :, :], yo[:])
        S.finish('sp')
        S.finish('pool')
        build_nc.ninst = S.ninst
    return nc


def _rope_tables(n_pos):
    inv_freq = (1.0 / (10000.0 ** (np.arange(0, 64, 2, dtype=np.float32) / np.float32(64)))).astype(np.float32)
    ang = np.arange(n_pos, dtype=np.float32)[:, None] * inv_freq[None, :]
    return np.concatenate([np.cos(ang), np.sin(ang)], axis=-1).astype(np.float32)


def _consts():
    s = np.arange(128)[:, None]
    t = np.arange(128)[None, :]
    maskA = np.zeros((2, 128, 1536), np.float32)
    ltri = np.zeros((2, 128, 256), np.float32)
    for d in range(2):
        strict = (s < t) if d == 0 else (s > t)
        incl = (s <= t) if d == 0 else (s >= t)
        maskA[d] = np.concatenate([strict] * 4 + [incl] * 4 + [strict.T] * 4, axis=1).astype(np.float32)
        ltri[d, :, 0:128] = -CDEC * incl
        ltri[d, :, 128:256] = -CDEC * strict
    return maskA, ltri


_NC_CACHE = {}


def kernel(**inputs):
    f32 = lambda a: np.ascontiguousarray(np.asarray(a, dtype=np.float32))
    x_prompt = f32(inputs["x_prompt"])
    x_sample = f32(inputs["x_sample"])
    SEQ = x_prompt.shape[1]
    DSEQ = x_sample.shape[1]
    NPT, NST = SEQ // 128, DSEQ // 128
    OWN = NPT // NCORES
    key = (NPT, NST)
    if key not in _NC_CACHE:
        _NC_CACHE[key] = build_nc(NPT, NST)
    nc = _NC_CACHE[key]
    xp = x_prompt[0].reshape(NPT, 128, D)
    ztile = np.zeros((1, 128, D), np.float32)
    cs_all = _rope_tables(max(SEQ, DSEQ)).reshape(-1, 128, 64)
    maskA, ltri = _consts()
    shared = {
        "xP": xp, "csP": cs_all[:NPT],
        "w_in": f32(inputs["w_in"][0]), "w_out": f32(inputs["w_out"][0]),
        "w_ff1": f32(inputs["w_ff1"][0]), "w_ff2": f32(inputs["w_ff2"][0]),
        "norm1_g": f32(f32(inputs["norm1_g"][0]).reshape(8, 128).T), "norm2_g": f32(f32(inputs["norm2_g"][0]).reshape(8, 128).T),
        "w0": f32(inputs["w0"][0]), "a0": f32(inputs["a0"][0]),
        "w_up": f32(inputs["w_up"][0]), "a_up": f32(inputs["a_up"][0]), "g_up": f32(inputs["g_up"][0]),
        "ident_d": np.eye(128, dtype=np.float32), "maskA_d": maskA, "ltri_d": ltri,
    }
    for nm in ["q_norm_g", "k_norm_g", "lam_q1", "lam_k1", "lam_q2", "lam_k2", "subln_g", "mu_prev",
               "mu_next", "k_k", "k_a", "r_k", "ln_x_g", "ln_x_b"]:
        shared[nm] = f32(inputs[nm][0]).reshape(1, -1)
    xtok = x_prompt[0]
    SEGT = OWN * 128
    NG = NCORES - 1

    def seg_halo(tok, s, segt):
        n = tok.shape[0]
        out = np.zeros((segt + 256, D), np.float32)
        lo, hi = s * segt - 128, (s + 1) * segt + 128
        a, b = max(lo, 0), min(hi, n)
        out[a - lo:b - lo] = tok[a:b]
        return out

    mu_p = f32(inputs["mu_prev"][0]); mu_n = f32(inputs["mu_next"][0])
    dir_par = []
    for d in range(2):
        dir_par.append(dict(mu=np.stack([mu_p, mu_n] if d == 0 else [mu_n, mu_p]),
                            w0=f32(inputs["w0"][0, d]), a0=f32(inputs["a0"][0, d]),
                            wup=f32(inputs["w_up"][0, d]), aup=f32(inputs["a_up"][0, d])))
    Jm = np.ascontiguousarray(np.eye(128, dtype=np.float32)[::-1])
    in_maps = []
    for c in range(NCORES):
        m = dict(shared)
        groups, dirs = [], []
        for g in range(NG):
            if g < c:
                groups.append(seg_halo(xtok, g, SEGT)); dirs.append(0)
            else:
                groups.append(seg_halo(xtok, NG + c - g, SEGT)[::-1]); dirs.append(1)
        m["xctx"] = np.ascontiguousarray(np.stack(groups)).reshape(NG, OWN + 2, 128, D)
        own = seg_halo(xtok, c, SEGT)
        m["xown"] = np.ascontiguousarray(np.stack([own, own[::-1]])).reshape(2, OWN + 2, 128, D)
        smp = seg_halo(x_sample[c], 0, DSEQ)
        m["xsmp"] = np.ascontiguousarray(np.stack([smp, smp[::-1]])).reshape(2, NST + 2, 128, D)
        dirs = dirs + [0, 1]
        m["grp_mu"] = np.ascontiguousarray(np.stack([dir_par[d]["mu"] for d in dirs]))
        m["grp_w0"] = np.ascontiguousarray(np.stack([dir_par[d]["w0"] for d in dirs]))
        m["grp_a0"] = np.ascontiguousarray(np.stack([dir_par[d]["a0"] for d in dirs]))
        m["grp_wup"] = np.ascontiguousarray(np.stack([dir_par[d]["wup"] for d in dirs]))
        m["grp_aup"] = np.ascontiguousarray(np.stack([dir_par[d]["aup"] for d in dirs]))
        fl = np.zeros((1, 8), np.float32); fl[0, c] = 1.0
        m["flags_d"] = fl
        m["J_d"] = Jm
        m["csQ"] = np.ascontiguousarray(cs_all[OWN * c:OWN * (c + 1)])
        in_maps.append(m)
    if inputs.get("_maps_only"):
        return nc, in_maps
    res = run_bass_kernel_spmd(nc, in_maps, core_ids=list(range(NCORES)))
    y_prompt = np.concatenate([res.results[c]["out_p"].reshape(OWN * 128, D) for c in range(NCORES)], axis=0)[None]
    y_sample = np.stack([res.results[c]["out_s"].reshape(NST * 128, D) for c in range(NCORES)], axis=0)
    return (y_prompt.astype(np.float32), y_sample.astype(np.float32))
```

```python
import math
import numpy as np
from contextlib import ExitStack
import concourse.bass as bass
import concourse.mybir as mybir
from concourse.bass_utils import run_bass_kernel_spmd

F32 = mybir.dt.float32
BF16 = mybir.dt.bfloat16
AF = mybir.ActivationFunctionType
ALU = mybir.AluOpType
AX = mybir.AxisListType

D = 1024
NCORES = 8
DA_COLS = 1536
RW_COLS = 1792
IN_COLS = 3328
DFF = 4096
CDEC = math.exp(-0.5)
LAMBDA_INIT = 0.8 - 0.6 * math.exp(0.0)


class Sched:
    def __init__(self, nc, es, n_dma_sems=24):
        self.nc = nc
        self.E = {'pe': nc.tensor, 'act': nc.scalar, 'dve': nc.vector, 'pool': nc.gpsimd, 'sp': nc.sync}
        self.sem = {k: es.enter_context(nc.semaphore("s_" + k)) for k in ['pe', 'act', 'dve', 'pool']}
        self.cnt = {k: 0 for k in self.sem}
        self.seen = {k: {} for k in self.E}
        self.lastw = {}
        self.readers = {}
        self.dsems = [es.enter_context(nc.semaphore("d%d" % i)) for i in range(n_dma_sems)]
        self.dcnt = [0] * n_dma_sems
        self.dnext = 0
        self.ninst = 0
        self.psum_names = set()
        self._rec = None

    def _key(self, a):
        if isinstance(a, tuple):
            if a[0].name in self.psum_names:
                return a[0].name
            return a[1]
        return a.name

    def _ap(self, a):
        return a[0] if isinstance(a, tuple) else a

    def _wait(self, eng, tok):
        sem, val, name = tok
        if self.seen[eng].get(name, 0) >= val:
            return
        self.E[eng].wait_ge(sem, val)
        self.seen[eng][name] = val

    def _deps(self, eng, ins, outs):
        toks = []
        for a in ins:
            k = self._key(a)
            if k in self.lastw:
                toks.append(self.lastw[k])
            if k in self.psum_names:
                toks.extend(t for e, t in self.readers.get(k, {}).items() if e != eng)
        for a in outs:
            k = self._key(a)
            if k in self.lastw:
                toks.append(self.lastw[k])
            toks.extend(self.readers.get(k, {}).values())
        for t in toks:
            if eng == 'pe' and t[2] == 'pe':
                continue
            self._wait(eng, t)

    def _commit(self, tok, ins, outs):
        for a in outs:
            k = self._key(a)
            self.lastw[k] = tok
            self.readers[k] = {}
        for a in ins:
            k = self._key(a)
            self.readers.setdefault(k, {})[tok[2]] = tok

    def rec_begin(self):
        self._rec = []

    def rec_end(self):
        r, self._rec = self._rec, None
        return r

    def replay(self, items):
        for it in items:
            if it[0] == 'op':
                self.op(*it[1:])
            else:
                self.dma(it[1], it[2], q=it[3])

    @staticmethod
    def merge(a, b):
        out, ia, ib = [], 0, 0
        while ia < len(a) or ib < len(b):
            if ib >= len(b) or (ia < len(a) and ia * len(b) <= ib * len(a)):
                out.append(a[ia]); ia += 1
            else:
                out.append(b[ib]); ib += 1
        return out

    def op(self, eng, fn, outs, ins):
        if _STOPPED[0]:
            return None
        if self._rec is not None:
            self._rec.append(('op', eng, fn, outs, ins))
            return None
        self._deps(eng, ins, outs)
        inst = fn()
        self.cnt[eng] += 1
        self.ninst += 1
        inst.then_inc(self.sem[eng], 1)
        tok = (self.sem[eng], self.cnt[eng], eng)
        self._commit(tok, ins, outs)
        return tok

    def dma(self, out, in_, q='sp'):
        if _STOPPED[0]:
            return None
        if self._rec is not None:
            self._rec.append(('dma', out, in_, q))
            return None
        i = self.dnext
        self.dnext = (self.dnext + 1) % len(self.dsems)
        name = 'dma%d' % i
        if self.dcnt[i] > 0:
            self._wait(q, (self.dsems[i], 16 * self.dcnt[i], name))
        self._deps(q, [in_], [out])
        self.E[q].dma_start(out=self._ap(out), in_=self._ap(in_)).then_inc(self.dsems[i], 16)
        self.dcnt[i] += 1
        self.ninst += 1
        tok = (self.dsems[i], 16 * self.dcnt[i], name)
        self._commit(tok, [in_], [out])
        return tok

    def barrier(self):
        if _STOPPED[0]:
            return
        for e in self.E:
            for k in self.sem:
                if self.cnt[k] > 0:
                    self._wait(e, (self.sem[k], self.cnt[k], k))
            for i in range(len(self.dsems)):
                if self.dcnt[i] > 0:
                    self._wait(e, (self.dsems[i], 16 * self.dcnt[i], 'dma%d' % i))
        self.lastw = {}
        self.readers = {}

    def finish(self, q='sp'):
        for i in range(len(self.dsems)):
            if self.dcnt[i] > 0:
                self._wait(q, (self.dsems[i], 16 * self.dcnt[i], 'dma%d' % i))


class _Stop(Exception):
    pass


import os as _os
_KSTOP = _os.environ.get('KSTOP', '')


_STOPPED = [False]


def _stop(tag):
    if _KSTOP == tag:
        _STOPPED[0] = True


def K(ap, key):
    return (ap, key)


def build_nc(NPT, NST, inv_dt=BF16):
    OWN = NPT // NCORES
    NOWN = OWN + NST
    NTOK = NOWN * 128
    nc = bass.Bass("TRN2", target_bir_lowering=False)
    dram = lambda name, shape, dt=F32, kind="ExternalInput": nc.dram_tensor(name, shape, dt, kind=kind).ap()
    NG = NCORES - 1
    xctx = dram("xctx", [NG, OWN + 2, 128, D])
    xown = dram("xown", [2, OWN + 2, 128, D])
    xP = dram("xP", [NPT, 128, D])
    xsmp = dram("xsmp", [2, NST + 2, 128, D])
    grp_mu = dram("grp_mu", [NG + 2, 2, RW_COLS])
    grp_w0 = dram("grp_w0", [NG + 2, 512])
    grp_a0 = dram("grp_a0", [NG + 2, 512])
    grp_wup = dram("grp_wup", [NG + 2, 64, 512])
    grp_aup = dram("grp_aup", [NG + 2, 64, 512])
    flags_d = dram("flags_d", [1, 8])
    J_d = dram("J_d", [128, 128])
    csP = dram("csP", [NPT, 128, 64])
    csQ = dram("csQ", [OWN, 128, 64])
    w_in = dram("w_in", [D, IN_COLS])
    w_out = dram("w_out", [D, D])
    w_ff1 = dram("w_ff1", [D, DFF])
    w_ff2 = dram("w_ff2", [DFF, D])
    norm1_g = dram("norm1_g", [128, 8])
    norm2_g = dram("norm2_g", [128, 8])
    vec = {}
    for nm, n in [("q_norm_g", 64), ("k_norm_g", 64), ("lam_q1", 64), ("lam_k1", 64), ("lam_q2", 64),
                  ("lam_k2", 64), ("subln_g", 128), ("mu_prev", RW_COLS), ("mu_next", RW_COLS),
                  ("k_k", 512), ("k_a", 512), ("r_k", 512), ("ln_x_g", 512), ("ln_x_b", 512)]:
        vec[nm] = dram(nm, [1, n])
    w0 = dram("w0", [2, 512])
    a0 = dram("a0", [2, 512])
    w_up = dram("w_up", [2, 64, 512])
    a_up = dram("a_up", [2, 64, 512])
    g_up = dram("g_up", [128, 512])
    ident_d = dram("ident_d", [128, 128])
    maskA_d = dram("maskA_d", [2, 128, 1536])
    ltri_d = dram("ltri_d", [2, 128, 256])
    out_p = dram("out_p", [OWN, 128, D], kind="ExternalOutput")
    out_s = dram("out_s", [NST, 128, D], kind="ExternalOutput")
    scr = lambda name, shape, dt=F32: nc.dram_tensor(name, shape, dt).ap()
    yF_s = scr("yF_s", [NOWN, 128, 512])
    bonF_s = scr("bonF_s", [NOWN, 128, 8])
    mixT_s = scr("mixT_s", [8, 128, NTOK], BF16)
    kT_sP = scr("kT_sP", [4, 128, NPT * 128], BF16)
    kT_sS = scr("kT_sS", [4, 128, NST * 128], BF16)
    V_sP = scr("V_sP", [4, 128, NPT, 129], BF16)
    V_sS = scr("V_sS", [4, 128, NST, 129], BF16)
    w1b_s = scr("w1b_s", [8, 128, DFF], BF16)

    with ExitStack() as es:
        S = Sched(nc, es)
        V, G, A, P = nc.vector, nc.gpsimd, nc.scalar, nc.tensor

        def sbuf(stack, name, shape, dt=F32):
            return stack.enter_context(nc.sbuf_tensor(name, shape, dt))

        def psum(stack, name, shape, dt=F32):
            S.psum_names.add(name)
            return stack.enter_context(nc.psum_tensor(name, shape, dt))

        def MM(out, lhsT, rhs, start=True, stop=True):
            S.op('pe', lambda: P.matmul(S._ap(out), S._ap(lhsT), S._ap(rhs), start=start, stop=stop), [out], [lhsT, rhs])

        def TR(out, in_, idt):
            S.op('pe', lambda: P.transpose(S._ap(out), S._ap(in_), S._ap(idt)), [out], [in_, idt])

        def ACT(out, in_, func, bias=None, scale=None, accum=None, extra_out=()):
            kw = {}
            ins = [in_]
            if bias is not None:
                kw['bias'] = S._ap(bias) if not isinstance(bias, float) else bias
                if not isinstance(bias, float):
                    ins.append(bias)
            if scale is not None:
                kw['scale'] = S._ap(scale) if not isinstance(scale, float) else scale
                if not isinstance(scale, float):
                    ins.append(scale)
            outs = [out] + list(extra_out)
            if accum is not None:
                kw['accum_out'] = S._ap(accum)
                outs.append(accum)
            S.op('act', lambda: A.activation(S._ap(out), S._ap(in_), func, **kw), outs, ins)

        def EW(eng):
            return {'dve': V, 'pool': G}[eng]

        def TT(eng, out, a, b, op):
            S.op(eng, lambda: EW(eng).tensor_tensor(S._ap(out), S._ap(a), S._ap(b), op), [out], [a, b])

        def TS(eng, out, a, s1, s2, op0, op1=None):
            ins = [a] + [s for s in (s1, s2) if s is not None and not isinstance(s, float)]
            g = lambda s: s if (s is None or isinstance(s, float)) else S._ap(s)
            if op1 is None:
                S.op(eng, lambda: EW(eng).tensor_scalar(S._ap(out), S._ap(a), g(s1), None, op0), [out], ins)
            else:
                S.op(eng, lambda: EW(eng).tensor_scalar(S._ap(out), S._ap(a), g(s1), g(s2), op0, op1), [out], ins)

        def STT(eng, out, a, s, b, op0, op1):
            ins = [a, b] + ([] if isinstance(s, float) else [s])
            sv = s if isinstance(s, float) else S._ap(s)
            S.op(eng, lambda: EW(eng).scalar_tensor_tensor(S._ap(out), S._ap(a), sv, S._ap(b), op0, op1), [out], ins)

        def CP(eng, out, in_):
            if eng == 'act':
                S.op('act', lambda: A.copy(S._ap(out), S._ap(in_)), [out], [in_])
            else:
                S.op(eng, lambda: EW(eng).tensor_copy(S._ap(out), S._ap(in_)), [out], [in_])

        def RED(eng, out, in_, op=ALU.add):
            S.op(eng, lambda: EW(eng).tensor_reduce(S._ap(out), S._ap(in_), AX.X, op), [out], [in_])

        def MSET(eng, out, val):
            S.op(eng, lambda: EW(eng).memset(S._ap(out), val), [out], [])

        def RECIP(out, in_):
            S.op('dve', lambda: V.reciprocal(S._ap(out), S._ap(in_)), [out], [in_])

        def rsqrt_small(out, in_, mul, add):
            TS('dve', out, in_, float(mul), float(add), ALU.mult, ALU.add)
            S.op('act', lambda: A.sqrt(S._ap(out), S._ap(out)), [out], [out])
            RECIP(out, out)

        def bc(ap, shape):
            return ap.to_broadcast(shape)

        idf = sbuf(es, "idf", [128, 128])
        idb = sbuf(es, "idb", [128, 128], BF16)
        g1T = sbuf(es, "g1T", [128, 8])
        g2T = sbuf(es, "g2T", [128, 8])
        S.dma(idf[:], ident_d[:, :])
        CP('dve', idb[:], idf[:])
        S.dma(g1T[:], norm1_g[:, :])
        S.dma(g2T[:], norm2_g[:, :])

        pT = psum(es, "pT", [128, 1024], BF16)
        pZ = psum(es, "pZ", [128, 512])
        pL0 = psum(es, "pL0", [128, 512])
        pL1 = psum(es, "pL1", [128, 512])
        pA = psum(es, "pA", [128, 512])
        pB = psum(es, "pB", [128, 512])
        pC = psum(es, "pC", [128, 512])
        pD = psum(es, "pD", [128, 512])

        xt = [sbuf(es, "xt%d" % i, [128, D]) for i in range(2)]
        xb = sbuf(es, "xb", [128, D], BF16)
        junk = xb
        xT = sbuf(es, "xT", [128, 8, 128], BF16)
        ssq = sbuf(es, "ssq", [128, 1])
        rstd = sbuf(es, "rstd", [128, 1])
        WS = 512
        wstage = [None, None]
        xcount = [0]

        def load_weight_bf16(dst, src_cols, gT):
            c0, c1 = src_cols
            n = c1 - c0
            for kc in range(8):
                for s0 in range(0, n, WS):
                    s1 = min(n, s0 + WS)
                    ws = wstage[xcount[0] % 2]
                    xcount[0] += 1
                    S.dma(ws[:, 0:s1 - s0], w_in[kc * 128:(kc + 1) * 128, c0 + s0:c0 + s1])
                    TS('pool', dst[:, kc, s0:s1], ws[:, 0:s1 - s0], gT[:, kc:kc + 1], None, ALU.mult)

        def project_tile(src_ap, w_b, col_groups, evac):
            xtile = xt[xcount[0] % 2]
            xcount[0] += 1
            S.dma(xtile[:], src_ap)
            ACT(junk[:], xtile[:], AF.Square, accum=ssq[:])
            rsqrt_small(rstd[:], ssq[:], 1.0 / D, 1e-6)
            CP('pool', xb[:], xtile[:])
            for hlf in range(2):
                for k4 in range(4):
                    kc = hlf * 4 + k4
                    TR(K(pT[:, hlf * 512 + k4 * 128: hlf * 512 + (k4 + 1) * 128], "pT%d" % hlf), xb[:, kc * 128:(kc + 1) * 128], idb[:])
                eng = 'act' if hlf == 0 else 'dve'
                CP(eng, K(xT[:, hlf * 4:(hlf + 1) * 4, :], "xT%d" % hlf),
                   K(pT[:, hlf * 512:(hlf + 1) * 512].rearrange("p (a b) -> p a b", a=4), "pT%d" % hlf))
            zb_ = [pZ, pL0, pL1]
            for gi, (c0, c1) in enumerate(col_groups):
                n = c1 - c0
                pz_ = zb_[gi % 3]
                for kc in range(8):
                    MM(pz_[:, 0:n], K(xT[:, kc, :], "xT%d" % (kc // 4)), w_b[:, kc, c0:c1], start=(kc == 0), stop=(kc == 7))
                evac(gi, pz_[:, 0:n], rstd)

        with ExitStack() as e1:
            wrw = sbuf(e1, "wrw", [128, 8, RW_COLS], BF16)
            Z = [sbuf(e1, "Zr%d" % i, [128, RW_COLS]) for i in range(3)]
            wstage[0], wstage[1] = Z[1], Z[2]
            load_weight_bf16(wrw, (DA_COLS, IN_COLS), g1T)
            ZP = [sbuf(e1, "zp%d" % j, [128, RW_COLS]) for j in range(2)]
            zn = sbuf(e1, "zn", [128, RW_COLS])
            mu_p = sbuf(e1, "mu_p", [128, RW_COLS])
            mu_n = sbuf(e1, "mu_n", [128, RW_COLS])
            mu_c = sbuf(e1, "mu_c", [128, RW_COLS])
            S.dma(mu_p[:], vec["mu_prev"].partition_broadcast(128))
            S.dma(mu_n[:], vec["mu_next"].partition_broadcast(128))
            TT('pool', mu_c[:], mu_p[:], mu_n[:], ALU.add)
            TS('pool', mu_c[:], mu_c[:], -1.0, 1.0, ALU.mult, ALU.add)
            bt = {}
            for nm in ["k_k", "k_a", "r_k", "ln_x_g", "ln_x_b"]:
                bt[nm] = sbuf(e1, "b_" + nm, [128, 512])
                S.dma(bt[nm][:], vec[nm].partition_broadcast(128))
            w0a0 = sbuf(e1, "w0a0", [128, 512])
            onesf = sbuf(e1, "onesf", [128, 128])
            MSET('pool', onesf[:], 1.0)
            MSET('pool', w0a0[:], 0.0)
            waup = sbuf(e1, "waup", [128, 512])
            gup = sbuf(e1, "gup", [128, 512])
            S.dma(gup[:], g_up[:, :])
            mS4 = sbuf(e1, "mS4", [128, 512])
            mI4 = sbuf(e1, "mI4", [128, 512])
            mB4 = sbuf(e1, "mB4", [128, 512])
            ltri = sbuf(e1, "ltri", [128, 256])
            negc = sbuf(e1, "negc", [128, 1])
            MSET('pool', negc[:], -CDEC)
            T = {}
            for nm in ["sgw", "arate", "Winv", "We", "kk", "kd", "t0", "t1", "ydir", "yF", "gte", "f1"]:
                T[nm] = sbuf(e1, "T_" + nm, [128, 512])
            T["Wt"] = T["kd"]
            LO = [sbuf(e1, "lo%d" % j, [128, 128]) for j in range(2)]
            LOT = [sbuf(e1, "loT%d" % j, [128, 128]) for j in range(2)]
            sgT = sbuf(e1, "sgT", [128, 128])
            s8 = [sbuf(e1, "s8_%d" % i, [128, 8]) for i in range(6)]
            bonF = sbuf(e1, "bonF", [128, 8])
            rtok = sbuf(e1, "rtok", [128, 512], BF16)
            orw = sbuf(e1, "orw", [128, 512], BF16)
            orwT = sbuf(e1, "orwT", [128, 4, 128], BF16)
            BS = []
            for j in range(2):
                o = {}
                o["arT"] = sbuf(e1, "arT%d" % j, [128, 4, 2, 128], BF16)
                o["bT"] = sbuf(e1, "bT%d" % j, [128, 4, 128], BF16)
                o["kTt"] = sbuf(e1, "kTt%d" % j, [128, 4, 128], BF16)
                o["atok"] = sbuf(e1, "atok%d" % j, [128, 512], BF16)
                o["btok"] = sbuf(e1, "btok%d" % j, [128, 512], BF16)
                o["ktok"] = sbuf(e1, "ktok%d" % j, [128, 512], BF16)
                o["vb"] = sbuf(e1, "vb%d" % j, [128, 512], BF16)
                o["WC"] = sbuf(e1, "WC%d" % j, [128, 4])
                o["vfin"] = sbuf(e1, "vfin%d" % j, [128, 512])
                o["sgs"] = sbuf(e1, "sgs%d" % j, [128, 128])
                o["bon"] = sbuf(e1, "bon%d" % j, [128, 8])
                BS.append(o)
            P1T = sbuf(e1, "P1T", [128, 4, 128], BF16)
            U0 = sbuf(e1, "U0", [128, 512])
            ArbT = sbuf(e1, "ArbT", [128, 8, 128], BF16)
            ArkT = sbuf(e1, "ArkT", [128, 8, 128], BF16)
            NN = [sbuf(e1, "NN%d" % j, [128, 8, 128], inv_dt) for j in range(2)]
            NTt = [sbuf(e1, "NTt%d" % j, [128, 8, 128], inv_dt) for j in range(2)]
            RR = [sbuf(e1, "RR%d" % j, [128, 8, 128], inv_dt) for j in range(2)]
            AakT = sbuf(e1, "AakT", [128, 8, 128], BF16)
            X0a = sbuf(e1, "X0a", [128, 512], BF16)
            idi = sbuf(e1, "idi", [128, 128], inv_dt)
            CP('pool', idi[:], idf[:])
            Hf = sbuf(e1, "Hf", [128, 4, 128])
            Hb = sbuf(e1, "Hb", [128, 4, 128], BF16)
            HfW = sbuf(e1, "HfW", [128, 4, 128])
            tmpH = T["f1"][:].rearrange("p (q t) -> p q t", q=4)
            Ub = sbuf(e1, "Ub", [128, 512], BF16)

            def compute_z(src_ap, zbuf):
                groups = [(0, 512), (512, 1024), (1024, 1536), (1536, 1792)]

                def evac(gi, pz, rs):
                    c0, c1 = groups[gi]
                    ACT(zbuf[:, c0:c1], pz, AF.Copy, scale=rs[:, 0:1])
                project_tile(src_ap, wrw, groups, evac)

            def blk(banks, h):
                return banks[h % 2][:, (h // 2) * 128:(h // 2 + 1) * 128]

            def hrows(h):
                return slice((h % 2) * 64, (h % 2) * 64 + 64)
            f2 = lambda t3, b: t3[:].rearrange("p (q b) t -> p q b t", b=2)[:, :, b, :]
            bv = lambda t2: t2[:].rearrange("p (q t) -> p q t", q=4)
            h8 = lambda ap: ap.rearrange("p (h n) -> p h n", h=8)

            Jf = sbuf(e1, "Jf", [128, 128])
            Jb = sbuf(e1, "Jb", [128, 128], BF16)
            S.dma(Jf[:], J_d[:, :])
            CP('dve', Jb[:], Jf[:])
            flg = sbuf(e1, "flg", [128, 8])
            nflg = sbuf(e1, "nflg", [128, 8])
            S.dma(flg[:], flags_d.partition_broadcast(128))
            TS('dve', nflg[:], flg[:], -1.0, 1.0, ALU.mult, ALU.add)
            Hsave = T["yF"]
            Hbk = T["gte"]
            S.dma(mS4[:], maskA_d[0, :, 0:512])
            S.dma(mI4[:], maskA_d[0, :, 512:1024])
            S.dma(mB4[:], maskA_d[0, :, 1024:1536])
            S.dma(ltri[:], ltri_d[0, :, :])
            Hf2 = Hf[:].rearrange("p q t -> p (q t)")
            Hb2 = Hb[:].rearrange("p q t -> p (q t)")

            def load_group_params(gi):
                S.dma(mu_p[:], grp_mu[gi, 0:1, :].partition_broadcast(128))
                S.dma(mu_n[:], grp_mu[gi, 1:2, :].partition_broadcast(128))
                S.dma(w0a0[0:1, :], grp_w0[gi:gi + 1, :])
                S.dma(w0a0[64:65, :], grp_a0[gi:gi + 1, :])
                S.dma(waup[0:64, :], grp_wup[gi, :, :])
                S.dma(waup[64:128, :], grp_aup[gi, :, :])

            def rwkv_pass(d, xsrc, n_slots, n_own, own_base):
                compute_z(xsrc[0, :, :], Z[0])
                compute_z(xsrc[1, :, :], Z[1])

                def stage12a(i):
                    zp, lo, loT = ZP[i % 2], LO[i % 2], LOT[i % 2]
                    compute_z(xsrc[i + 2, :, :], Z[(i + 2) % 3])
                    zc, za, zb = Z[(i + 1) % 3], Z[(i + 2) % 3], Z[i % 3]
                    zprev_src, znext_src = zb, za
                    S.dma(zp[1:128, :], zc[0:127, :])
                    S.dma(zp[0:1, :], zprev_src[127:128, :])
                    S.dma(zn[0:127, :], zc[1:128, :])
                    S.dma(zn[127:128, :], znext_src[0:1, :])
                    TT('pool', zp[:], zp[:], mu_p[:], ALU.mult)
                    TT('dve', zn[:], zn[:], mu_n[:], ALU.mult)
                    TT('dve', zp[:], zp[:], zn[:], ALU.add)
                    TT('pool', zn[:], zc[:], mu_c[:], ALU.mult)
                    TT('dve', zp[:], zp[:], zn[:], ALU.add)
                    zf = zp
                    r_, k_, v_ = zf[:, 0:512], zf[:, 512:1024], zf[:, 1024:1536]
                    _stop('S12a')
                    ACT(lo[:, 0:64], zf[:, 1536:1600], AF.Tanh)
                    CP('pool', lo[:, 64:128], zf[:, 1600:1664])

                def stage12b(i):
                    own = i >= n_slots - n_own
                    o = BS[i % 2]
                    zf, loT = ZP[i % 2], LOT[i % 2]
                    r_, k_, v_ = zf[:, 0:512], zf[:, 512:1024], zf[:, 1024:1536]
                    TR(pL1[:, 0:128], LO[i % 2][:], idf[:])
                    CP('act', loT[:], pL1[:, 0:128])
                    MM(pL0[:], loT[0:64, :], waup[0:64, :], start=True, stop=False)
                    MM(pL0[:], onesf[0:1, :], w0a0[0:1, :], start=False, stop=True)
                    MM(pL1[:], loT[64:128, :], waup[64:128, :], start=True, stop=False)
                    MM(pL1[:], onesf[64:65, :], w0a0[64:65, :], start=False, stop=True)
                    ACT(T["sgw"][:], pL0[:], AF.Sigmoid)
                    ACT(T["arate"][:], pL1[:], AF.Sigmoid)
                    MM(pL0[:], ltri[:, 0:128], T["sgw"][:])
                    MM(pL1[:], ltri[:, 128:256], T["sgw"][:])
                    for p in range(4):
                        MM(pZ[:, p:p + 1], T["sgw"][:, p * 128:(p + 1) * 128], negc[:], start=True, stop=True)
                    ACT(o["WC"][:], pZ[:, 0:4], AF.Exp)
                    if own:
                        ACT(T["Wt"][:], pL0[:], AF.Exp)
                        TT('dve', rtok[:], r_, T["Wt"][:], ALU.mult)
                    ACT(T["Winv"][:], pL0[:], AF.Exp, scale=-1.0)
                    ACT(T["We"][:], pL1[:], AF.Exp)
                    _stop('S12b')
                    TT('pool', T["kk"][:], k_, bt["k_k"][:], ALU.mult)
                    TT('pool', T["t0"][:], T["kk"][:], T["kk"][:], ALU.mult)
                    RED('dve', s8[0][:], h8(T["t0"][:]))
                    rsqrt_small(s8[0][:], s8[0][:], 1.0, 1e-12)
                    TT('dve', h8(T["kk"][:]), h8(T["kk"][:]), bc(s8[0][:].unsqueeze(2), [128, 8, 64]), ALU.mult)
                    STT('dve', T["t0"][:], T["arate"][:], -1.0, bt["k_a"][:], ALU.add, ALU.mult)
                    STT('dve', T["kd"][:], T["t0"][:], 1.0, k_, ALU.add, ALU.mult)
                    STT('dve', o["atok"][:], T["kk"][:], -1.0, T["We"][:], ALU.mult, ALU.mult)
                    TT('pool', T["t1"][:], T["kk"][:], T["arate"][:], ALU.mult)
                    TT('pool', o["btok"][:], T["t1"][:], T["Winv"][:], ALU.mult)
                    TT('pool', o["ktok"][:], T["kd"][:], T["Winv"][:], ALU.mult)
                    CP('act', o["vb"][:], v_)
                    for p in range(4):
                        TR(pT[:, p * 128:(p + 1) * 128], o["atok"][:, p * 128:(p + 1) * 128], idb[:])
                    for p in range(4):
                        TR(pT[:, 512 + p * 128:512 + (p + 1) * 128], o["btok"][:, p * 128:(p + 1) * 128], idb[:])
                    CP('act', o["arT"][:, :, 0, :], pT[:, 0:512].rearrange("p (a b) -> p a b", a=4))
                    CP('dve', o["bT"][:], pT[:, 512:1024].rearrange("p (a b) -> p a b", a=4))
                    for p in range(4):
                        TR(pT[:, p * 128:(p + 1) * 128], o["ktok"][:, p * 128:(p + 1) * 128], idb[:])
                    if own:
                        for p in range(4):
                            TR(pT[:, 512 + p * 128:512 + (p + 1) * 128], rtok[:, p * 128:(p + 1) * 128], idb[:])
                    CP('act', o["kTt"][:], pT[:, 0:512].rearrange("p (a b) -> p a b", a=4))
                    if own:
                        CP('dve', o["arT"][:, :, 1, :], pT[:, 512:1024].rearrange("p (a b) -> p a b", a=4))
                        TT('pool', T["t0"][:], r_, bt["r_k"][:], ALU.mult)
                        TT('pool', T["t0"][:], T["t0"][:], T["kd"][:], ALU.mult)
                        RED('dve', o["bon"][:], h8(T["t0"][:]))
                        if d == 1:
                            CP('pool', o["vfin"][:], v_)
                            ACT(o["sgs"][:], zf[:, 1664:1792], AF.Sigmoid)

                def stage34(i):
                    own = i >= n_slots - n_own
                    o = BS[i % 2]
                    arT, bT, kTt, atok = o["arT"], o["bT"], o["kTt"], o["atok"]
                    for h in range(8):
                        MM(blk((pA, pB), h), bT[hrows(h), h // 2, :], arT[hrows(h), h // 2, 0, :])
                    for h in range(8):
                        MM(blk((pC, pD), h), kTt[hrows(h), h // 2, :], arT[hrows(h), h // 2, 0, :])
                    for b_, bank in enumerate((pA, pB)):
                        TT('dve', f2(NN[0], b_), bv(bank), bv(mS4), ALU.mult)
                    for b_, bank in enumerate((pC, pD)):
                        TT('dve', f2(AakT, b_), bv(bank), bv(mS4), ALU.mult)
                    for h in range(8):
                        MM(blk((pA, pB), h), arT[hrows(h), h // 2, 0, :], bT[hrows(h), h // 2, :])
                    for b_, bank in enumerate((pA, pB)):
                        TT('dve', f2(NTt[0], b_), bv(bank), bv(mB4), ALU.mult)
                    if own:
                        for h in range(8):
                            MM(blk((pC, pD), h), bT[hrows(h), h // 2, :], arT[hrows(h), h // 2, 1, :])
                        for b_, bank in enumerate((pC, pD)):
                            TT('dve', f2(ArbT, b_), bv(bank), bv(mI4), ALU.mult)
                        for h in range(8):
                            MM(blk((pA, pB), h), kTt[hrows(h), h // 2, :], arT[hrows(h), h // 2, 1, :])
                        for b_, bank in enumerate((pA, pB)):
                            TT('dve', f2(ArkT, b_), bv(bank), bv(mI4), ALU.mult)
                    _stop('A')
                    TT('dve', RR[0][:], NN[0][:], bc(idi[:].unsqueeze(1), [128, 8, 128]), ALU.add)
                    for lev in range(1, 7):
                        cur, nxt = (lev - 1) % 2, lev % 2
                        last = lev == 6
                        if not last:
                            for h in range(8):
                                MM(blk((pA, pB), h), NTt[cur][:, h, :], NN[cur][:, h, :])
                        for h in range(8):
                            MM(blk((pC, pD), h), NN[cur][:, h, :], NTt[cur][:, h, :])
                        if not last:
                            CP('act', f2(NN[nxt], 0), bv(pA))
                            CP('act', f2(NN[nxt], 1), bv(pB))
                        CP('act', f2(NTt[nxt], 0), bv(pC))
                        CP('dve', f2(NTt[nxt], 1), bv(pD))
                        for h in range(8):
                            MM(blk((pA, pB), h), NTt[nxt][:, h, :], RR[cur][:, h, :])
                        TT('dve', f2(RR[nxt], 0), bv(pA), f2(RR[cur], 0), ALU.add)
                        TT('dve', f2(RR[nxt], 1), bv(pB), f2(RR[cur], 1), ALU.add)
                    _stop('B')
                    Rf = RR[0]
                    for h in range(8):
                        MM(blk((pC, pD), h), atok[:, (h // 2) * 128:(h // 2 + 1) * 128], Rf[:, h, :])
                    for hh, bank in enumerate((pC, pD)):
                        src = bank[hh * 64:(hh + 1) * 64, :].rearrange("p (q t) -> p q t", q=4)
                        CP('act' if hh == 0 else 'dve', P1T[hh * 64:(hh + 1) * 64, :, :], src)
                    for h in range(8):
                        MM(pA[:, h * 64:(h + 1) * 64], AakT[:, h, :], o["vb"][:, h * 64:(h + 1) * 64])
                    CP('act', X0a[:], pA[:])
                    for h in range(8):
                        MM(pB[:, h * 64:(h + 1) * 64], Rf[:, h, :], X0a[:, h * 64:(h + 1) * 64])
                    CP('act', U0[:], pB[:])
                    _stop('C')
                    for p in range(4):
                        MM(pC[:, p * 128:(p + 1) * 128], P1T[:, p, :], Hb[:, p, :])
                    TT('dve', Ub[:], pC[:], U0[:], ALU.add)
                    if own:
                        for h in range(8):
                            p, hh = h // 2, h % 2
                            cs_ = slice(h * 64, h * 64 + 64)
                            MM(pD[:, cs_], arT[:, p, 1, :], Hb[:, p, hh * 64:(hh + 1) * 64], start=True, stop=False)
                            MM(pD[:, cs_], ArbT[:, h, :], Ub[:, cs_], start=False, stop=False)
                            MM(pD[:, cs_], ArkT[:, h, :], o["vb"][:, cs_], start=False, stop=True)
                        CP('act', T["ydir"][:], pD[:])
                    for p in range(4):
                        pc_ = slice(p * 128, (p + 1) * 128)
                        MM(pA[:, pc_], o["btok"][:, pc_], Ub[:, pc_], start=True, stop=False)
                        MM(pA[:, pc_], o["ktok"][:, pc_], o["vb"][:, pc_], start=False, stop=True)
                    for hh in range(2):
                        rows = slice(hh * 64, hh * 64 + 64)
                        cols = slice(hh * 64, hh * 64 + 64)
                        WCb = bc(o["WC"][rows, :].unsqueeze(2), [64, 4, 64])
                        TT('dve', HfW[rows, :, cols], Hf[rows, :, cols], WCb, ALU.mult)
                        pblk = pA[rows, :].rearrange("p (q h i) -> p q h i", q=4, h=2)[:, :, hh, :]
                        TT('dve', tmpH[rows, :, cols], pblk, WCb, ALU.mult)
                        TT('dve', Hf[rows, :, cols], tmpH[rows, :, cols], HfW[rows, :, cols], ALU.add)
                        CP('act', Hb[rows, :, cols], Hf[rows, :, cols])
                    _stop('D')
                    if own:
                        if d == 0:
                            ot = own_base + i
                            S.dma(yF_s[ot, :, :], T["ydir"][:])
                            S.dma(bonF_s[ot, :, :], o["bon"][:])
                        else:
                            ot = own_base + (n_slots - 1 - i)
                            S.dma(T["yF"][:], yF_s[ot, :, :])
                            S.dma(bonF[:], bonF_s[ot, :, :])
                            TR(pB[:, 0:128], o["sgs"][:], idf[:])
                            CP('act', sgT[:], pB[:, 0:128])
                            MM(pC[:], sgT[:], gup[:])
                            CP('act', T["gte"][:], pC[:])
                            MM(pC[:], Jf[:], T["yF"][:])
                            MM(pB[:, 0:8], Jf[:], bonF[:])
                            y = T["yF"]
                            y3 = h8(y[:])
                            TT('dve', y[:], pC[:], T["ydir"][:], ALU.add)
                            RED('dve', s8[2][:], y3)
                            TS('dve', s8[2][:], s8[2][:], 1.0 / 64, None, ALU.mult)
                            TT('dve', y3, y3, bc(s8[2][:].unsqueeze(2), [128, 8, 64]), ALU.subtract)
                            TT('pool', T["f1"][:], y[:], y[:], ALU.mult)
                            RED('dve', s8[3][:], h8(T["f1"][:]))
                            rsqrt_small(s8[3][:], s8[3][:], 1.0 / 64, 64e-5)
                            TT('dve', y3, y3, bc(s8[3][:].unsqueeze(2), [128, 8, 64]), ALU.mult)
                            TT('pool', y[:], y[:], bt["ln_x_g"][:], ALU.mult)
                            TT('pool', y[:], y[:], bt["ln_x_b"][:], ALU.add)
                            TT('dve', s8[4][:], pB[:, 0:8], o["bon"][:], ALU.add)
                            TT('dve', h8(T["f1"][:]), h8(o["vfin"][:]), bc(s8[4][:].unsqueeze(2), [128, 8, 64]), ALU.mult)
                            TT('pool', y[:], y[:], T["f1"][:], ALU.add)
                            TT('dve', orw[:], y[:], T["gte"][:], ALU.mult)
                            for p in range(4):
                                MM(pD[:, p * 128:(p + 1) * 128], orw[:, p * 128:(p + 1) * 128], Jb[:])
                            CP('act', orwT[:], pD[:].rearrange("p (a b) -> p a b", a=4))
                            S.dma(mixT_s[4:8, :, ot * 128:(ot + 1) * 128].rearrange("c p t -> p c t"), orwT[:])

                stage12a(0)
                stage12b(0)
                if n_slots > 1:
                    stage12a(1)
                for i in range(n_slots):
                    S.rec_begin()
                    if i + 1 < n_slots:
                        stage12b(i + 1)
                    if i + 2 < n_slots:
                        stage12a(i + 2)
                    A_ = S.rec_end()
                    S.rec_begin()
                    stage34(i)
                    B_ = S.rec_end()
                    S.replay(S.merge(A_, B_) if not _os.environ.get("NOMERGE") else (B_ + A_))

            def set_state(src2):
                if src2 is None:
                    MSET('pool', Hf[:], 0.0)
                else:
                    CP('pool', Hf2, src2)
                CP('act', Hb2, Hf2)

            def switch_step(g):
                STT('dve', Hsave[:], Hf2, flg[:, g:g + 1], Hsave[:], ALU.mult, ALU.add)
                TS('dve', Hf2, Hf2, nflg[:, g:g + 1], None, ALU.mult)
                CP('act', Hb2, Hf2)

            MSET('pool', Hb[:], 0.0)
            MSET('pool', HfW[:], 0.0)
            MSET('pool', Hsave[:], 0.0)
            set_state(None)
            for g in range(NG):
                switch_step(g)
                load_group_params(g)
                rwkv_pass(0, xctx[g], OWN, 0, 0)
            switch_step(NG)
            CP('pool', Hbk[:], Hf2)
            set_state(Hsave[:])
            load_group_params(NG)
            rwkv_pass(0, xown[0], OWN, OWN, 0)
            set_state(None)
            rwkv_pass(0, xsmp[0], NST, NST, OWN)
            set_state(Hbk[:])
            load_group_params(NG + 1)
            rwkv_pass(1, xown[1], OWN, OWN, 0)
            set_state(None)
            rwkv_pass(1, xsmp[1], NST, NST, OWN)

        S.barrier()
        with ExitStack() as e2:
            wda = sbuf(e2, "wda", [128, 8, DA_COLS], BF16)
            wstage[0], wstage[1] = [sbuf(e2, "wstg2_%d" % i, [128, WS]) for i in range(2)]
            load_weight_bf16(wda, (0, DA_COLS), g1T)
            gq = sbuf(e2, "gq", [128, 64])
            gk = sbuf(e2, "gk", [128, 64])
            S.dma(gq[:], vec["q_norm_g"].partition_broadcast(128))
            S.dma(gk[:], vec["k_norm_g"].partition_broadcast(128))
            gsub = sbuf(e2, "gsub", [128, 128])
            S.dma(gsub[:], vec["subln_g"].partition_broadcast(128))
            TS('pool', gsub[:], gsub[:], 1.0 - LAMBDA_INIT, None, ALU.mult)
            lv = [sbuf(e2, "lv%d" % i, [128, 64]) for i in range(4)]
            for i, nm in enumerate(["lam_q1", "lam_k1", "lam_q2", "lam_k2"]):
                S.dma(lv[i][:], vec[nm].partition_broadcast(128))
            l2 = sbuf(e2, "l2", [128, 2])
            neglam = sbuf(e2, "neglam", [128, 1])
            TT('dve', lv[0][:], lv[0][:], lv[1][:], ALU.mult)
            TT('dve', lv[2][:], lv[2][:], lv[3][:], ALU.mult)
            RED('dve', l2[:, 0:1], lv[0][:])
            RED('dve', l2[:, 1:2], lv[2][:])
            ACT(l2[:], l2[:], AF.Exp)
            TT('dve', neglam[:], l2[:, 1:2], l2[:, 0:1], ALU.subtract)
            TS('dve', neglam[:], neglam[:], -LAMBDA_INIT, None, ALU.add)
            cs = sbuf(e2, "cs", [128, 64])
            zq = sbuf(e2, "zq", [128, 512])
            zk = sbuf(e2, "zk", [128, 512])
            zv = sbuf(e2, "zv", [128, 512])
            qn = sbuf(e2, "qn", [128, 512])
            u1 = sbuf(e2, "u1", [128, 256])
            u2 = sbuf(e2, "u2", [128, 256])
            rb = sbuf(e2, "rbf", [128, 512], BF16)
            s8q = sbuf(e2, "s8q", [128, 8])
            kTst = sbuf(e2, "kTst", [128, 4, 128], BF16)
            Vst = sbuf(e2, "Vst", [128, 4, 129], BF16)
            MSET('pool', Vst[:], 1.0)
            qT_P = sbuf(e2, "qT_P", [128, 4, OWN * 128], BF16)
            qT_S = sbuf(e2, "qT_S", [128, 4, NST * 128], BF16)

            def norm_rope(zsrc, g64, cstile, out_bf):
                z3 = zsrc[:].rearrange("p (h n) -> p h n", h=8)
                TT('pool', qn[:], zsrc[:], zsrc[:], ALU.mult)
                RED('dve', s8q[:], qn[:].rearrange("p (h n) -> p h n", h=8))
                rsqrt_small(s8q[:], s8q[:], 1.0 / 64, 1e-6)
                TT('dve', qn[:].rearrange("p (h n) -> p h n", h=8), z3, bc(s8q[:].unsqueeze(2), [128, 8, 64]), ALU.mult)
                TT('pool', qn[:].rearrange("p (h n) -> p h n", h=8), qn[:].rearrange("p (h n) -> p h n", h=8),
                   bc(g64[:].unsqueeze(1), [128, 8, 64]), ALU.mult)
                q4 = qn[:].rearrange("p (h c n) -> p h c n", h=8, c=2)
                o4 = out_bf[:].rearrange("p (h c n) -> p h c n", h=8, c=2)
                x1, x2 = q4[:, :, 0, :], q4[:, :, 1, :]
                cosb = bc(cstile[:, 0:32].unsqueeze(1), [128, 8, 32])
                sinb = bc(cstile[:, 32:64].unsqueeze(1), [128, 8, 32])
                a1 = u1[:].rearrange("p (h n) -> p h n", h=8)
                a2 = u2[:].rearrange("p (h n) -> p h n", h=8)
                TT('dve', a1, x1, cosb, ALU.mult)
                TT('pool', a2, x2, sinb, ALU.mult)
                TT('dve', o4[:, :, 0, :], a1, a2, ALU.subtract)
                TT('pool', a1, x2, cosb, ALU.mult)
                TT('dve', a2, x1, sinb, ALU.mult)
                TT('pool', o4[:, :, 1, :], a1, a2, ALU.add)

            ZK = [zk, sbuf(e2, "zk1", [128, 512])]
            ZV = [zv, sbuf(e2, "zv1", [128, 512])]
            CS = [cs, sbuf(e2, "cs1", [128, 64])]

            def kv_pass(xsrc, n_tiles, cs_src, kT_s, V_s):
                def stage_a(t):
                    S.dma(CS[t % 2][:], cs_src[t, :, :])

                    def evac(gi, pz, rs):
                        ACT([ZK, ZV][gi][t % 2][:], pz, AF.Copy, scale=rs[:, 0:1])
                    project_tile(xsrc[t, :, :], wda, [(512, 1024), (1024, 1536)], evac)

                def stage_b(t):
                    norm_rope(ZK[t % 2], gk, CS[t % 2], rb)
                    for p in range(4):
                        TR(K(pT[:, p * 128:(p + 1) * 128], "pT0"), rb[:, p * 128:(p + 1) * 128], idb[:])
                    CP('act', kTst[:], K(pT[:, 0:512].rearrange("p (a b) -> p a b", a=4), "pT0"))
                    S.dma(kT_s[:, :, t * 128:(t + 1) * 128].rearrange("h p t -> p h t"), kTst[:])
                    CP('act', Vst[:, :, 0:128], ZV[t % 2][:].rearrange("p (h n) -> p h n", h=4))
                    S.dma(V_s[:, :, t, :].rearrange("h p n -> p h n"), Vst[:])
                stage_a(0)
                for t in range(n_tiles):
                    S.rec_begin()
                    if t + 1 < n_tiles:
                        stage_a(t + 1)
                    A_ = S.rec_end()
                    S.rec_begin()
                    stage_b(t)
                    B_ = S.rec_end()
                    S.replay(S.merge(A_, B_))

            def q_pass(xsrc, tile0, n_tiles, cs_src, qT):
                for t in range(n_tiles):
                    S.dma(cs[:], cs_src[t, :, :])

                    def evac(gi, pz, rs):
                        ACT(zq[:], pz, AF.Copy, scale=rs[:, 0:1])
                    project_tile(xsrc[tile0 + t, :, :], wda, [(0, 512)], evac)
                    norm_rope(zq, gq, cs, rb)
                    for p in range(4):
                        TR(K(pT[:, p * 128:(p + 1) * 128], "pT0"), rb[:, p * 128:(p + 1) * 128], idb[:])
                    CP('act', qT[:, :, t * 128:(t + 1) * 128], K(pT[:, 0:512].rearrange("p (a b) -> p a b", a=4), "pT0"))

            kv_pass(xP, NPT, csP, kT_sP, V_sP)
            kv_pass(xsmp[0, 1:NST + 1], NST, csP, kT_sS, V_sS)
            q_pass(xown[0], 1, OWN, csQ, qT_P)
            q_pass(xsmp[0], 1, NST, csP, qT_S)

            PTb = [sbuf(e2, "PTb%d" % i, [128, 512], BF16) for i in range(2)]
            o0 = sbuf(e2, "o0", [128, 128])
            o1 = sbuf(e2, "o1", [128, 128])
            rc = sbuf(e2, "rc", [128, 2])
            ssd = sbuf(e2, "ssd", [128, 1])
            ob = sbuf(e2, "ob", [128, 128], BF16)
            oT = sbuf(e2, "oT", [128, 128], BF16)
            pO = [[pA, pB], [pC, pD]]
            pS = [pL0, pL1]

            osv = [sbuf(e2, "osv%d" % j, [128, 128]) for j in range(4)]
            pOb = [pA, pB, pC, pD]

            def attention(qT, n_q_tiles, kT_s, V_s, n_kt, tok_base):
                kTh = sbuf(e2a, "kTh_%d" % tok_base, [128, n_kt * 128], BF16)
                Vh = sbuf(e2a, "Vh_%d" % tok_base, [128, n_kt, 129], BF16)
                cnt = 0
                for h in range(4):
                    S.dma(kTh[:], kT_s[h, :, :])
                    S.dma(Vh[:], V_s[h, :, :, :])
                    for qg in range(0, n_q_tiles, 4):
                        nq = min(4, n_q_tiles - qg)
                        for c in range(2):
                            rows = slice(c * 64, c * 64 + 64)
                            def qk(kt_, slot):
                                MM(pS[slot % 2][:, 0:nq * 128], kTh[rows, kt_ * 128:(kt_ + 1) * 128], qT[rows, h, qg * 128:(qg + nq) * 128])
                            qk(0, cnt)
                            for kt in range(n_kt):
                                ps_ = pS[cnt % 2]
                                pt_ = PTb[cnt % 2]
                                if kt + 1 < n_kt:
                                    qk(kt + 1, cnt + 1)
                                cnt += 1
                                ACT(pt_[:, 0:nq * 128], ps_[:, 0:nq * 128], AF.Exp, scale=0.125)
                                for j in range(nq):
                                    MM(pOb[j][:, 0:129], pt_[:, j * 128:(j + 1) * 128], Vh[:, kt, :], start=(kt == 0), stop=(kt == n_kt - 1))
                            for j in range(nq):
                                RECIP(rc[:, c:c + 1], pOb[j][:, 128:129])
                                if c == 0:
                                    ACT(osv[j][:], pOb[j][:, 0:128], AF.Copy, scale=rc[:, 0:1])
                                else:
                                    TT('dve', rc[:, 1:2], rc[:, 1:2], neglam[:], ALU.mult)
                                    STT('dve', o1[:], pOb[j][:, 0:128], rc[:, 1:2], osv[j][:], ALU.mult, ALU.add)
                                    ACT(o0[:], o1[:], AF.Square, accum=ssd[:])
                                    rsqrt_small(ssd[:], ssd[:], 1.0 / 128, 1e-6)
                                    STT('dve', ob[:], o1[:], ssd[:, 0:1], gsub[:], ALU.mult, ALU.mult)
                                    TR(pT[:, 0:128], ob[:], idb[:])
                                    CP('act', oT[:], pT[:, 0:128])
                                    tk = tok_base + (qg + j) * 128
                                    S.dma(mixT_s[h, :, tk:tk + 128], oT[:])

            S.barrier()
            with ExitStack() as e2a:
                attention(qT_P, OWN, kT_sP, V_sP, NPT, 0)
            S.barrier()
            with ExitStack() as e2a:
                attention(qT_S, NST, kT_sS, V_sS, NST, OWN * 128)

        S.barrier()
        with ExitStack() as e3:
            wo = sbuf(e3, "wo", [128, 8, D], BF16)
            w2 = sbuf(e3, "w2", [128, 32, D], BF16)
            w1blk = [sbuf(e3, "w1blk%d" % i, [128, 8, 512], BF16) for i in range(2)]
            wcast = [sbuf(e3, "wcast%d" % i, [128, WS], BF16) for i in range(2)]
            wstage[0], wstage[1] = [sbuf(e3, "wstg3_%d" % i, [128, WS]) for i in range(2)]
            wc = 0
            for kc in range(8):
                for s0 in range(0, D, WS):
                    ws = wstage[xcount[0] % 2]
                    xcount[0] += 1
                    S.dma(ws[:, 0:WS], w_out[kc * 128:(kc + 1) * 128, s0:s0 + WS])
                    CP('pool', wo[:, kc, s0:s0 + WS], ws[:, 0:WS])
            for fc in range(32):
                for s0 in range(0, D, WS):
                    ws = wstage[xcount[0] % 2]
                    xcount[0] += 1
                    S.dma(ws[:, 0:WS], w_ff2[fc * 128:(fc + 1) * 128, s0:s0 + WS])
                    CP('pool', w2[:, fc, s0:s0 + WS], ws[:, 0:WS])
            for kc in range(8):
                for s0 in range(0, DFF, WS):
                    ws = wstage[xcount[0] % 2]
                    xcount[0] += 1
                    wcb = wcast[wc % 2]
                    wc += 1
                    S.dma(ws[:, 0:WS], w_ff1[kc * 128:(kc + 1) * 128, s0:s0 + WS])
                    TS('pool', wcb[:], ws[:, 0:WS], g2T[:, kc:kc + 1], None, ALU.mult)
                    S.dma(w1b_s[kc, :, s0:s0 + WS], wcb[:])
            GT = 4
            mixg = sbuf(e3, "mixg", [128, 8, GT * 128], BF16)
            xmid = [sbuf(e3, "xmid%d" % j, [128, D]) for j in range(GT)]
            rs2 = sbuf(e3, "rs2", [128, GT])
            xmb = sbuf(e3, "xmb", [128, D], BF16)
            xmT = sbuf(e3, "xmT", [128, 8, GT * 128], BF16)
            uT = sbuf(e3, "uT", [128, 32, GT * 128], BF16)
            rl = [sbuf(e3, "rl%d" % i, [128, GT * 128]) for i in range(2)]
            yo = sbuf(e3, "yo", [128, D])
            own_src = [(xown[0], 1 + t, out_p, t) for t in range(OWN)] + [(xsmp[0], 1 + t, out_s, t) for t in range(NST)]
            wbc = 0
            for g0 in range(0, NOWN, GT):
                ng = min(GT, NOWN - g0)
                nt = ng * 128
                S.dma(mixg[:, :, 0:nt], mixT_s[:, :, g0 * 128:g0 * 128 + nt].rearrange("c p t -> p c t"))
                for j in range(ng):
                    src, sidx, _, _ = own_src[g0 + j]
                    xr = xt[xcount[0] % 2]
                    xcount[0] += 1
                    S.dma(xr[:], src[sidx, :, :])
                    for hf in range(2):
                        pz = [pZ, pL0][hf]
                        for ch in range(8):
                            MM(pz[:], mixg[:, ch, j * 128:(j + 1) * 128], wo[:, ch, hf * 512:(hf + 1) * 512], start=(ch == 0), stop=(ch == 7))
                        TT('dve', xmid[j][:, hf * 512:(hf + 1) * 512], pz[:], xr[:, hf * 512:(hf + 1) * 512], ALU.add)
                    ACT(junk[:], xmid[j][:], AF.Square, accum=ssq[:])
                    rsqrt_small(rs2[:, j:j + 1], ssq[:], 1.0 / D, 1e-6)
                    CP('pool', xmb[:], xmid[j][:])
                    for hlf in range(2):
                        for k4 in range(4):
                            kc = hlf * 4 + k4
                            TR(K(pT[:, hlf * 512 + k4 * 128: hlf * 512 + (k4 + 1) * 128], "pT%d" % hlf), xmb[:, kc * 128:(kc + 1) * 128], idb[:])
                        CP('act' if hlf == 0 else 'dve', xmT[:, hlf * 4:(hlf + 1) * 4, j * 128:(j + 1) * 128],
                           K(pT[:, hlf * 512:(hlf + 1) * 512].rearrange("p (a b) -> p a b", a=4), "pT%d" % hlf))
                    TT('dve', rs2[:, j:j + 1], rs2[:, j:j + 1], rs2[:, j:j + 1], ALU.mult)
                for fb in range(8):
                    wb_ = w1blk[wbc % 2]
                    wbc += 1
                    S.dma(wb_[:], w1b_s[:, :, fb * 512:(fb + 1) * 512].rearrange("k p f -> p k f"))
                    for f4 in range(4):
                        fc = fb * 4 + f4
                        pf = [pA, pB, pC, pD][fc % 4]
                        for kc in range(8):
                            MM(pf[:, 0:nt], wb_[:, kc, f4 * 128:(f4 + 1) * 128], xmT[:, kc, 0:nt], start=(kc == 0), stop=(kc == 7))
                        r_ = rl[fc % 2]
                        ACT(r_[:, 0:nt], pf[:, 0:nt], AF.Relu)
                        TT('pool' if fc % 2 else 'dve', uT[:, fc, 0:nt], r_[:, 0:nt], r_[:, 0:nt], ALU.mult)
                for j in range(ng):
                    _, _, dst, didx = own_src[g0 + j]
                    for hf in range(2):
                        pz = [pL1, pZ][hf]
                        for fc in range(32):
                            MM(pz[:], uT[:, fc, j * 128:(j + 1) * 128], w2[:, fc, hf * 512:(hf + 1) * 512], start=(fc == 0), stop=(fc == 31))
                        STT('dve', yo[:, hf * 512:(hf + 1) * 512], pz[:], rs2[:, j:j + 1], xmid[j][:, hf * 512:(hf + 1) * 512], ALU.mult, ALU.add)
                    S.dma(dst[didx, :, :], yo[:])
        S.finish('sp')
        S.finish('pool')
        build_nc.ninst = S.ninst
    return nc


def _rope_tables(n_pos):
    inv_freq = (1.0 / (10000.0 ** (np.arange(0, 64, 2, dtype=np.float32) / np.float32(64)))).astype(np.float32)
    ang = np.arange(n_pos, dtype=np.float32)[:, None] * inv_freq[None, :]
    return np.concatenate([np.cos(ang), np.sin(ang)], axis=-1).astype(np.float32)


def _consts():
    s = np.arange(128)[:, None]
    t = np.arange(128)[None, :]
    maskA = np.zeros((2, 128, 1536), np.float32)
    ltri = np.zeros((2, 128, 256), np.float32)
    for d in range(2):
        strict = (s < t) if d == 0 else (s > t)
        incl = (s <= t) if d == 0 else (s >= t)
        maskA[d] = np.concatenate([strict] * 4 + [incl] * 4 + [strict.T] * 4, axis=1).astype(np.float32)
        ltri[d, :, 0:128] = -CDEC * incl
        ltri[d, :, 128:256] = -CDEC * strict
    return maskA, ltri


_NC_CACHE = {}


def kernel(**inputs):
    f32 = lambda a: np.ascontiguousarray(np.asarray(a, dtype=np.float32))
    x_prompt = f32(inputs["x_prompt"])
    x_sample = f32(inputs["x_sample"])
    SEQ = x_prompt.shape[1]
    DSEQ = x_sample.shape[1]
    NPT, NST = SEQ // 128, DSEQ // 128
    OWN = NPT // NCORES
    key = (NPT, NST)
    if key not in _NC_CACHE:
        _NC_CACHE[key] = build_nc(NPT, NST)
    nc = _NC_CACHE[key]
    xp = x_prompt[0].reshape(NPT, 128, D)
    ztile = np.zeros((1, 128, D), np.float32)
    cs_all = _rope_tables(max(SEQ, DSEQ)).reshape(-1, 128, 64)
    maskA, ltri = _consts()
    shared = {
        "xP": xp, "csP": cs_all[:NPT],
        "w_in": f32(inputs["w_in"][0]), "w_out": f32(inputs["w_out"][0]),
        "w_ff1": f32(inputs["w_ff1"][0]), "w_ff2": f32(inputs["w_ff2"][0]),
        "norm1_g": f32(f32(inputs["norm1_g"][0]).reshape(8, 128).T), "norm2_g": f32(f32(inputs["norm2_g"][0]).reshape(8, 128).T),
        "w0": f32(inputs["w0"][0]), "a0": f32(inputs["a0"][0]),
        "w_up": f32(inputs["w_up"][0]), "a_up": f32(inputs["a_up"][0]), "g_up": f32(inputs["g_up"][0]),
        "ident_d": np.eye(128, dtype=np.float32), "maskA_d": maskA, "ltri_d": ltri,
    }
    for nm in ["q_norm_g", "k_norm_g", "lam_q1", "lam_k1", "lam_q2", "lam_k2", "subln_g", "mu_prev",
               "mu_next", "k_k", "k_a", "r_k", "ln_x_g", "ln_x_b"]:
        shared[nm] = f32(inputs[nm][0]).reshape(1, -1)
    xtok = x_prompt[0]
    SEGT = OWN * 128
    NG = NCORES - 1

    def seg_halo(tok, s, segt):
        n = tok.shape[0]
        out = np.zeros((segt + 256, D), np.float32)
        lo, hi = s * segt - 128, (s + 1) * segt + 128
        a, b = max(lo, 0), min(hi, n)
        out[a - lo:b - lo] = tok[a:b]
        return out

    mu_p = f32(inputs["mu_prev"][0]); mu_n = f32(inputs["mu_next"][0])
    dir_par = []
    for d in range(2):
        dir_par.append(dict(mu=np.stack([mu_p, mu_n] if d == 0 else [mu_n, mu_p]),
                            w0=f32(inputs["w0"][0, d]), a0=f32(inputs["a0"][0, d]),
                            wup=f32(inputs["w_up"][0, d]), aup=f32(inputs["a_up"][0, d])))
    Jm = np.ascontiguousarray(np.eye(128, dtype=np.float32)[::-1])
    in_maps = []
    for c in range(NCORES):
        m = dict(shared)
        groups, dirs = [], []
        for g in range(NG):
            if g < c:
                groups.append(seg_halo(xtok, g, SEGT)); dirs.append(0)
            else:
                groups.append(seg_halo(xtok, NG + c - g, SEGT)[::-1]); dirs.append(1)
        m["xctx"] = np.ascontiguousarray(np.stack(groups)).reshape(NG, OWN + 2, 128, D)
        own = seg_halo(xtok, c, SEGT)
        m["xown"] = np.ascontiguousarray(np.stack([own, own[::-1]])).reshape(2, OWN + 2, 128, D)
        smp = seg_halo(x_sample[c], 0, DSEQ)
        m["xsmp"] = np.ascontiguousarray(np.stack([smp, smp[::-1]])).reshape(2, NST + 2, 128, D)
        dirs = dirs + [0, 1]
        m["grp_mu"] = np.ascontiguousarray(np.stack([dir_par[d]["mu"] for d in dirs]))
        m["grp_w0"] = np.ascontiguousarray(np.stack([dir_par[d]["w0"] for d in dirs]))
        m["grp_a0"] = np.ascontiguousarray(np.stack([dir_par[d]["a0"] for d in dirs]))
        m["grp_wup"] = np.ascontiguousarray(np.stack([dir_par[d]["wup"] for d in dirs]))
        m["grp_aup"] = np.ascontiguousarray(np.stack([dir_par[d]["aup"] for d in dirs]))
        fl = np.zeros((1, 8), np.float32); fl[0, c] = 1.0
        m["flags_d"] = fl
        m["J_d"] = Jm
        m["csQ"] = np.ascontiguousarray(cs_all[OWN * c:OWN * (c + 1)])
        in_maps.append(m)
    if inputs.get("_maps_only"):
        return nc, in_maps
    res = run_bass_kernel_spmd(nc, in_maps, core_ids=list(range(NCORES)))
    y_prompt = np.concatenate([res.results[c]["out_p"].reshape(OWN * 128, D) for c in range(NCORES)], axis=0)[None]
    y_sample = np.stack([res.results[c]["out_s"].reshape(NST * 128, D) for c in range(NCORES)], axis=0)
    return (y_prompt.astype(np.float32), y_sample.astype(np.float32))
```

```python
import math
import numpy as np
from contextlib import ExitStack
import concourse.bass as bass
import concourse.mybir as mybir
from concourse.bass_utils import run_bass_kernel_spmd

F32 = mybir.dt.float32
BF16 = mybir.dt.bfloat16
AF = mybir.ActivationFunctionType
ALU = mybir.AluOpType
AX = mybir.AxisListType

D = 1024
NCORES = 8
DA_COLS = 1536
RW_COLS = 1792
IN_COLS = 3328
DFF = 4096
CDEC = math.exp(-0.5)
LAMBDA_INIT = 0.8 - 0.6 * math.exp(0.0)


class Sched:
    def __init__(self, nc, es, n_dma_sems=24):
        self.nc = nc
        self.E = {'pe': nc.tensor, 'act': nc.scalar, 'dve': nc.vector, 'pool': nc.gpsimd, 'sp': nc.sync}
        self.sem = {k: es.enter_context(nc.semaphore("s_" + k)) for k in ['pe', 'act', 'dve', 'pool']}
        self.cnt = {k: 0 for k in self.sem}
        self.seen = {k: {} for k in self.E}
        self.lastw = {}
        self.readers = {}
        self.dsems = [es.enter_context(nc.semaphore("d%d" % i)) for i in range(n_dma_sems)]
        self.dcnt = [0] * n_dma_sems
        self.dnext = 0
        self.ninst = 0
        self.psum_names = set()
        self._rec = None

    def _key(self, a):
        if isinstance(a, tuple):
            if a[0].name in self.psum_names:
                return a[0].name
            return a[1]
        return a.name

    def _ap(self, a):
        return a[0] if isinstance(a, tuple) else a

    def _wait(self, eng, tok):
        sem, val, name = tok
        if self.seen[eng].get(name, 0) >= val:
            return
        self.E[eng].wait_ge(sem, val)
        self.seen[eng][name] = val

    def _deps(self, eng, ins, outs):
        toks = []
        for a in ins:
            k = self._key(a)
            if k in self.lastw:
                toks.append(self.lastw[k])
            if k in self.psum_names:
                toks.extend(t for e, t in self.readers.get(k, {}).items() if e != eng)
        for a in outs:
            k = self._key(a)
            if k in self.lastw:
                toks.append(self.lastw[k])
            toks.extend(self.readers.get(k, {}).values())
        for t in toks:
            if eng == 'pe' and t[2] == 'pe':
                continue
            self._wait(eng, t)

    def _commit(self, tok, ins, outs):
        for a in outs:
            k = self._key(a)
            self.lastw[k] = tok
            self.readers[k] = {}
        for a in ins:
            k = self._key(a)
            self.readers.setdefault(k, {})[tok[2]] = tok

    def rec_begin(self):
        self._rec = []

    def rec_end(self):
        r, self._rec = self._rec, None
        return r

    def replay(self, items):
        for it in items:
            if it[0] == 'op':
                self.op(*it[1:])
            else:
                self.dma(it[1], it[2], q=it[3])

    @staticmethod
    def merge(a, b):
        out, ia, ib = [], 0, 0
        while ia < len(a) or ib < len(b):
            if ib >= len(b) or (ia < len(a) and ia * len(b) <= ib * len(a)):
                out.append(a[ia]); ia += 1
            else:
                out.append(b[ib]); ib += 1
        return out

    def op(self, eng, fn, outs, ins):
        if _STOPPED[0]:
            return None
        if self._rec is not None:
            self._rec.append(('op', eng, fn, outs, ins))
            return None
        self._deps(eng, ins, outs)
        inst = fn()
        self.cnt[eng] += 1
        self.ninst += 1
        inst.then_inc(self.sem[eng], 1)
        tok = (self.sem[eng], self.cnt[eng], eng)
        self._commit(tok, ins, outs)
        return tok

    def dma(self, out, in_, q='sp'):
        if _STOPPED[0]:
            return None
        if self._rec is not None:
            self._rec.append(('dma', out, in_, q))
            return None
        i = self.dnext
        self.dnext = (self.dnext + 1) % len(self.dsems)
        name = 'dma%d' % i
        if self.dcnt[i] > 0:
            self._wait(q, (self.dsems[i], 16 * self.dcnt[i], name))
        self._deps(q, [in_], [out])
        self.E[q].dma_start(out=self._ap(out), in_=self._ap(in_)).then_inc(self.dsems[i], 16)
        self.dcnt[i] += 1
        self.ninst += 1
        tok = (self.dsems[i], 16 * self.dcnt[i], name)
        self._commit(tok, [in_], [out])
        return tok

    def barrier(self):
        if _STOPPED[0]:
            return
        for e in self.E:
            for k in self.sem:
                if self.cnt[k] > 0:
                    self._wait(e, (self.sem[k], self.cnt[k], k))
            for i in range(len(self.dsems)):
                if self.dcnt[i] > 0:
                    self._wait(e, (self.dsems[i], 16 * self.dcnt[i], 'dma%d' % i))
        self.lastw = {}
        self.readers = {}

    def finish(self, q='sp'):
        for i in range(len(self.dsems)):
            if self.dcnt[i] > 0:
                self._wait(q, (self.dsems[i], 16 * self.dcnt[i], 'dma%d' % i))


class _Stop(Exception):
    pass


import os as _os
_KSTOP = _os.environ.get('KSTOP', '')


_STOPPED = [False]


def _stop(tag):
    if _KSTOP == tag:
        _STOPPED[0] = True


def K(ap, key):
    return (ap, key)


def build_nc(NPT, NST, inv_dt=BF16):
    OWN = NPT // NCORES
    NOWN = OWN + NST
    NTOK = NOWN * 128
    nc = bass.Bass("TRN2", target_bir_lowering=False)
    dram = lambda name, shape, dt=F32, kind="ExternalInput": nc.dram_tensor(name, shape, dt, kind=kind).ap()
    NG = NCORES - 1
    xctx = dram("xctx", [NG, OWN + 2, 128, D])
    xown = dram("xown", [2, OWN + 2, 128, D])
    xP = dram("xP", [NPT, 128, D])
    xsmp = dram("xsmp", [2, NST + 2, 128, D])
    grp_mu = dram("grp_mu", [NG + 2, 2, RW_COLS])
    grp_w0 = dram("grp_w0", [NG + 2, 512])
    grp_a0 = dram("grp_a0", [NG + 2, 512])
    grp_wup = dram("grp_wup", [NG + 2, 64, 512])
    grp_aup = dram("grp_aup", [NG + 2, 64, 512])
    flags_d = dram("flags_d", [1, 8])
    J_d = dram("J_d", [128, 128])
    csP = dram("csP", [NPT, 128, 64])
    csQ = dram("csQ", [OWN, 128, 64])
    w_in = dram("w_in", [D, IN_COLS])
    w_out = dram("w_out", [D, D])
    w_ff1 = dram("w_ff1", [D, DFF])
    w_ff2 = dram("w_ff2", [DFF, D])
    norm1_g = dram("norm1_g", [128, 8])
    norm2_g = dram("norm2_g", [128, 8])
    vec = {}
    for nm, n in [("q_norm_g", 64), ("k_norm_g", 64), ("lam_q1", 64), ("lam_k1", 64), ("lam_q2", 64),
                  ("lam_k2", 64), ("subln_g", 128), ("mu_prev", RW_COLS), ("mu_next", RW_COLS),
                  ("k_k", 512), ("k_a", 512), ("r_k", 512), ("ln_x_g", 512), ("ln_x_b", 512)]:
        vec[nm] = dram(nm, [1, n])
    w0 = dram("w0", [2, 512])
    a0 = dram("a0", [2, 512])
    w_up = dram("w_up", [2, 64, 512])
    a_up = dram("a_up", [2, 64, 512])
    g_up = dram("g_up", [128, 512])
    ident_d = dram("ident_d", [128, 128])
    maskA_d = dram("maskA_d", [2, 128, 1536])
    ltri_d = dram("ltri_d", [2, 128, 256])
    out_p = dram("out_p", [OWN, 128, D], kind="ExternalOutput")
    out_s = dram("out_s", [NST, 128, D], kind="ExternalOutput")
    scr = lambda name, shape, dt=F32: nc.dram_tensor(name, shape, dt).ap()
    yF_s = scr("yF_s", [NOWN, 128, 512])
    bonF_s = scr("bonF_s", [NOWN, 128, 8])
    mixT_s = scr("mixT_s", [8, 128, NTOK], BF16)
    kT_sP = scr("kT_sP", [4, 128, NPT * 128], BF16)
    kT_sS = scr("kT_sS", [4, 128, NST * 128], BF16)
    V_sP = scr("V_sP", [4, 128, NPT, 129], BF16)
    V_sS = scr("V_sS", [4, 128, NST, 129], BF16)
    w1b_s = scr("w1b_s", [8, 128, DFF], BF16)

    with ExitStack() as es:
        S = Sched(nc, es)
        V, G, A, P = nc.vector, nc.gpsimd, nc.scalar, nc.tensor

        def sbuf(stack, name, shape, dt=F32):
            return stack.enter_context(nc.sbuf_tensor(name, shape, dt))

        def psum(stack, name, shape, dt=F32):
            S.psum_names.add(name)
            return stack.enter_context(nc.psum_tensor(name, shape, dt))

        def MM(out, lhsT, rhs, start=True, stop=True):
            S.op('pe', lambda: P.matmul(S._ap(out), S._ap(lhsT), S._ap(rhs), start=start, stop=stop), [out], [lhsT, rhs])

        def TR(out, in_, idt):
            S.op('pe', lambda: P.transpose(S._ap(out), S._ap(in_), S._ap(idt)), [out], [in_, idt])

        def ACT(out, in_, func, bias=None, scale=None, accum=None, extra_out=()):
            kw = {}
            ins = [in_]
            if bias is not None:
                kw['bias'] = S._ap(bias) if not isinstance(bias, float) else bias
                if not isinstance(bias, float):
                    ins.append(bias)
            if scale is not None:
                kw['scale'] = S._ap(scale) if not isinstance(scale, float) else scale
                if not isinstance(scale, float):
                    ins.append(scale)
            outs = [out] + list(extra_out)
            if accum is not None:
                kw['accum_out'] = S._ap(accum)
                outs.append(accum)
            S.op('act', lambda: A.activation(S._ap(out), S._ap(in_), func, **kw), outs, ins)

        def EW(eng):
            return {'dve': V, 'pool': G}[eng]

        def TT(eng, out, a, b, op):
            S.op(eng, lambda: EW(eng).tensor_tensor(S._ap(out), S._ap(a), S._ap(b), op), [out], [a, b])

        def TS(eng, out, a, s1, s2, op0, op1=None):
            ins = [a] + [s for s in (s1, s2) if s is not None and not isinstance(s, float)]
            g = lambda s: s if (s is None or isinstance(s, float)) else S._ap(s)
            if op1 is None:
                S.op(eng, lambda: EW(eng).tensor_scalar(S._ap(out), S._ap(a), g(s1), None, op0), [out], ins)
            else:
                S.op(eng, lambda: EW(eng).tensor_scalar(S._ap(out), S._ap(a), g(s1), g(s2), op0, op1), [out], ins)

        def STT(eng, out, a, s, b, op0, op1):
            ins = [a, b] + ([] if isinstance(s, float) else [s])
            sv = s if isinstance(s, float) else S._ap(s)
            S.op(eng, lambda: EW(eng).scalar_tensor_tensor(S._ap(out), S._ap(a), sv, S._ap(b), op0, op1), [out], ins)

        def CP(eng, out, in_):
            if eng == 'act':
                S.op('act', lambda: A.copy(S._ap(out), S._ap(in_)), [out], [in_])
            else:
                S.op(eng, lambda: EW(eng).tensor_copy(S._ap(out), S._ap(in_)), [out], [in_])

        def RED(eng, out, in_, op=ALU.add):
            S.op(eng, lambda: EW(eng).tensor_reduce(S._ap(out), S._ap(in_), AX.X, op), [out], [in_])

        def MSET(eng, out, val):
            S.op(eng, lambda: EW(eng).memset(S._ap(out), val), [out], [])

        def RECIP(out, in_):
            S.op('dve', lambda: V.reciprocal(S._ap(out), S._ap(in_)), [out], [in_])

        def rsqrt_small(out, in_, mul, add):
            TS('dve', out, in_, float(mul), float(add), ALU.mult, ALU.add)
            S.op('act', lambda: A.sqrt(S._ap(out), S._ap(out)), [out], [out])
            RECIP(out, out)

        def bc(ap, shape):
            return ap.to_broadcast(shape)

        idf = sbuf(es, "idf", [128, 128])
        idb = sbuf(es, "idb", [128, 128], BF16)
        g1T = sbuf(es, "g1T", [128, 8])
        g2T = sbuf(es, "g2T", [128, 8])
        S.dma(idf[:], ident_d[:, :])
        CP('dve', idb[:], idf[:])
        S.dma(g1T[:], norm1_g[:, :])
        S.dma(g2T[:], norm2_g[:, :])

        pT = psum(es, "pT", [128, 1024], BF16)
        pZ = psum(es, "pZ", [128, 512])
        pL0 = psum(es, "pL0", [128, 512])
        pL1 = psum(es, "pL1", [128, 512])
        pA = psum(es, "pA", [128, 512])
        pB = psum(es, "pB", [128, 512])
        pC = psum(es, "pC", [128, 512])
        pD = psum(es, "pD", [128, 512])

        xt = [sbuf(es, "xt%d" % i, [128, D]) for i in range(2)]
        xb = sbuf(es, "xb", [128, D], BF16)
        junk = xb
        xT = sbuf(es, "xT", [128, 8, 128], BF16)
        ssq = sbuf(es, "ssq", [128, 1])
        rstd = sbuf(es, "rstd", [128, 1])
        WS = 512
        wstage = [None, None]
        xcount = [0]

        def load_weight_bf16(dst, src_cols, gT):
            c0, c1 = src_cols
            n = c1 - c0
            for kc in range(8):
                for s0 in range(0, n, WS):
                    s1 = min(n, s0 + WS)
                    ws = wstage[xcount[0] % 2]
                    xcount[0] += 1
                    S.dma(ws[:, 0:s1 - s0], w_in[kc * 128:(kc + 1) * 128, c0 + s0:c0 + s1])
                    TS('pool', dst[:, kc, s0:s1], ws[:, 0:s1 - s0], gT[:, kc:kc + 1], None, ALU.mult)

        def project_tile(src_ap, w_b, col_groups, evac):
            xtile = xt[xcount[0] % 2]
            xcount[0] += 1
            S.dma(xtile[:], src_ap)
            ACT(junk[:], xtile[:], AF.Square, accum=ssq[:])
            rsqrt_small(rstd[:], ssq[:], 1.0 / D, 1e-6)
            CP('pool', xb[:], xtile[:])
            for hlf in range(2):
                for k4 in range(4):
                    kc = hlf * 4 + k4
                    TR(K(pT[:, hlf * 512 + k4 * 128: hlf * 512 + (k4 + 1) * 128], "pT%d" % hlf), xb[:, kc * 128:(kc + 1) * 128], idb[:])
                eng = 'act' if hlf == 0 else 'dve'
                CP(eng, K(xT[:, hlf * 4:(hlf + 1) * 4, :], "xT%d" % hlf),
                   K(pT[:, hlf * 512:(hlf + 1) * 512].rearrange("p (a b) -> p a b", a=4), "pT%d" % hlf))
            zb_ = [pZ, pL0, pL1]
            for gi, (c0, c1) in enumerate(col_groups):
                n = c1 - c0
                pz_ = zb_[gi % 3]
                for kc in range(8):
                    MM(pz_[:, 0:n], K(xT[:, kc, :], "xT%d" % (kc // 4)), w_b[:, kc, c0:c1], start=(kc == 0), stop=(kc == 7))
                evac(gi, pz_[:, 0:n], rstd)

        with ExitStack() as e1:
            wrw = sbuf(e1, "wrw", [128, 8, RW_COLS], BF16)
            Z = [sbuf(e1, "Zr%d" % i, [128, RW_COLS]) for i in range(3)]
            wstage[0], wstage[1] = Z[1], Z[2]
            load_weight_bf16(wrw, (DA_COLS, IN_COLS), g1T)
            ZP = [sbuf(e1, "zp%d" % j, [128, RW_COLS]) for j in range(2)]
            zn = sbuf(e1, "zn", [128, RW_COLS])
            mu_p = sbuf(e1, "mu_p", [128, RW_COLS])
            mu_n = sbuf(e1, "mu_n", [128, RW_COLS])
            mu_c = sbuf(e1, "mu_c", [128, RW_COLS])
            S.dma(mu_p[:], vec["mu_prev"].partition_broadcast(128))
            S.dma(mu_n[:], vec["mu_next"].partition_broadcast(128))
            TT('pool', mu_c[:], mu_p[:], mu_n[:], ALU.add)
            TS('pool', mu_c[:], mu_c[:], -1.0, 1.0, ALU.mult, ALU.add)
            bt = {}
            for nm in ["k_k", "k_a", "r_k", "ln_x_g", "ln_x_b"]:
                bt[nm] = sbuf(e1, "b_" + nm, [128, 512])
                S.dma(bt[nm][:], vec[nm].partition_broadcast(128))
            w0a0 = sbuf(e1, "w0a0", [128, 512])
            onesf = sbuf(e1, "onesf", [128, 128])
            MSET('pool', onesf[:], 1.0)
            MSET('pool', w0a0[:], 0.0)
            waup = sbuf(e1, "waup", [128, 512])
            gup = sbuf(e1, "gup", [128, 512])
            S.dma(gup[:], g_up[:, :])
            mS4 = sbuf(e1, "mS4", [128, 512])
            mI4 = sbuf(e1, "mI4", [128, 512])
            mB4 = sbuf(e1, "mB4", [128, 512])
            ltri = sbuf(e1, "ltri", [128, 256])
            negc = sbuf(e1, "negc", [128, 1])
            MSET('pool', negc[:], -CDEC)
            T = {}
            for nm in ["sgw", "arate", "Winv", "We", "kk", "kd", "t0", "t1", "ydir", "yF", "gte", "f1"]:
                T[nm] = sbuf(e1, "T_" + nm, [128, 512])
            T["Wt"] = T["kd"]
            LO = [sbuf(e1, "lo%d" % j, [128, 128]) for j in range(2)]
            LOT = [sbuf(e1, "loT%d" % j, [128, 128]) for j in range(2)]
            sgT = sbuf(e1, "sgT", [128, 128])
            s8 = [sbuf(e1, "s8_%d" % i, [128, 8]) for i in range(6)]
            bonF = sbuf(e1, "bonF", [128, 8])
            rtok = sbuf(e1, "rtok", [128, 512], BF16)
            orw = sbuf(e1, "orw", [128, 512], BF16)
            orwT = sbuf(e1, "orwT", [128, 4, 128], BF16)
            BS = []
            for j in range(2):
                o = {}
                o["arT"] = sbuf(e1, "arT%d" % j, [128, 4, 2, 128], BF16)
                o["bT"] = sbuf(e1, "bT%d" % j, [128, 4, 128], BF16)
                o["kTt"] = sbuf(e1, "kTt%d" % j, [128, 4, 128], BF16)
                o["atok"] = sbuf(e1, "atok%d" % j, [128, 512], BF16)
                o["btok"] = sbuf(e1, "btok%d" % j, [128, 512], BF16)
                o["ktok"] = sbuf(e1, "ktok%d" % j, [128, 512], BF16)
                o["vb"] = sbuf(e1, "vb%d" % j, [128, 512], BF16)
                o["WC"] = sbuf(e1, "WC%d" % j, [128, 4])
                o["vfin"] = sbuf(e1, "vfin%d" % j, [128, 512])
                o["sgs"] = sbuf(e1, "sgs%d" % j, [128, 128])
                o["bon"] = sbuf(e1, "bon%d" % j, [128, 8])
                BS.append(o)
            P1T = sbuf(e1, "P1T", [128, 4, 128], BF16)
            U0 = sbuf(e1, "U0", [128, 512])
            ArbT = sbuf(e1, "ArbT", [128, 8, 128], BF16)
            ArkT = sbuf(e1, "ArkT", [128, 8, 128], BF16)
            NN = [sbuf(e1, "NN%d" % j, [128, 8, 128], inv_dt) for j in range(2)]
            NTt = [sbuf(e1, "NTt%d" % j, [128, 8, 128], inv_dt) for j in range(2)]
            RR = [sbuf(e1, "RR%d" % j, [128, 8, 128], inv_dt) for j in range(2)]
            AakT = sbuf(e1, "AakT", [128, 8, 128], BF16)
            X0a = sbuf(e1, "X0a", [128, 512], BF16)
            idi = sbuf(e1, "idi", [128, 128], inv_dt)
            CP('pool', idi[:], idf[:])
            Hf = sbuf(e1, "Hf", [128, 4, 128])
            Hb = sbuf(e1, "Hb", [128, 4, 128], BF16)
            HfW = sbuf(e1, "HfW", [128, 4, 128])
            tmpH = T["f1"][:].rearrange("p (q t) -> p q t", q=4)
            Ub = sbuf(e1, "Ub", [128, 512], BF16)

            def compute_z(src_ap, zbuf):
                groups = [(0, 512), (512, 1024), (1024, 1536), (1536, 1792)]

                def evac(gi, pz, rs):
                    c0, c1 = groups[gi]
                    ACT(zbuf[:, c0:c1], pz, AF.Copy, scale=rs[:, 0:1])
                project_tile(src_ap, wrw, groups, evac)

            def blk(banks, h):
                return banks[h % 2][:, (h // 2) * 128:(h // 2 + 1) * 128]

            def hrows(h):
                return slice((h % 2) * 64, (h % 2) * 64 + 64)
            f2 = lambda t3, b: t3[:].rearrange("p (q b) t -> p q b t", b=2)[:, :, b, :]
            bv = lambda t2: t2[:].rearrange("p (q t) -> p q t", q=4)
            h8 = lambda ap: ap.rearrange("p (h n) -> p h n", h=8)

            Jf = sbuf(e1, "Jf", [128, 128])
            Jb = sbuf(e1, "Jb", [128, 128], BF16)
            S.dma(Jf[:], J_d[:, :])
            CP('dve', Jb[:], Jf[:])
            flg = sbuf(e1, "flg", [128, 8])
            nflg = sbuf(e1, "nflg", [128, 8])
            S.dma(flg[:], flags_d.partition_broadcast(128))
            TS('dve', nflg[:], flg[:], -1.0, 1.0, ALU.mult, ALU.add)
            Hsave = T["yF"]
            Hbk = T["gte"]
            S.dma(mS4[:], maskA_d[0, :, 0:512])
            S.dma(mI4[:], maskA_d[0, :, 512:1024])
            S.dma(mB4[:], maskA_d[0, :, 1024:1536])
            S.dma(ltri[:], ltri_d[0, :, :])
            Hf2 = Hf[:].rearrange("p q t -> p (q t)")
            Hb2 = Hb[:].rearrange("p q t -> p (q t)")

            def load_group_params(gi):
                S.dma(mu_p[:], grp_mu[gi, 0:1, :].partition_broadcast(128))
                S.dma(mu_n[:], grp_mu[gi, 1:2, :].partition_broadcast(128))
                S.dma(w0a0[0:1, :], grp_w0[gi:gi + 1, :])
                S.dma(w0a0[64:65, :], grp_a0[gi:gi + 1, :])
                S.dma(waup[0:64, :], grp_wup[gi, :, :])
                S.dma(waup[64:128, :], grp_aup[gi, :, :])

            def rwkv_pass(d, xsrc, n_slots, n_own, own_base):
                compute_z(xsrc[0, :, :], Z[0])
                compute_z(xsrc[1, :, :], Z[1])

                def stage12a(i):
                    zp, lo, loT = ZP[i % 2], LO[i % 2], LOT[i % 2]
                    compute_z(xsrc[i + 2, :, :], Z[(i + 2) % 3])
                    zc, za, zb = Z[(i + 1) % 3], Z[(i + 2) % 3], Z[i % 3]
                    zprev_src, znext_src = zb, za
                    S.dma(zp[1:128, :], zc[0:127, :])
                    S.dma(zp[0:1, :], zprev_src[127:128, :])
                    S.dma(zn[0:127, :], zc[1:128, :])
                    S.dma(zn[127:128, :], znext_src[0:1, :])
                    TT('pool', zp[:], zp[:], mu_p[:], ALU.mult)
                    TT('dve', zn[:], zn[:], mu_n[:], ALU.mult)
                    TT('dve', zp[:], zp[:], zn[:], ALU.add)
                    TT('pool', zn[:], zc[:], mu_c[:], ALU.mult)
                    TT('dve', zp[:], zp[:], zn[:], ALU.add)
                    zf = zp
                    r_, k_, v_ = zf[:, 0:512], zf[:, 512:1024], zf[:, 1024:1536]
                    _stop('S12a')
                    ACT(lo[:, 0:64], zf[:, 1536:1600], AF.Tanh)
                    CP('pool', lo[:, 64:128], zf[:, 1600:1664])

                def stage12b(i):
                    own = i >= n_slots - n_own
                    o = BS[i % 2]
                    zf, loT = ZP[i % 2], LOT[i % 2]
                    r_, k_, v_ = zf[:, 0:512], zf[:, 512:1024], zf[:, 1024:1536]
                    TR(pL1[:, 0:128], LO[i % 2][:], idf[:])
                    CP('act', loT[:], pL1[:, 0:128])
                    MM(pL0[:], loT[0:64, :], waup[0:64, :], start=True, stop=False)
                    MM(pL0[:], onesf[0:1, :], w0a0[0:1, :], start=False, stop=True)
                    MM(pL1[:], loT[64:128, :], waup[64:128, :], start=True, stop=False)
                    MM(pL1[:], onesf[64:65, :], w0a0[64:65, :], start=False, stop=True)
                    ACT(T["sgw"][:], pL0[:], AF.Sigmoid)
                    ACT(T["arate"][:], pL1[:], AF.Sigmoid)
                    MM(pL0[:], ltri[:, 0:128], T["sgw"][:])
                    MM(pL1[:], ltri[:, 128:256], T["sgw"][:])
                    for p in range(4):
                        MM(pZ[:, p:p + 1], T["sgw"][:, p * 128:(p + 1) * 128], negc[:], start=True, stop=True)
                    ACT(o["WC"][:], pZ[:, 0:4], AF.Exp)
                    if own:
                        ACT(T["Wt"][:], pL0[:], AF.Exp)
                        TT('dve', rtok[:], r_, T["Wt"][:], ALU.mult)
                    ACT(T["Winv"][:], pL0[:], AF.Exp, scale=-1.0)
                    ACT(T["We"][:], pL1[:], AF.Exp)
                    _stop('S12b')
                    TT('pool', T["kk"][:], k_, bt["k_k"][:], ALU.mult)
                    TT('pool', T["t0"][:], T["kk"][:], T["kk"][:], ALU.mult)
                    RED('dve', s8[0][:], h8(T["t0"][:]))
                    rsqrt_small(s8[0][:], s8[0][:], 1.0, 1e-12)
                    TT('dve', h8(T["kk"][:]), h8(T["kk"][:]), bc(s8[0][:].unsqueeze(2), [128, 8, 64]), ALU.mult)
                    STT('dve', T["t0"][:], T["arate"][:], -1.0, bt["k_a"][:], ALU.add, ALU.mult)
                    STT('dve', T["kd"][:], T["t0"][:], 1.0, k_, ALU.add, ALU.mult)
                    STT('dve', o["atok"][:], T["kk"][:], -1.0, T["We"][:], ALU.mult, ALU.mult)
                    TT('pool', T["t1"][:], T["kk"][:], T["arate"][:], ALU.mult)
                    TT('pool', o["btok"][:], T["t1"][:], T["Winv"][:], ALU.mult)
                    TT('pool', o["ktok"][:], T["kd"][:], T["Winv"][:], ALU.mult)
                    CP('act', o["vb"][:], v_)
                    for p in range(4):
                        TR(pT[:, p * 128:(p + 1) * 128], o["atok"][:, p * 128:(p + 1) * 128], idb[:])
                    for p in range(4):
                        TR(pT[:, 512 + p * 128:512 + (p + 1) * 128], o["btok"][:, p * 128:(p + 1) * 128], idb[:])
                    CP('act', o["arT"][:, :, 0, :], pT[:, 0:512].rearrange("p (a b) -> p a b", a=4))
                    CP('dve', o["bT"][:], pT[:, 512:1024].rearrange("p (a b) -> p a b", a=4))
                    for p in range(4):
                        TR(pT[:, p * 128:(p + 1) * 128], o["ktok"][:, p * 128:(p + 1) * 128], idb[:])
                    if own:
                        for p in range(4):
                            TR(pT[:, 512 + p * 128:512 + (p + 1) * 128], rtok[:, p * 128:(p + 1) * 128], idb[:])
                    CP('act', o["kTt"][:], pT[:, 0:512].rearrange("p (a b) -> p a b", a=4))
                    if own:
                        CP('dve', o["arT"][:, :, 1, :], pT[:, 512:1024].rearrange("p (a b) -> p a b", a=4))
                        TT('pool', T["t0"][:], r_, bt["r_k"][:], ALU.mult)
                        TT('pool', T["t0"][:], T["t0"][:], T["kd"][:], ALU.mult)
                        RED('dve', o["bon"][:], h8(T["t0"][:]))
                        if d == 1:
                            CP('pool', o["vfin"][:], v_)
                            ACT(o["sgs"][:], zf[:, 1664:1792], AF.Sigmoid)

                def stage34(i):
                    own = i >= n_slots - n_own
                    o = BS[i % 2]
                    arT, bT, kTt, atok = o["arT"], o["bT"], o["kTt"], o["atok"]
                    for h in range(8):
                        MM(blk((pA, pB), h), bT[hrows(h), h // 2, :], arT[hrows(h), h // 2, 0, :])
                    for h in range(8):
                        MM(blk((pC, pD), h), kTt[hrows(h), h // 2, :], arT[hrows(h), h // 2, 0, :])
                    for b_, bank in enumerate((pA, pB)):
                        TT('dve', f2(NN[0], b_), bv(bank), bv(mS4), ALU.mult)
                    for b_, bank in enumerate((pC, pD)):
                        TT('dve', f2(AakT, b_), bv(bank), bv(mS4), ALU.mult)
                    for h in range(8):
                        MM(blk((pA, pB), h), arT[hrows(h), h // 2, 0, :], bT[hrows(h), h // 2, :])
                    for b_, bank in enumerate((pA, pB)):
                        TT('dve', f2(NTt[0], b_), bv(bank), bv(mB4), ALU.mult)
                    if own:
                        for h in range(8):
                            MM(blk((pC, pD), h), bT[hrows(h), h // 2, :], arT[hrows(h), h // 2, 1, :])
                        for b_, bank in enumerate((pC, pD)):
                            TT('dve', f2(ArbT, b_), bv(bank), bv(mI4), ALU.mult)
                        for h in range(8):
                            MM(blk((pA, pB), h), kTt[hrows(h), h // 2, :], arT[hrows(h), h // 2, 1, :])
                        for b_, bank in enumerate((pA, pB)):
                            TT('dve', f2(ArkT, b_), bv(bank), bv(mI4), ALU.mult)
                    _stop('A')
                    TT('dve', RR[0][:], NN[0][:], bc(idi[:].unsqueeze(1), [128, 8, 128]), ALU.add)
                    for lev in range(1, 7):
                        cur, nxt = (lev - 1) % 2, lev % 2
                        last = lev == 6
                        if not last:
                            for h in range(8):
                                MM(blk((pA, pB), h), NTt[cur][:, h, :], NN[cur][:, h, :])
                        for h in range(8):
                            MM(blk((pC, pD), h), NN[cur][:, h, :], NTt[cur][:, h, :])
                        if not last:
                            CP('act', f2(NN[nxt], 0), bv(pA))
                            CP('act', f2(NN[nxt], 1), bv(pB))
                        CP('act', f2(NTt[nxt], 0), bv(pC))
                        CP('dve', f2(NTt[nxt], 1), bv(pD))
                        for h in range(8):
                            MM(blk((pA, pB), h), NTt[nxt][:, h, :], RR[cur][:, h, :])
                        TT('dve', f2(RR[nxt], 0), bv(pA), f2(RR[cur], 0), ALU.add)
                        TT('dve', f2(RR[nxt], 1), bv(pB), f2(RR[cur], 1), ALU.add)
                    _stop('B')
                    Rf = RR[0]
                    for h in range(8):
                        MM(blk((pC, pD), h), atok[:, (h // 2) * 128:(h // 2 + 1) * 128], Rf[:, h, :])
                    for hh, bank in enumerate((pC, pD)):
                        src = bank[hh * 64:(hh + 1) * 64, :].rearrange("p (q t) -> p q t", q=4)
                        CP('act' if hh == 0 else 'dve', P1T[hh * 64:(hh + 1) * 64, :, :], src)
                    for h in range(8):
                        MM(pA[:, h * 64:(h + 1) * 64], AakT[:, h, :], o["vb"][:, h * 64:(h + 1) * 64])
                    CP('act', X0a[:], pA[:])
                    for h in range(8):
                        MM(pB[:, h * 64:(h + 1) * 64], Rf[:, h, :], X0a[:, h * 64:(h + 1) * 64])
                    CP('act', U0[:], pB[:])
                    _stop('C')
                    for p in range(4):
                        MM(pC[:, p * 128:(p + 1) * 128], P1T[:, p, :], Hb[:, p, :])
                    TT('dve', Ub[:], pC[:], U0[:], ALU.add)
                    if own:
                        for h in range(8):
                            p, hh = h // 2, h % 2
                            cs_ = slice(h * 64, h * 64 + 64)
                            MM(pD[:, cs_], arT[:, p, 1, :], Hb[:, p, hh * 64:(hh + 1) * 64], start=True, stop=False)
                            MM(pD[:, cs_], ArbT[:, h, :], Ub[:, cs_], start=False, stop=False)
                            MM(pD[:, cs_], ArkT[:, h, :], o["vb"][:, cs_], start=False, stop=True)
                        CP('act', T["ydir"][:], pD[:])
                    for p in range(4):
                        pc_ = slice(p * 128, (p + 1) * 128)
                        MM(pA[:, pc_], o["btok"][:, pc_], Ub[:, pc_], start=True, stop=False)
                        MM(pA[:, pc_], o["ktok"][:, pc_], o["vb"][:, pc_], start=False, stop=True)
                    for hh in range(2):
                        rows = slice(hh * 64, hh * 64 + 64)
                        cols = slice(hh * 64, hh * 64 + 64)
                        WCb = bc(o["WC"][rows, :].unsqueeze(2), [64, 4, 64])
                        TT('dve', HfW[rows, :, cols], Hf[rows, :, cols], WCb, ALU.mult)
                        pblk = pA[rows, :].rearrange("p (q h i) -> p q h i", q=4, h=2)[:, :, hh, :]
                        TT('dve', tmpH[rows, :, cols], pblk, WCb, ALU.mult)
                        TT('dve', Hf[rows, :, cols], tmpH[rows, :, cols], HfW[rows, :, cols], ALU.add)
                        CP('act', Hb[rows, :, cols], Hf[rows, :, cols])
                    _stop('D')
                    if own:
                        if d == 0:
                            ot = own_base + i
                            S.dma(yF_s[ot, :, :], T["ydir"][:])
                            S.dma(bonF_s[ot, :, :], o["bon"][:])
                        else:
                            ot = own_base + (n_slots - 1 - i)
                            S.dma(T["yF"][:], yF_s[ot, :, :])
                            S.dma(bonF[:], bonF_s[ot, :, :])
                            TR(pB[:, 0:128], o["sgs"][:], idf[:])
                            CP('act', sgT[:], pB[:, 0:128])
                            MM(pC[:], sgT[:], gup[:])
                            CP('act', T["gte"][:], pC[:])
                            MM(pC[:], Jf[:], T["yF"][:])
                            MM(pB[:, 0:8], Jf[:], bonF[:])
                            y = T["yF"]
                            y3 = h8(y[:])
                            TT('dve', y[:], pC[:], T["ydir"][:], ALU.add)
                            RED('dve', s8[2][:], y3)
                            TS('dve', s8[2][:], s8[2][:], 1.0 / 64, None, ALU.mult)
                            TT('dve', y3, y3, bc(s8[2][:].unsqueeze(2), [128, 8, 64]), ALU.subtract)
                            TT('pool', T["f1"][:], y[:], y[:], ALU.mult)
                            RED('dve', s8[3][:], h8(T["f1"][:]))
                            rsqrt_small(s8[3][:], s8[3][:], 1.0 / 64, 64e-5)
                            TT('dve', y3, y3, bc(s8[3][:].unsqueeze(2), [128, 8, 64]), ALU.mult)
                            TT('pool', y[:], y[:], bt["ln_x_g"][:], ALU.mult)
                            TT('pool', y[:], y[:], bt["ln_x_b"][:], ALU.add)
                            TT('dve', s8[4][:], pB[:, 0:8], o["bon"][:], ALU.add)
                            TT('dve', h8(T["f1"][:]), h8(o["vfin"][:]), bc(s8[4][:].unsqueeze(2), [128, 8, 64]), ALU.mult)
                            TT('pool', y[:], y[:], T["f1"][:], ALU.add)
                            TT('dve', orw[:], y[:], T["gte"][:], ALU.mult)
                            for p in range(4):
                                MM(pD[:, p * 128:(p + 1) * 128], orw[:, p * 128:(p + 1) * 128], Jb[:])
                            CP('act', orwT[:], pD[:].rearrange("p (a b) -> p a b", a=4))
                            S.dma(mixT_s[4:8, :, ot * 128:(ot + 1) * 128].rearrange("c p t -> p c t"), orwT[:])

                stage12a(0)
                stage12b(0)
                if n_slots > 1:
                    stage12a(1)
                for i in range(n_slots):
                    S.rec_begin()
                    if i + 1 < n_slots:
                        stage12b(i + 1)
                    if i + 2 < n_slots:
                        stage12a(i + 2)
                    A_ = S.rec_end()
                    S.rec_begin()
                    stage34(i)
                    B_ = S.rec_end()
                    S.replay(S.merge(A_, B_) if not _os.environ.get("NOMERGE") else (B_ + A_))

            def set_state(src2):
                if src2 is None:
                    MSET('pool', Hf[:], 0.0)
                else:
                    CP('pool', Hf2, src2)
                CP('act', Hb2, Hf2)

            def switch_step(g):
                STT('dve', Hsave[:], Hf2, flg[:, g:g + 1], Hsave[:], ALU.mult, ALU.add)
                TS('dve', Hf2, Hf2, nflg[:, g:g + 1], None, ALU.mult)
                CP('act', Hb2, Hf2)

            MSET('pool', Hb[:], 0.0)
            MSET('pool', HfW[:], 0.0)
            MSET('pool', Hsave[:], 0.0)
            set_state(None)
            for g in range(NG):
                switch_step(g)
                load_group_params(g)
                rwkv_pass(0, xctx[g], OWN, 0, 0)
            switch_step(NG)
            CP('pool', Hbk[:], Hf2)
            set_state(Hsave[:])
            load_group_params(NG)
            rwkv_pass(0, xown[0], OWN, OWN, 0)
            set_state(None)
            rwkv_pass(0, xsmp[0], NST, NST, OWN)
            set_state(Hbk[:])
            load_group_params(NG + 1)
            rwkv_pass(1, xown[1], OWN, OWN, 0)
            set_state(None)
            rwkv_pass(1, xsmp[1], NST, NST, OWN)

        S.barrier()
        with ExitStack() as e2:
            wda = sbuf(e2, "wda", [128, 8, DA_COLS], BF16)
            wstage[0], wstage[1] = [sbuf(e2, "wstg2_%d" % i, [128, WS]) for i in range(2)]
            load_weight_bf16(wda, (0, DA_COLS), g1T)
            gq = sbuf(e2, "gq", [128, 64])
            gk = sbuf(e2, "gk", [128, 64])
            S.dma(gq[:], vec["q_norm_g"].partition_broadcast(128))
            S.dma(gk[:], vec["k_norm_g"].partition_broadcast(128))
            gsub = sbuf(e2, "gsub", [128, 128])
            S.dma(gsub[:], vec["subln_g"].partition_broadcast(128))
            TS('pool', gsub[:], gsub[:], 1.0 - LAMBDA_INIT, None, ALU.mult)
            lv = [sbuf(e2, "lv%d" % i, [128, 64]) for i in range(4)]
            for i, nm in enumerate(["lam_q1", "lam_k1", "lam_q2", "lam_k2"]):
                S.dma(lv[i][:], vec[nm].partition_broadcast(128))
            l2 = sbuf(e2, "l2", [128, 2])
            neglam = sbuf(e2, "neglam", [128, 1])
            TT('dve', lv[0][:], lv[0][:], lv[1][:], ALU.mult)
            TT('dve', lv[2][:], lv[2][:], lv[3][:], ALU.mult)
            RED('dve', l2[:, 0:1], lv[0][:])
            RED('dve', l2[:, 1:2], lv[2][:])
            ACT(l2[:], l2[:], AF.Exp)
            TT('dve', neglam[:], l2[:, 1:2], l2[:, 0:1], ALU.subtract)
            TS('dve', neglam[:], neglam[:], -LAMBDA_INIT, None, ALU.add)
            cs = sbuf(e2, "cs", [128, 64])
            zq = sbuf(e2, "zq", [128, 512])
            zk = sbuf(e2, "zk", [128, 512])
            zv = sbuf(e2, "zv", [128, 512])
            qn = sbuf(e2, "qn", [128, 512])
            u1 = sbuf(e2, "u1", [128, 256])
            u2 = sbuf(e2, "u2", [128, 256])
            rb = sbuf(e2, "rbf", [128, 512], BF16)
            s8q = sbuf(e2, "s8q", [128, 8])
            kTst = sbuf(e2, "kTst", [128, 4, 128], BF16)
            Vst = sbuf(e2, "Vst", [128, 4, 129], BF16)
            MSET('pool', Vst[:], 1.0)
            qT_P = sbuf(e2, "qT_P", [128, 4, OWN * 128], BF16)
            qT_S = sbuf(e2, "qT_S", [128, 4, NST * 128], BF16)

            def norm_rope(zsrc, g64, cstile, out_bf):
                z3 = zsrc[:].rearrange("p (h n) -> p h n", h=8)
                TT('pool', qn[:], zsrc[:], zsrc[:], ALU.mult)
                RED('dve', s8q[:], qn[:].rearrange("p (h n) -> p h n", h=8))
                rsqrt_small(s8q[:], s8q[:], 1.0 / 64, 1e-6)
                TT('dve', qn[:].rearrange("p (h n) -> p h n", h=8), z3, bc(s8q[:].unsqueeze(2), [128, 8, 64]), ALU.mult)
                TT('pool', qn[:].rearrange("p (h n) -> p h n", h=8), qn[:].rearrange("p (h n) -> p h n", h=8),
                   bc(g64[:].unsqueeze(1), [128, 8, 64]), ALU.mult)
                q4 = qn[:].rearrange("p (h c n) -> p h c n", h=8, c=2)
                o4 = out_bf[:].rearrange("p (h c n) -> p h c n", h=8, c=2)
                x1, x2 = q4[:, :, 0, :], q4[:, :, 1, :]
                cosb = bc(cstile[:, 0:32].unsqueeze(1), [128, 8, 32])
                sinb = bc(cstile[:, 32:64].unsqueeze(1), [128, 8, 32])
                a1 = u1[:].rearrange("p (h n) -> p h n", h=8)
                a2 = u2[:].rearrange("p (h n) -> p h n", h=8)
                TT('dve', a1, x1, cosb, ALU.mult)
                TT('pool', a2, x2, sinb, ALU.mult)
                TT('dve', o4[:, :, 0, :], a1, a2, ALU.subtract)
                TT('pool', a1, x2, cosb, ALU.mult)
                TT('dve', a2, x1, sinb, ALU.mult)
                TT('pool', o4[:, :, 1, :], a1, a2, ALU.add)

            ZK = [zk, sbuf(e2, "zk1", [128, 512])]
            ZV = [zv, sbuf(e2, "zv1", [128, 512])]
            CS = [cs, sbuf(e2, "cs1", [128, 64])]

            def kv_pass(xsrc, n_tiles, cs_src, kT_s, V_s):
                def stage_a(t):
                    S.dma(CS[t % 2][:], cs_src[t, :, :])

                    def evac(gi, pz, rs):
                        ACT([ZK, ZV][gi][t % 2][:], pz, AF.Copy, scale=rs[:, 0:1])
                    project_tile(xsrc[t, :, :], wda, [(512, 1024), (1024, 1536)], evac)

                def stage_b(t):
                    norm_rope(ZK[t % 2], gk, CS[t % 2], rb)
                    for p in range(4):
                        TR(K(pT[:, p * 128:(p + 1) * 128], "pT0"), rb[:, p * 128:(p + 1) * 128], idb[:])
                    CP('act', kTst[:], K(pT[:, 0:512].rearrange("p (a b) -> p a b", a=4), "pT0"))
                    S.dma(kT_s[:, :, t * 128:(t + 1) * 128].rearrange("h p t -> p h t"), kTst[:])
                    CP('act', Vst[:, :, 0:128], ZV[t % 2][:].rearrange("p (h n) -> p h n", h=4))
                    S.dma(V_s[:, :, t, :].rearrange("h p n -> p h n"), Vst[:])
                stage_a(0)
                for t in range(n_tiles):
                    S.rec_begin()
                    if t + 1 < n_tiles:
                        stage_a(t + 1)
                    A_ = S.rec_end()
                    S.rec_begin()
                    stage_b(t)
                    B_ = S.rec_end()
                    S.replay(S.merge(A_, B_))

            def q_pass(xsrc, tile0, n_tiles, cs_src, qT):
                for t in range(n_tiles):
                    S.dma(cs[:], cs_src[t, :, :])

                    def evac(gi, pz, rs):
                        ACT(zq[:], pz, AF.Copy, scale=rs[:, 0:1])
                    project_tile(xsrc[tile0 + t, :, :], wda, [(0, 512)], evac)
                    norm_rope(zq, gq, cs, rb)
                    for p in range(4):
                        TR(K(pT[:, p * 128:(p + 1) * 128], "pT0"), rb[:, p * 128:(p + 1) * 128], idb[:])
                    CP('act', qT[:, :, t * 128:(t + 1) * 128], K(pT[:, 0:512].rearrange("p (a b) -> p a b", a=4), "pT0"))

            kv_pass(xP, NPT, csP, kT_sP, V_sP)
            kv_pass(xsmp[0, 1:NST + 1], NST, csP, kT_sS, V_sS)
            q_pass(xown[0], 1, OWN, csQ, qT_P)
            q_pass(xsmp[0], 1, NST, csP, qT_S)

            PTb = [sbuf(e2, "PTb%d" % i, [128, 512], BF16) for i in range(2)]
            o0 = sbuf(e2, "o0", [128, 128])
            o1 = sbuf(e2, "o1", [128, 128])
            rc = sbuf(e2, "rc", [128, 2])
            ssd = sbuf(e2, "ssd", [128, 1])
            ob = sbuf(e2, "ob", [128, 128], BF16)
            oT = sbuf(e2, "oT", [128, 128], BF16)
            pO = [[pA, pB], [pC, pD]]
            pS = [pL0, pL1]

            osv = [sbuf(e2, "osv%d" % j, [128, 128]) for j in range(4)]
            pOb = [pA, pB, pC, pD]

            def attention(qT, n_q_tiles, kT_s, V_s, n_kt, tok_base):
                kTh = sbuf(e2a, "kTh_%d" % tok_base, [128, n_kt * 128], BF16)
                Vh = sbuf(e2a, "Vh_%d" % tok_base, [128, n_kt, 129], BF16)
                cnt = 0
                for h in range(4):
                    S.dma(kTh[:], kT_s[h, :, :])
                    S.dma(Vh[:], V_s[h, :, :, :])
                    for qg in range(0, n_q_tiles, 4):
                        nq = min(4, n_q_tiles - qg)
                        for c in range(2):
                            rows = slice(c * 64, c * 64 + 64)
                            def qk(kt_, slot):
                                MM(pS[slot % 2][:, 0:nq * 128], kTh[rows, kt_ * 128:(kt_ + 1) * 128], qT[rows, h, qg * 128:(qg + nq) * 128])
                            qk(0, cnt)
                            for kt in range(n_kt):
                                ps_ = pS[cnt % 2]
                                pt_ = PTb[cnt % 2]
                                if kt + 1 < n_kt:
                                    qk(kt + 1, cnt + 1)
                                cnt += 1
                                ACT(pt_[:, 0:nq * 128], ps_[:, 0:nq * 128], AF.Exp, scale=0.125)
                                for j in range(nq):
                                    MM(pOb[j][:, 0:129], pt_[:, j * 128:(j + 1) * 128], Vh[:, kt, :], start=(kt == 0), stop=(kt == n_kt - 1))
                            for j in range(nq):
                                RECIP(rc[:, c:c + 1], pOb[j][:, 128:129])
                                if c == 0:
                                    ACT(osv[j][:], pOb[j][:, 0:128], AF.Copy, scale=rc[:, 0:1])
                                else:
                                    TT('dve', rc[:, 1:2], rc[:, 1:2], neglam[:], ALU.mult)
                                    STT('dve', o1[:], pOb[j][:, 0:128], rc[:, 1:2], osv[j][:], ALU.mult, ALU.add)
                                    ACT(o0[:], o1[:], AF.Square, accum=ssd[:])
                                    rsqrt_small(ssd[:], ssd[:], 1.0 / 128, 1e-6)
                                    STT('dve', ob[:], o1[:], ssd[:, 0:1], gsub[:], ALU.mult, ALU.mult)
                                    TR(pT[:, 0:128], ob[:], idb[:])
                                    CP('act', oT[:], pT[:, 0:128])
                                    tk = tok_base + (qg + j) * 128
                                    S.dma(mixT_s[h, :, tk:tk + 128], oT[:])

            S.barrier()
            wcast2 = [sbuf(e2, "wcast2_%d" % i, [128, WS], BF16) for i in range(2)]

            def precast_w1():
                wc_ = 0
                for kc in range(8):
                    for s0 in range(0, DFF, WS):
                        ws = wstage[xcount[0] % 2]
                        xcount[0] += 1
                        wcb = wcast2[wc_ % 2]
                        wc_ += 1
                        S.dma(ws[:, 0:WS], w_ff1[kc * 128:(kc + 1) * 128, s0:s0 + WS])
                        TS('pool', wcb[:], ws[:, 0:WS], g2T[:, kc:kc + 1], None, ALU.mult)
                        S.dma(w1b_s[kc, :, s0:s0 + WS], wcb[:])
            with ExitStack() as e2a:
                S.rec_begin()
                precast_w1()
                A_ = S.rec_end()
                S.rec_begin()
                attention(qT_P, OWN, kT_sP, V_sP, NPT, 0)
                B_ = S.rec_end()
                S.replay(S.merge(A_, B_))
            S.barrier()
            with ExitStack() as e2a:
                attention(qT_S, NST, kT_sS, V_sS, NST, OWN * 128)

        S.barrier()
        with ExitStack() as e3:
            wo = sbuf(e3, "wo", [128, 8, D], BF16)
            w2 = sbuf(e3, "w2", [128, 32, D], BF16)
            w1blk = [sbuf(e3, "w1blk%d" % i, [128, 8, 512], BF16) for i in range(2)]
            wcast = [sbuf(e3, "wcast%d" % i, [128, WS], BF16) for i in range(2)]
            wstage[0], wstage[1] = [sbuf(e3, "wstg3_%d" % i, [128, WS]) for i in range(2)]
            wc = 0
            for kc in range(8):
                for s0 in range(0, D, WS):
                    ws = wstage[xcount[0] % 2]
                    xcount[0] += 1
                    S.dma(ws[:, 0:WS], w_out[kc * 128:(kc + 1) * 128, s0:s0 + WS])
                    CP('pool', wo[:, kc, s0:s0 + WS], ws[:, 0:WS])
            for fc in range(32):
                for s0 in range(0, D, WS):
                    ws = wstage[xcount[0] % 2]
                    xcount[0] += 1
                    S.dma(ws[:, 0:WS], w_ff2[fc * 128:(fc + 1) * 128, s0:s0 + WS])
                    CP('pool', w2[:, fc, s0:s0 + WS], ws[:, 0:WS])
            GT = 4
            mixg = sbuf(e3, "mixg", [128, 8, GT * 128], BF16)
            xmid = [sbuf(e3, "xmid%d" % j, [128, D]) for j in range(GT)]
            rs2 = sbuf(e3, "rs2", [128, GT])
            xmb = sbuf(e3, "xmb", [128, D], BF16)
            xmT = sbuf(e3, "xmT", [128, 8, GT * 128], BF16)
            uT = sbuf(e3, "uT", [128, 32, GT * 128], BF16)
            rl = [sbuf(e3, "rl%d" % i, [128, GT * 128]) for i in range(2)]
            yo = sbuf(e3, "yo", [128, D])
            own_src = [(xown[0], 1 + t, out_p, t) for t in range(OWN)] + [(xsmp[0], 1 + t, out_s, t) for t in range(NST)]
            wbc = 0
            for g0 in range(0, NOWN, GT):
                ng = min(GT, NOWN - g0)
                nt = ng * 128
                S.dma(mixg[:, :, 0:nt], mixT_s[:, :, g0 * 128:g0 * 128 + nt].rearrange("c p t -> p c t"))
                for j in range(ng):
                    src, sidx, _, _ = own_src[g0 + j]
                    xr = xt[xcount[0] % 2]
                    xcount[0] += 1
                    S.dma(xr[:], src[sidx, :, :])
                    for hf in range(2):
                        pz = [pZ, pL0][hf]
                        for ch in range(8):
                            MM(pz[:], mixg[:, ch, j * 128:(j + 1) * 128], wo[:, ch, hf * 512:(hf + 1) * 512], start=(ch == 0), stop=(ch == 7))
                        TT('dve', xmid[j][:, hf * 512:(hf + 1) * 512], pz[:], xr[:, hf * 512:(hf + 1) * 512], ALU.add)
                    ACT(junk[:], xmid[j][:], AF.Square, accum=ssq[:])
                    rsqrt_small(rs2[:, j:j + 1], ssq[:], 1.0 / D, 1e-6)
                    CP('pool', xmb[:], xmid[j][:])
                    for hlf in range(2):
                        for k4 in range(4):
                            kc = hlf * 4 + k4
                            TR(K(pT[:, hlf * 512 + k4 * 128: hlf * 512 + (k4 + 1) * 128], "pT%d" % hlf), xmb[:, kc * 128:(kc + 1) * 128], idb[:])
                        CP('act' if hlf == 0 else 'dve', xmT[:, hlf * 4:(hlf + 1) * 4, j * 128:(j + 1) * 128],
                           K(pT[:, hlf * 512:(hlf + 1) * 512].rearrange("p (a b) -> p a b", a=4), "pT%d" % hlf))
                    TT('dve', rs2[:, j:j + 1], rs2[:, j:j + 1], rs2[:, j:j + 1], ALU.mult)
                for fb in range(8):
                    wb_ = w1blk[wbc % 2]
                    wbc += 1
                    S.dma(wb_[:], w1b_s[:, :, fb * 512:(fb + 1) * 512].rearrange("k p f -> p k f"))
                    for f4 in range(4):
                        fc = fb * 4 + f4
                        pf = [pA, pB, pC, pD][fc % 4]
                        for kc in range(8):
                            MM(pf[:, 0:nt], wb_[:, kc, f4 * 128:(f4 + 1) * 128], xmT[:, kc, 0:nt], start=(kc == 0), stop=(kc == 7))
                        r_ = rl[fc % 2]
                        ACT(r_[:, 0:nt], pf[:, 0:nt], AF.Relu)
                        TT('pool' if fc % 2 else 'dve', uT[:, fc, 0:nt], r_[:, 0:nt], r_[:, 0:nt], ALU.mult)
                for j in range(ng):
                    _, _, dst, didx = own_src[g0 + j]
                    for hf in range(2):
                        pz = [pL1, pZ][hf]
                        for fc in range(32):
                            MM(pz[:], uT[:, fc, j * 128:(j + 1) * 128], w2[:, fc, hf * 512:(hf + 1) * 512], start=(fc == 0), stop=(fc == 31))
                        STT('dve', yo[:, hf * 512:(hf + 1) * 512], pz[:], rs2[:, j:j + 1], xmid[j][:, hf * 512:(hf + 1) * 512], ALU.mult, ALU.add)
                    S.dma(dst[didx, :, :], yo[:])
        S.finish('sp')
        S.finish('pool')
        build_nc.ninst = S.ninst
    return nc


def _rope_tables(n_pos):
    inv_freq = (1.0 / (10000.0 ** (np.arange(0, 64, 2, dtype=np.float32) / np.float32(64)))).astype(np.float32)
    ang = np.arange(n_pos, dtype=np.float32)[:, None] * inv_freq[None, :]
    return np.concatenate([np.cos(ang), np.sin(ang)], axis=-1).astype(np.float32)


def _consts():
    s = np.arange(128)[:, None]
    t = np.arange(128)[None, :]
    maskA = np.zeros((2, 128, 1536), np.float32)
    ltri = np.zeros((2, 128, 256), np.float32)
    for d in range(2):
        strict = (s < t) if d == 0 else (s > t)
        incl = (s <= t) if d == 0 else (s >= t)
        maskA[d] = np.concatenate([strict] * 4 + [incl] * 4 + [strict.T] * 4, axis=1).astype(np.float32)
        ltri[d, :, 0:128] = -CDEC * incl
        ltri[d, :, 128:256] = -CDEC * strict
    return maskA, ltri


_NC_CACHE = {}


def kernel(**inputs):
    f32 = lambda a: np.ascontiguousarray(np.asarray(a, dtype=np.float32))
    x_prompt = f32(inputs["x_prompt"])
    x_sample = f32(inputs["x_sample"])
    SEQ = x_prompt.shape[1]
    DSEQ = x_sample.shape[1]
    NPT, NST = SEQ // 128, DSEQ // 128
    OWN = NPT // NCORES
    key = (NPT, NST)
    if key not in _NC_CACHE:
        _NC_CACHE[key] = build_nc(NPT, NST)
    nc = _NC_CACHE[key]
    xp = x_prompt[0].reshape(NPT, 128, D)
    ztile = np.zeros((1, 128, D), np.float32)
    cs_all = _rope_tables(max(SEQ, DSEQ)).reshape(-1, 128, 64)
    maskA, ltri = _consts()
    shared = {
        "xP": xp, "csP": cs_all[:NPT],
        "w_in": f32(inputs["w_in"][0]), "w_out": f32(inputs["w_out"][0]),
        "w_ff1": f32(inputs["w_ff1"][0]), "w_ff2": f32(inputs["w_ff2"][0]),
        "norm1_g": f32(f32(inputs["norm1_g"][0]).reshape(8, 128).T), "norm2_g": f32(f32(inputs["norm2_g"][0]).reshape(8, 128).T),
        "w0": f32(inputs["w0"][0]), "a0": f32(inputs["a0"][0]),
        "w_up": f32(inputs["w_up"][0]), "a_up": f32(inputs["a_up"][0]), "g_up": f32(inputs["g_up"][0]),
        "ident_d": np.eye(128, dtype=np.float32), "maskA_d": maskA, "ltri_d": ltri,
    }
    for nm in ["q_norm_g", "k_norm_g", "lam_q1", "lam_k1", "lam_q2", "lam_k2", "subln_g", "mu_prev",
               "mu_next", "k_k", "k_a", "r_k", "ln_x_g", "ln_x_b"]:
        shared[nm] = f32(inputs[nm][0]).reshape(1, -1)
    xtok = x_prompt[0]
    SEGT = OWN * 128
    NG = NCORES - 1

    def seg_halo(tok, s, segt):
        n = tok.shape[0]
        out = np.zeros((segt + 256, D), np.float32)
        lo, hi = s * segt - 128, (s + 1) * segt + 128
        a, b = max(lo, 0), min(hi, n)
        out[a - lo:b - lo] = tok[a:b]
        return out

    mu_p = f32(inputs["mu_prev"][0]); mu_n = f32(inputs["mu_next"][0])
    dir_par = []
    for d in range(2):
        dir_par.append(dict(mu=np.stack([mu_p, mu_n] if d == 0 else [mu_n, mu_p]),
                            w0=f32(inputs["w0"][0, d]), a0=f32(inputs["a0"][0, d]),
                            wup=f32(inputs["w_up"][0, d]), aup=f32(inputs["a_up"][0, d])))
    Jm = np.ascontiguousarray(np.eye(128, dtype=np.float32)[::-1])
    in_maps = []
    for c in range(NCORES):
        m = dict(shared)
        groups, dirs = [], []
        for g in range(NG):
            if g < c:
                groups.append(seg_halo(xtok, g, SEGT)); dirs.append(0)
            else:
                groups.append(seg_halo(xtok, NG + c - g, SEGT)[::-1]); dirs.append(1)
        m["xctx"] = np.ascontiguousarray(np.stack(groups)).reshape(NG, OWN + 2, 128, D)
        own = seg_halo(xtok, c, SEGT)
        m["xown"] = np.ascontiguousarray(np.stack([own, own[::-1]])).reshape(2, OWN + 2, 128, D)
        smp = seg_halo(x_sample[c], 0, DSEQ)
        m["xsmp"] = np.ascontiguousarray(np.stack([smp, smp[::-1]])).reshape(2, NST + 2, 128, D)
        dirs = dirs + [0, 1]
        m["grp_mu"] = np.ascontiguousarray(np.stack([dir_par[d]["mu"] for d in dirs]))
        m["grp_w0"] = np.ascontiguousarray(np.stack([dir_par[d]["w0"] for d in dirs]))
        m["grp_a0"] = np.ascontiguousarray(np.stack([dir_par[d]["a0"] for d in dirs]))
        m["grp_wup"] = np.ascontiguousarray(np.stack([dir_par[d]["wup"] for d in dirs]))
        m["grp_aup"] = np.ascontiguousarray(np.stack([dir_par[d]["aup"] for d in dirs]))
        fl = np.zeros((1, 8), np.float32); fl[0, c] = 1.0
        m["flags_d"] = fl
        m["J_d"] = Jm
        m["csQ"] = np.ascontiguousarray(cs_all[OWN * c:OWN * (c + 1)])
        in_maps.append(m)
    if inputs.get("_maps_only"):
        return nc, in_maps
    res = run_bass_kernel_spmd(nc, in_maps, core_ids=list(range(NCORES)))
    y_prompt = np.concatenate([res.results[c]["out_p"].reshape(OWN * 128, D) for c in range(NCORES)], axis=0)[None]
    y_sample = np.stack([res.results[c]["out_s"].reshape(NST * 128, D) for c in range(NCORES)], axis=0)
    return (y_prompt.astype(np.float32), y_sample.astype(np.float32))
```

```python
import math
import numpy as np
from contextlib import ExitStack
import concourse.bass as bass
import concourse.mybir as mybir
from concourse.bass_utils import run_bass_kernel_spmd

F32 = mybir.dt.float32
BF16 = mybir.dt.bfloat16
AF = mybir.ActivationFunctionType
ALU = mybir.AluOpType
AX = mybir.AxisListType

D = 1024
NCORES = 8
DA_COLS = 1536
RW_COLS = 1792
IN_COLS = 3328
DFF = 4096
CDEC = math.exp(-0.5)
LAMBDA_INIT = 0.8 - 0.6 * math.exp(0.0)


class Sched:
    def __init__(self, nc, es, n_dma_sems=24):
        self.nc = nc
        self.E = {'pe': nc.tensor, 'act': nc.scalar, 'dve': nc.vector, 'pool': nc.gpsimd, 'sp': nc.sync}
        self.sem = {k: es.enter_context(nc.semaphore("s_" + k)) for k in ['pe', 'act', 'dve', 'pool']}
        self.cnt = {k: 0 for k in self.sem}
        self.seen = {k: {} for k in self.E}
        self.lastw = {}
        self.readers = {}
        self.dsems = [es.enter_context(nc.semaphore("d%d" % i)) for i in range(n_dma_sems)]
        self.dcnt = [0] * n_dma_sems
        self.dnext = 0
        self.ninst = 0
        self.psum_names = set()
        self._rec = None

    def _key(self, a):
        if isinstance(a, tuple):
            if a[0].name in self.psum_names:
                return a[0].name
            return a[1]
        return a.name

    def _ap(self, a):
        return a[0] if isinstance(a, tuple) else a

    def _wait(self, eng, tok):
        sem, val, name = tok
        if self.seen[eng].get(name, 0) >= val:
            return
        self.E[eng].wait_ge(sem, val)
        self.seen[eng][name] = val

    def _deps(self, eng, ins, outs):
        toks = []
        for a in ins:
            k = self._key(a)
            if k in self.lastw:
                toks.append(self.lastw[k])
            if k in self.psum_names:
                toks.extend(t for e, t in self.readers.get(k, {}).items() if e != eng)
        for a in outs:
            k = self._key(a)
            if k in self.lastw:
                toks.append(self.lastw[k])
            toks.extend(self.readers.get(k, {}).values())
        for t in toks:
            if eng == 'pe' and t[2] == 'pe':
                continue
            self._wait(eng, t)

    def _commit(self, tok, ins, outs):
        for a in outs:
            k = self._key(a)
            self.lastw[k] = tok
            self.readers[k] = {}
        for a in ins:
            k = self._key(a)
            self.readers.setdefault(k, {})[tok[2]] = tok

    def rec_begin(self):
        self._rec = []

    def rec_end(self):
        r, self._rec = self._rec, None
        return r

    def replay(self, items):
        for it in items:
            if it[0] == 'op':
                self.op(*it[1:])
            else:
                self.dma(it[1], it[2], q=it[3])

    @staticmethod
    def merge(a, b):
        out, ia, ib = [], 0, 0
        while ia < len(a) or ib < len(b):
            if ib >= len(b) or (ia < len(a) and ia * len(b) <= ib * len(a)):
                out.append(a[ia]); ia += 1
            else:
                out.append(b[ib]); ib += 1
        return out

    def op(self, eng, fn, outs, ins):
        if _STOPPED[0]:
            return None
        if self._rec is not None:
            self._rec.append(('op', eng, fn, outs, ins))
            return None
        self._deps(eng, ins, outs)
        inst = fn()
        self.cnt[eng] += 1
        self.ninst += 1
        inst.then_inc(self.sem[eng], 1)
        tok = (self.sem[eng], self.cnt[eng], eng)
        self._commit(tok, ins, outs)
        return tok

    def dma(self, out, in_, q='sp'):
        if _STOPPED[0]:
            return None
        if self._rec is not None:
            self._rec.append(('dma', out, in_, q))
            return None
        i = self.dnext
        self.dnext = (self.dnext + 1) % len(self.dsems)
        name = 'dma%d' % i
        if self.dcnt[i] > 0:
            self._wait(q, (self.dsems[i], 16 * self.dcnt[i], name))
        self._deps(q, [in_], [out])
        self.E[q].dma_start(out=self._ap(out), in_=self._ap(in_)).then_inc(self.dsems[i], 16)
        self.dcnt[i] += 1
        self.ninst += 1
        tok = (self.dsems[i], 16 * self.dcnt[i], name)
        self._commit(tok, [in_], [out])
        return tok

    def barrier(self):
        if _STOPPED[0]:
            return
        for e in self.E:
            for k in self.sem:
                if self.cnt[k] > 0:
                    self._wait(e, (self.sem[k], self.cnt[k], k))
            for i in range(len(self.dsems)):
                if self.dcnt[i] > 0:
                    self._wait(e, (self.dsems[i], 16 * self.dcnt[i], 'dma%d' % i))
        self.lastw = {}
        self.readers = {}

    def finish(self, q='sp'):
        for i in range(len(self.dsems)):
            if self.dcnt[i] > 0:
                self._wait(q, (self.dsems[i], 16 * self.dcnt[i], 'dma%d' % i))


class _Stop(Exception):
    pass


import os as _os
_KSTOP = _os.environ.get('KSTOP', '')


_STOPPED = [False]


def _stop(tag):
    if _KSTOP == tag:
        _STOPPED[0] = True


def K(ap, key):
    return (ap, key)


def build_nc(NPT, NST, inv_dt=BF16):
    OWN = NPT // NCORES
    NOWN = OWN + NST
    NTOK = NOWN * 128
    nc = bass.Bass("TRN2", target_bir_lowering=False)
    dram = lambda name, shape, dt=F32, kind="ExternalInput": nc.dram_tensor(name, shape, dt, kind=kind).ap()
    NG = NCORES - 1
    xctx = dram("xctx", [NG, OWN + 2, 128, D])
    xown = dram("xown", [2, OWN + 2, 128, D])
    xP = dram("xP", [NPT, 128, D])
    xsmp = dram("xsmp", [2, NST + 2, 128, D])
    grp_mu = dram("grp_mu", [NG + 2, 2, RW_COLS])
    grp_w0 = dram("grp_w0", [NG + 2, 512])
    grp_a0 = dram("grp_a0", [NG + 2, 512])
    grp_wup = dram("grp_wup", [NG + 2, 64, 512])
    grp_aup = dram("grp_aup", [NG + 2, 64, 512])
    flags_d = dram("flags_d", [1, 8])
    J_d = dram("J_d", [128, 128])
    csP = dram("csP", [NPT, 128, 64])
    csQ = dram("csQ", [OWN, 128, 64])
    w_in = dram("w_in", [D, IN_COLS])
    w_out = dram("w_out", [D, D])
    w_ff1 = dram("w_ff1", [D, DFF])
    w_ff2 = dram("w_ff2", [DFF, D])
    norm1_g = dram("norm1_g", [128, 8])
    norm2_g = dram("norm2_g", [128, 8])
    vec = {}
    for nm, n in [("q_norm_g", 64), ("k_norm_g", 64), ("lam_q1", 64), ("lam_k1", 64), ("lam_q2", 64),
                  ("lam_k2", 64), ("subln_g", 128), ("mu_prev", RW_COLS), ("mu_next", RW_COLS),
                  ("k_k", 512), ("k_a", 512), ("r_k", 512), ("ln_x_g", 512), ("ln_x_b", 512)]:
        vec[nm] = dram(nm, [1, n])
    w0 = dram("w0", [2, 512])
    a0 = dram("a0", [2, 512])
    w_up = dram("w_up", [2, 64, 512])
    a_up = dram("a_up", [2, 64, 512])
    g_up = dram("g_up", [128, 512])
    ident_d = dram("ident_d", [128, 128])
    maskA_d = dram("maskA_d", [2, 128, 1536])
    ltri_d = dram("ltri_d", [2, 128, 256])
    out_p = dram("out_p", [OWN, 128, D], kind="ExternalOutput")
    out_s = dram("out_s", [NST, 128, D], kind="ExternalOutput")
    scr = lambda name, shape, dt=F32: nc.dram_tensor(name, shape, dt).ap()
    yF_s = scr("yF_s", [NOWN, 128, 512])
    bonF_s = scr("bonF_s", [NOWN, 128, 8])
    mixT_s = scr("mixT_s", [8, 128, NTOK], BF16)
    kT_sP = scr("kT_sP", [4, 128, NPT * 128], BF16)
    kT_sS = scr("kT_sS", [4, 128, NST * 128], BF16)
    V_sP = scr("V_sP", [4, 128, NPT, 129], BF16)
    V_sS = scr("V_sS", [4, 128, NST, 129], BF16)
    w1b_s = scr("w1b_s", [8, 128, DFF], BF16)

    with ExitStack() as es:
        S = Sched(nc, es)
        V, G, A, P = nc.vector, nc.gpsimd, nc.scalar, nc.tensor

        def sbuf(stack, name, shape, dt=F32):
            return stack.enter_context(nc.sbuf_tensor(name, shape, dt))

        def psum(stack, name, shape, dt=F32):
            S.psum_names.add(name)
            return stack.enter_context(nc.psum_tensor(name, shape, dt))

        def MM(out, lhsT, rhs, start=True, stop=True):
            S.op('pe', lambda: P.matmul(S._ap(out), S._ap(lhsT), S._ap(rhs), start=start, stop=stop), [out], [lhsT, rhs])

        def TR(out, in_, idt):
            S.op('pe', lambda: P.transpose(S._ap(out), S._ap(in_), S._ap(idt)), [out], [in_, idt])

        def ACT(out, in_, func, bias=None, scale=None, accum=None, extra_out=()):
            kw = {}
            ins = [in_]
            if bias is not None:
                kw['bias'] = S._ap(bias) if not isinstance(bias, float) else bias
                if not isinstance(bias, float):
                    ins.append(bias)
            if scale is not None:
                kw['scale'] = S._ap(scale) if not isinstance(scale, float) else scale
                if not isinstance(scale, float):
                    ins.append(scale)
            outs = [out] + list(extra_out)
            if accum is not None:
                kw['accum_out'] = S._ap(accum)
                outs.append(accum)
            S.op('act', lambda: A.activation(S._ap(out), S._ap(in_), func, **kw), outs, ins)

        def EW(eng):
            return {'dve': V, 'pool': G}[eng]

        def TT(eng, out, a, b, op):
            S.op(eng, lambda: EW(eng).tensor_tensor(S._ap(out), S._ap(a), S._ap(b), op), [out], [a, b])

        def TS(eng, out, a, s1, s2, op0, op1=None):
            ins = [a] + [s for s in (s1, s2) if s is not None and not isinstance(s, float)]
            g = lambda s: s if (s is None or isinstance(s, float)) else S._ap(s)
            if op1 is None:
                S.op(eng, lambda: EW(eng).tensor_scalar(S._ap(out), S._ap(a), g(s1), None, op0), [out], ins)
            else:
                S.op(eng, lambda: EW(eng).tensor_scalar(S._ap(out), S._ap(a), g(s1), g(s2), op0, op1), [out], ins)

        def STT(eng, out, a, s, b, op0, op1):
            ins = [a, b] + ([] if isinstance(s, float) else [s])
            sv = s if isinstance(s, float) else S._ap(s)
            S.op(eng, lambda: EW(eng).scalar_tensor_tensor(S._ap(out), S._ap(a), sv, S._ap(b), op0, op1), [out], ins)

        def CP(eng, out, in_):
            if eng == 'act':
                S.op('act', lambda: A.copy(S._ap(out), S._ap(in_)), [out], [in_])
            else:
                S.op(eng, lambda: EW(eng).tensor_copy(S._ap(out), S._ap(in_)), [out], [in_])

        def RED(eng, out, in_, op=ALU.add):
            S.op(eng, lambda: EW(eng).tensor_reduce(S._ap(out), S._ap(in_), AX.X, op), [out], [in_])

        def MSET(eng, out, val):
            S.op(eng, lambda: EW(eng).memset(S._ap(out), val), [out], [])

        def RECIP(out, in_):
            S.op('dve', lambda: V.reciprocal(S._ap(out), S._ap(in_)), [out], [in_])

        def rsqrt_small(out, in_, mul, add):
            TS('dve', out, in_, float(mul), float(add), ALU.mult, ALU.add)
            S.op('act', lambda: A.sqrt(S._ap(out), S._ap(out)), [out], [out])
            RECIP(out, out)

        def bc(ap, shape):
            return ap.to_broadcast(shape)

        idf = sbuf(es, "idf", [128, 128])
        idb = sbuf(es, "idb", [128, 128], BF16)
        g1T = sbuf(es, "g1T", [128, 8])
        g2T = sbuf(es, "g2T", [128, 8])
        S.dma(idf[:], ident_d[:, :])
        CP('dve', idb[:], idf[:])
        S.dma(g1T[:], norm1_g[:, :])
        S.dma(g2T[:], norm2_g[:, :])

        pT = psum(es, "pT", [128, 1024], BF16)
        pZ = psum(es, "pZ", [128, 512])
        pL0 = psum(es, "pL0", [128, 512])
        pL1 = psum(es, "pL1", [128, 512])
        pA = psum(es, "pA", [128, 512])
        pB = psum(es, "pB", [128, 512])
        pC = psum(es, "pC", [128, 512])
        pD = psum(es, "pD", [128, 512])

        xt = [sbuf(es, "xt%d" % i, [128, D]) for i in range(2)]
        xb = sbuf(es, "xb", [128, D], BF16)
        junk = xb
        xT = sbuf(es, "xT", [128, 8, 128], BF16)
        ssq = sbuf(es, "ssq", [128, 1])
        rstd = sbuf(es, "rstd", [128, 1])
        WS = 512
        wstage = [None, None]
        xcount = [0]

        def load_weight_bf16(dst, src_cols, gT):
            c0, c1 = src_cols
            n = c1 - c0
            for kc in range(8):
                for s0 in range(0, n, WS):
                    s1 = min(n, s0 + WS)
                    ws = wstage[xcount[0] % 2]
                    xcount[0] += 1
                    S.dma(ws[:, 0:s1 - s0], w_in[kc * 128:(kc + 1) * 128, c0 + s0:c0 + s1])
                    TS('pool', dst[:, kc, s0:s1], ws[:, 0:s1 - s0], gT[:, kc:kc + 1], None, ALU.mult)

        def project_tile(src_ap, w_b, col_groups, evac):
            xtile = xt[xcount[0] % 2]
            xcount[0] += 1
            S.dma(xtile[:], src_ap)
            ACT(junk[:], xtile[:], AF.Square, accum=ssq[:])
            rsqrt_small(rstd[:], ssq[:], 1.0 / D, 1e-6)
            CP('pool', xb[:], xtile[:])
            for hlf in range(2):
                for k4 in range(4):
                    kc = hlf * 4 + k4
                    TR(K(pT[:, hlf * 512 + k4 * 128: hlf * 512 + (k4 + 1) * 128], "pT%d" % hlf), xb[:, kc * 128:(kc + 1) * 128], idb[:])
                eng = 'act' if hlf == 0 else 'dve'
                CP(eng, K(xT[:, hlf * 4:(hlf + 1) * 4, :], "xT%d" % hlf),
                   K(pT[:, hlf * 512:(hlf + 1) * 512].rearrange("p (a b) -> p a b", a=4), "pT%d" % hlf))
            zb_ = [pZ, pL0, pL1]
            for gi, (c0, c1) in enumerate(col_groups):
                n = c1 - c0
                pz_ = zb_[gi % 3]
                for kc in range(8):
                    MM(pz_[:, 0:n], K(xT[:, kc, :], "xT%d" % (kc // 4)), w_b[:, kc, c0:c1], start=(kc == 0), stop=(kc == 7))
                evac(gi, pz_[:, 0:n], rstd)

        with ExitStack() as e1:
            wrw = sbuf(e1, "wrw", [128, 8, RW_COLS], BF16)
            Z = [sbuf(e1, "Zr%d" % i, [128, RW_COLS]) for i in range(3)]
            wstage[0], wstage[1] = Z[1], Z[2]
            load_weight_bf16(wrw, (DA_COLS, IN_COLS), g1T)
            ZP = [sbuf(e1, "zp%d" % j, [128, RW_COLS]) for j in range(2)]
            zn = sbuf(e1, "zn", [128, RW_COLS])
            mu_p = sbuf(e1, "mu_p", [128, RW_COLS])
            mu_n = sbuf(e1, "mu_n", [128, RW_COLS])
            mu_c = sbuf(e1, "mu_c", [128, RW_COLS])
            S.dma(mu_p[:], vec["mu_prev"].partition_broadcast(128))
            S.dma(mu_n[:], vec["mu_next"].partition_broadcast(128))
            TT('pool', mu_c[:], mu_p[:], mu_n[:], ALU.add)
            TS('pool', mu_c[:], mu_c[:], -1.0, 1.0, ALU.mult, ALU.add)
            bt = {}
            for nm in ["k_k", "k_a", "r_k", "ln_x_g", "ln_x_b"]:
                bt[nm] = sbuf(e1, "b_" + nm, [128, 512])
                S.dma(bt[nm][:], vec[nm].partition_broadcast(128))
            w0a0 = sbuf(e1, "w0a0", [128, 512])
            onesf = sbuf(e1, "onesf", [128, 128])
            MSET('pool', onesf[:], 1.0)
            MSET('pool', w0a0[:], 0.0)
            waup = sbuf(e1, "waup", [128, 512])
            gup = sbuf(e1, "gup", [128, 512])
            S.dma(gup[:], g_up[:, :])
            mS4 = sbuf(e1, "mS4", [128, 512])
            mI4 = sbuf(e1, "mI4", [128, 512])
            mB4 = sbuf(e1, "mB4", [128, 512])
            ltri = sbuf(e1, "ltri", [128, 256])
            negc = sbuf(e1, "negc", [128, 1])
            MSET('pool', negc[:], -CDEC)
            T = {}
            for nm in ["sgw", "arate", "Winv", "We", "kk", "kd", "t0", "t1", "ydir", "yF", "gte", "f1"]:
                T[nm] = sbuf(e1, "T_" + nm, [128, 512])
            T["Wt"] = T["kd"]
            LO = [sbuf(e1, "lo%d" % j, [128, 128]) for j in range(2)]
            LOT = [sbuf(e1, "loT%d" % j, [128, 128]) for j in range(2)]
            sgT = sbuf(e1, "sgT", [128, 128])
            s8 = [sbuf(e1, "s8_%d" % i, [128, 8]) for i in range(6)]
            bonF = sbuf(e1, "bonF", [128, 8])
            rtok = sbuf(e1, "rtok", [128, 512], BF16)
            orw = sbuf(e1, "orw", [128, 512], BF16)
            orwT = sbuf(e1, "orwT", [128, 4, 128], BF16)
            BS = []
            for j in range(2):
                o = {}
                o["arT"] = sbuf(e1, "arT%d" % j, [128, 4, 2, 128], BF16)
                o["bT"] = sbuf(e1, "bT%d" % j, [128, 4, 128], BF16)
                o["kTt"] = sbuf(e1, "kTt%d" % j, [128, 4, 128], BF16)
                o["atok"] = sbuf(e1, "atok%d" % j, [128, 512], BF16)
                o["btok"] = sbuf(e1, "btok%d" % j, [128, 512], BF16)
                o["ktok"] = sbuf(e1, "ktok%d" % j, [128, 512], BF16)
                o["vb"] = sbuf(e1, "vb%d" % j, [128, 512], BF16)
                o["WC"] = sbuf(e1, "WC%d" % j, [128, 4])
                o["vfin"] = sbuf(e1, "vfin%d" % j, [128, 512])
                o["sgs"] = sbuf(e1, "sgs%d" % j, [128, 128])
                o["bon"] = sbuf(e1, "bon%d" % j, [128, 8])
                BS.append(o)
            P1T = sbuf(e1, "P1T", [128, 4, 128], BF16)
            U0 = sbuf(e1, "U0", [128, 512])
            ArbT = sbuf(e1, "ArbT", [128, 8, 128], BF16)
            ArkT = sbuf(e1, "ArkT", [128, 8, 128], BF16)
            NN = [sbuf(e1, "NN%d" % j, [128, 8, 128], inv_dt) for j in range(2)]
            NTt = [sbuf(e1, "NTt%d" % j, [128, 8, 128], inv_dt) for j in range(2)]
            RR = [sbuf(e1, "RR%d" % j, [128, 8, 128], inv_dt) for j in range(2)]
            AakT = sbuf(e1, "AakT", [128, 8, 128], BF16)
            X0a = sbuf(e1, "X0a", [128, 512], BF16)
            idi = sbuf(e1, "idi", [128, 128], inv_dt)
            CP('pool', idi[:], idf[:])
            Hf = sbuf(e1, "Hf", [128, 4, 128])
            Hb = sbuf(e1, "Hb", [128, 4, 128], BF16)
            HfW = sbuf(e1, "HfW", [128, 4, 128])
            tmpH = T["f1"][:].rearrange("p (q t) -> p q t", q=4)
            Ub = sbuf(e1, "Ub", [128, 512], BF16)

            def compute_z(src_ap, zbuf):
                groups = [(0, 512), (512, 1024), (1024, 1536), (1536, 1792)]

                def evac(gi, pz, rs):
                    c0, c1 = groups[gi]
                    ACT(zbuf[:, c0:c1], pz, AF.Copy, scale=rs[:, 0:1])
                project_tile(src_ap, wrw, groups, evac)

            def blk(banks, h):
                return banks[h % 2][:, (h // 2) * 128:(h // 2 + 1) * 128]

            def hrows(h):
                return slice((h % 2) * 64, (h % 2) * 64 + 64)
            f2 = lambda t3, b: t3[:].rearrange("p (q b) t -> p q b t", b=2)[:, :, b, :]
            bv = lambda t2: t2[:].rearrange("p (q t) -> p q t", q=4)
            h8 = lambda ap: ap.rearrange("p (h n) -> p h n", h=8)

            Jf = sbuf(e1, "Jf", [128, 128])
            Jb = sbuf(e1, "Jb", [128, 128], BF16)
            S.dma(Jf[:], J_d[:, :])
            CP('dve', Jb[:], Jf[:])
            flg = sbuf(e1, "flg", [128, 8])
            nflg = sbuf(e1, "nflg", [128, 8])
            S.dma(flg[:], flags_d.partition_broadcast(128))
            TS('dve', nflg[:], flg[:], -1.0, 1.0, ALU.mult, ALU.add)
            Hsave = T["yF"]
            Hbk = T["gte"]
            S.dma(mS4[:], maskA_d[0, :, 0:512])
            S.dma(mI4[:], maskA_d[0, :, 512:1024])
            S.dma(mB4[:], maskA_d[0, :, 1024:1536])
            S.dma(ltri[:], ltri_d[0, :, :])
            Hf2 = Hf[:].rearrange("p q t -> p (q t)")
            Hb2 = Hb[:].rearrange("p q t -> p (q t)")

            def load_group_params(gi):
                S.dma(mu_p[:], grp_mu[gi, 0:1, :].partition_broadcast(128))
                S.dma(mu_n[:], grp_mu[gi, 1:2, :].partition_broadcast(128))
                S.dma(w0a0[0:1, :], grp_w0[gi:gi + 1, :])
                S.dma(w0a0[64:65, :], grp_a0[gi:gi + 1, :])
                S.dma(waup[0:64, :], grp_wup[gi, :, :])
                S.dma(waup[64:128, :], grp_aup[gi, :, :])

            def rwkv_pass(d, xsrc, n_slots, n_own, own_base):
                compute_z(xsrc[0, :, :], Z[0])
                compute_z(xsrc[1, :, :], Z[1])

                def stage12a(i):
                    zp, lo, loT = ZP[i % 2], LO[i % 2], LOT[i % 2]
                    compute_z(xsrc[i + 2, :, :], Z[(i + 2) % 3])
                    zc, za, zb = Z[(i + 1) % 3], Z[(i + 2) % 3], Z[i % 3]
                    zprev_src, znext_src = zb, za
                    S.dma(zp[1:128, :], zc[0:127, :])
                    S.dma(zp[0:1, :], zprev_src[127:128, :])
                    S.dma(zn[0:127, :], zc[1:128, :])
                    S.dma(zn[127:128, :], znext_src[0:1, :])
                    TT('pool', zp[:], zp[:], mu_p[:], ALU.mult)
                    TT('dve', zn[:], zn[:], mu_n[:], ALU.mult)
                    TT('dve', zp[:], zp[:], zn[:], ALU.add)
                    TT('pool', zn[:], zc[:], mu_c[:], ALU.mult)
                    TT('dve', zp[:], zp[:], zn[:], ALU.add)
                    zf = zp
                    r_, k_, v_ = zf[:, 0:512], zf[:, 512:1024], zf[:, 1024:1536]
                    _stop('S12a')
                    ACT(lo[:, 0:64], zf[:, 1536:1600], AF.Tanh)
                    CP('pool', lo[:, 64:128], zf[:, 1600:1664])

                def stage12b(i):
                    own = i >= n_slots - n_own
                    o = BS[i % 2]
                    zf, loT = ZP[i % 2], LOT[i % 2]
                    r_, k_, v_ = zf[:, 0:512], zf[:, 512:1024], zf[:, 1024:1536]
                    TR(pL1[:, 0:128], LO[i % 2][:], idf[:])
                    CP('act', loT[:], pL1[:, 0:128])
                    MM(pL0[:], loT[0:64, :], waup[0:64, :], start=True, stop=False)
                    MM(pL0[:], onesf[0:1, :], w0a0[0:1, :], start=False, stop=True)
                    MM(pL1[:], loT[64:128, :], waup[64:128, :], start=True, stop=False)
                    MM(pL1[:], onesf[64:65, :], w0a0[64:65, :], start=False, stop=True)
                    ACT(T["sgw"][:], pL0[:], AF.Sigmoid)
                    ACT(T["arate"][:], pL1[:], AF.Sigmoid)
                    MM(pL0[:], ltri[:, 0:128], T["sgw"][:])
                    MM(pL1[:], ltri[:, 128:256], T["sgw"][:])
                    for p in range(4):
                        MM(pZ[:, p:p + 1], T["sgw"][:, p * 128:(p + 1) * 128], negc[:], start=True, stop=True)
                    ACT(o["WC"][:], pZ[:, 0:4], AF.Exp)
                    if own:
                        ACT(T["Wt"][:], pL0[:], AF.Exp)
                        TT('dve', rtok[:], r_, T["Wt"][:], ALU.mult)
                    ACT(T["Winv"][:], pL0[:], AF.Exp, scale=-1.0)
                    ACT(T["We"][:], pL1[:], AF.Exp)
                    _stop('S12b')
                    TT('pool', T["kk"][:], k_, bt["k_k"][:], ALU.mult)
                    TT('pool', T["t0"][:], T["kk"][:], T["kk"][:], ALU.mult)
                    RED('dve', s8[0][:], h8(T["t0"][:]))
                    rsqrt_small(s8[0][:], s8[0][:], 1.0, 1e-12)
                    TT('dve', h8(T["kk"][:]), h8(T["kk"][:]), bc(s8[0][:].unsqueeze(2), [128, 8, 64]), ALU.mult)
                    STT('dve', T["t0"][:], T["arate"][:], -1.0, bt["k_a"][:], ALU.add, ALU.mult)
                    STT('dve', T["kd"][:], T["t0"][:], 1.0, k_, ALU.add, ALU.mult)
                    STT('dve', o["atok"][:], T["kk"][:], -1.0, T["We"][:], ALU.mult, ALU.mult)
                    TT('pool', T["t1"][:], T["kk"][:], T["arate"][:], ALU.mult)
                    TT('pool', o["btok"][:], T["t1"][:], T["Winv"][:], ALU.mult)
                    TT('pool', o["ktok"][:], T["kd"][:], T["Winv"][:], ALU.mult)
                    CP('act', o["vb"][:], v_)
                    for p in range(4):
                        TR(pT[:, p * 128:(p + 1) * 128], o["atok"][:, p * 128:(p + 1) * 128], idb[:])
                    for p in range(4):
                        TR(pT[:, 512 + p * 128:512 + (p + 1) * 128], o["btok"][:, p * 128:(p + 1) * 128], idb[:])
                    CP('act', o["arT"][:, :, 0, :], pT[:, 0:512].rearrange("p (a b) -> p a b", a=4))
                    CP('dve', o["bT"][:], pT[:, 512:1024].rearrange("p (a b) -> p a b", a=4))
                    for p in range(4):
                        TR(pT[:, p * 128:(p + 1) * 128], o["ktok"][:, p * 128:(p + 1) * 128], idb[:])
                    if own:
                        for p in range(4):
                            TR(pT[:, 512 + p * 128:512 + (p + 1) * 128], rtok[:, p * 128:(p + 1) * 128], idb[:])
                    CP('act', o["kTt"][:], pT[:, 0:512].rearrange("p (a b) -> p a b", a=4))
                    if own:
                        CP('dve', o["arT"][:, :, 1, :], pT[:, 512:1024].rearrange("p (a b) -> p a b", a=4))
                        TT('pool', T["t0"][:], r_, bt["r_k"][:], ALU.mult)
                        TT('pool', T["t0"][:], T["t0"][:], T["kd"][:], ALU.mult)
                        RED('dve', o["bon"][:], h8(T["t0"][:]))
                        if d == 1:
                            CP('pool', o["vfin"][:], v_)
                            ACT(o["sgs"][:], zf[:, 1664:1792], AF.Sigmoid)

                def stage34(i):
                    own = i >= n_slots - n_own
                    o = BS[i % 2]
                    arT, bT, kTt, atok = o["arT"], o["bT"], o["kTt"], o["atok"]
                    for h in range(8):
                        MM(blk((pA, pB), h), bT[hrows(h), h // 2, :], arT[hrows(h), h // 2, 0, :])
                    for h in range(8):
                        MM(blk((pC, pD), h), kTt[hrows(h), h // 2, :], arT[hrows(h), h // 2, 0, :])
                    for b_, bank in enumerate((pA, pB)):
                        TT('dve', f2(NN[0], b_), bv(bank), bv(mS4), ALU.mult)
                    for b_, bank in enumerate((pC, pD)):
                        TT('dve', f2(AakT, b_), bv(bank), bv(mS4), ALU.mult)
                    for h in range(8):
                        MM(blk((pA, pB), h), arT[hrows(h), h // 2, 0, :], bT[hrows(h), h // 2, :])
                    for b_, bank in enumerate((pA, pB)):
                        TT('dve', f2(NTt[0], b_), bv(bank), bv(mB4), ALU.mult)
                    if own:
                        for h in range(8):
                            MM(blk((pC, pD), h), bT[hrows(h), h // 2, :], arT[hrows(h), h // 2, 1, :])
                        for b_, bank in enumerate((pC, pD)):
                            TT('dve', f2(ArbT, b_), bv(bank), bv(mI4), ALU.mult)
                        for h in range(8):
                            MM(blk((pA, pB), h), kTt[hrows(h), h // 2, :], arT[hrows(h), h // 2, 1, :])
                        for b_, bank in enumerate((pA, pB)):
                            TT('dve', f2(ArkT, b_), bv(bank), bv(mI4), ALU.mult)
                    _stop('A')
                    TT('dve', RR[0][:], NN[0][:], bc(idi[:].unsqueeze(1), [128, 8, 128]), ALU.add)
                    for lev in range(1, 7):
                        cur, nxt = (lev - 1) % 2, lev % 2
                        last = lev == 6
                        if not last:
                            for h in range(8):
                                MM(blk((pA, pB), h), NTt[cur][:, h, :], NN[cur][:, h, :])
                        for h in range(8):
                            MM(blk((pC, pD), h), NN[cur][:, h, :], NTt[cur][:, h, :])
                        if not last:
                            CP('act', f2(NN[nxt], 0), bv(pA))
                            CP('act', f2(NN[nxt], 1), bv(pB))
                        CP('act', f2(NTt[nxt], 0), bv(pC))
                        CP('dve', f2(NTt[nxt], 1), bv(pD))
                        for h in range(8):
                            MM(blk((pA, pB), h), NTt[nxt][:, h, :], RR[cur][:, h, :])
                        TT('dve', f2(RR[nxt], 0), bv(pA), f2(RR[cur], 0), ALU.add)
                        TT('dve', f2(RR[nxt], 1), bv(pB), f2(RR[cur], 1), ALU.add)
                    _stop('B')
                    Rf = RR[0]
                    for h in range(8):
                        MM(blk((pC, pD), h), atok[:, (h // 2) * 128:(h // 2 + 1) * 128], Rf[:, h, :])
                    for hh, bank in enumerate((pC, pD)):
                        src = bank[hh * 64:(hh + 1) * 64, :].rearrange("p (q t) -> p q t", q=4)
                        CP('act' if hh == 0 else 'dve', P1T[hh * 64:(hh + 1) * 64, :, :], src)
                    for h in range(8):
                        MM(pA[:, h * 64:(h + 1) * 64], AakT[:, h, :], o["vb"][:, h * 64:(h + 1) * 64])
                    CP('act', X0a[:], pA[:])
                    for h in range(8):
                        MM(pB[:, h * 64:(h + 1) * 64], Rf[:, h, :], X0a[:, h * 64:(h + 1) * 64])
                    CP('act', U0[:], pB[:])
                    _stop('C')
                    for p in range(4):
                        MM(pC[:, p * 128:(p + 1) * 128], P1T[:, p, :], Hb[:, p, :])
                    TT('dve', Ub[:], pC[:], U0[:], ALU.add)
                    if own:
                        for h in range(8):
                            p, hh = h // 2, h % 2
                            cs_ = slice(h * 64, h * 64 + 64)
                            MM(pD[:, cs_], arT[:, p, 1, :], Hb[:, p, hh * 64:(hh + 1) * 64], start=True, stop=False)
                            MM(pD[:, cs_], ArbT[:, h, :], Ub[:, cs_], start=False, stop=False)
                            MM(pD[:, cs_], ArkT[:, h, :], o["vb"][:, cs_], start=False, stop=True)
                        CP('act', T["ydir"][:], pD[:])
                    for p in range(4):
                        pc_ = slice(p * 128, (p + 1) * 128)
                        MM(pA[:, pc_], o["btok"][:, pc_], Ub[:, pc_], start=True, stop=False)
                        MM(pA[:, pc_], o["ktok"][:, pc_], o["vb"][:, pc_], start=False, stop=True)
                    for hh in range(2):
                        rows = slice(hh * 64, hh * 64 + 64)
                        cols = slice(hh * 64, hh * 64 + 64)
                        WCb = bc(o["WC"][rows, :].unsqueeze(2), [64, 4, 64])
                        TT('dve', HfW[rows, :, cols], Hf[rows, :, cols], WCb, ALU.mult)
                        pblk = pA[rows, :].rearrange("p (q h i) -> p q h i", q=4, h=2)[:, :, hh, :]
                        TT('dve', tmpH[rows, :, cols], pblk, WCb, ALU.mult)
                        TT('dve', Hf[rows, :, cols], tmpH[rows, :, cols], HfW[rows, :, cols], ALU.add)
                        CP('act', Hb[rows, :, cols], Hf[rows, :, cols])
                    _stop('D')
                    if own:
                        if d == 0:
                            ot = own_base + i
                            S.dma(yF_s[ot, :, :], T["ydir"][:])
                            S.dma(bonF_s[ot, :, :], o["bon"][:])
                        else:
                            ot = own_base + (n_slots - 1 - i)
                            S.dma(T["yF"][:], yF_s[ot, :, :])
                            S.dma(bonF[:], bonF_s[ot, :, :])
                            TR(pB[:, 0:128], o["sgs"][:], idf[:])
                            CP('act', sgT[:], pB[:, 0:128])
                            MM(pC[:], sgT[:], gup[:])
                            CP('act', T["gte"][:], pC[:])
                            MM(pC[:], Jf[:], T["yF"][:])
                            MM(pB[:, 0:8], Jf[:], bonF[:])
                            y = T["yF"]
                            y3 = h8(y[:])
                            TT('dve', y[:], pC[:], T["ydir"][:], ALU.add)
                            RED('dve', s8[2][:], y3)
                            TS('dve', s8[2][:], s8[2][:], 1.0 / 64, None, ALU.mult)
                            TT('dve', y3, y3, bc(s8[2][:].unsqueeze(2), [128, 8, 64]), ALU.subtract)
                            TT('pool', T["f1"][:], y[:], y[:], ALU.mult)
                            RED('dve', s8[3][:], h8(T["f1"][:]))
                            rsqrt_small(s8[3][:], s8[3][:], 1.0 / 64, 64e-5)
                            TT('dve', y3, y3, bc(s8[3][:].unsqueeze(2), [128, 8, 64]), ALU.mult)
                            TT('pool', y[:], y[:], bt["ln_x_g"][:], ALU.mult)
                            TT('pool', y[:], y[:], bt["ln_x_b"][:], ALU.add)
                            TT('dve', s8[4][:], pB[:, 0:8], o["bon"][:], ALU.add)
                            TT('dve', h8(T["f1"][:]), h8(o["vfin"][:]), bc(s8[4][:].unsqueeze(2), [128, 8, 64]), ALU.mult)
                            TT('pool', y[:], y[:], T["f1"][:], ALU.add)
                            TT('dve', orw[:], y[:], T["gte"][:], ALU.mult)
                            for p in range(4):
                                MM(pD[:, p * 128:(p + 1) * 128], orw[:, p * 128:(p + 1) * 128], Jb[:])
                            CP('act', orwT[:], pD[:].rearrange("p (a b) -> p a b", a=4))
                            S.dma(mixT_s[4:8, :, ot * 128:(ot + 1) * 128].rearrange("c p t -> p c t"), orwT[:])

                stage12a(0)
                stage12b(0)
                if n_slots > 1:
                    stage12a(1)
                for i in range(n_slots):
                    S.rec_begin()
                    if i + 1 < n_slots:
                        stage12b(i + 1)
                    if i + 2 < n_slots:
                        stage12a(i + 2)
                    A_ = S.rec_end()
                    S.rec_begin()
                    stage34(i)
                    B_ = S.rec_end()
                    S.replay(S.merge(A_, B_) if not _os.environ.get("NOMERGE") else (B_ + A_))

            def set_state(src2):
                if src2 is None:
                    MSET('pool', Hf[:], 0.0)
                else:
                    CP('pool', Hf2, src2)
                CP('act', Hb2, Hf2)

            def switch_step(g):
                STT('dve', Hsave[:], Hf2, flg[:, g:g + 1], Hsave[:], ALU.mult, ALU.add)
                TS('dve', Hf2, Hf2, nflg[:, g:g + 1], None, ALU.mult)
                CP('act', Hb2, Hf2)

            MSET('pool', Hb[:], 0.0)
            MSET('pool', HfW[:], 0.0)
            MSET('pool', Hsave[:], 0.0)
            set_state(None)
            for g in range(NG):
                switch_step(g)
                load_group_params(g)
                rwkv_pass(0, xctx[g], OWN, 0, 0)
            switch_step(NG)
            CP('pool', Hbk[:], Hf2)
            set_state(Hsave[:])
            load_group_params(NG)
            rwkv_pass(0, xown[0], OWN, OWN, 0)
            set_state(None)
            rwkv_pass(0, xsmp[0], NST, NST, OWN)
            set_state(Hbk[:])
            load_group_params(NG + 1)
            rwkv_pass(1, xown[1], OWN, OWN, 0)
            set_state(None)
            rwkv_pass(1, xsmp[1], NST, NST, OWN)

        S.barrier()
        with ExitStack() as e2:
            wda = sbuf(e2, "wda", [128, 8, DA_COLS], BF16)
            wstage[0], wstage[1] = [sbuf(e2, "wstg2_%d" % i, [128, WS]) for i in range(2)]
            load_weight_bf16(wda, (0, DA_COLS), g1T)
            gq = sbuf(e2, "gq", [128, 64])
            gk = sbuf(e2, "gk", [128, 64])
            S.dma(gq[:], vec["q_norm_g"].partition_broadcast(128))
            S.dma(gk[:], vec["k_norm_g"].partition_broadcast(128))
            gsub = sbuf(e2, "gsub", [128, 128])
            S.dma(gsub[:], vec["subln_g"].partition_broadcast(128))
            TS('pool', gsub[:], gsub[:], 1.0 - LAMBDA_INIT, None, ALU.mult)
            lv = [sbuf(e2, "lv%d" % i, [128, 64]) for i in range(4)]
            for i, nm in enumerate(["lam_q1", "lam_k1", "lam_q2", "lam_k2"]):
                S.dma(lv[i][:], vec[nm].partition_broadcast(128))
            l2 = sbuf(e2, "l2", [128, 2])
            neglam = sbuf(e2, "neglam", [128, 1])
            TT('dve', lv[0][:], lv[0][:], lv[1][:], ALU.mult)
            TT('dve', lv[2][:], lv[2][:], lv[3][:], ALU.mult)
            RED('dve', l2[:, 0:1], lv[0][:])
            RED('dve', l2[:, 1:2], lv[2][:])
            ACT(l2[:], l2[:], AF.Exp)
            TT('dve', neglam[:], l2[:, 1:2], l2[:, 0:1], ALU.subtract)
            TS('dve', neglam[:], neglam[:], -LAMBDA_INIT, None, ALU.add)
            cs = sbuf(e2, "cs", [128, 64])
            zq = sbuf(e2, "zq", [128, 512])
            zk = sbuf(e2, "zk", [128, 512])
            zv = sbuf(e2, "zv", [128, 512])
            qn = sbuf(e2, "qn", [128, 512])
            u1 = sbuf(e2, "u1", [128, 256])
            u2 = sbuf(e2, "u2", [128, 256])
            rb = sbuf(e2, "rbf", [128, 512], BF16)
            s8q = sbuf(e2, "s8q", [128, 8])
            kTst = sbuf(e2, "kTst", [128, 4, 128], BF16)
            Vst = sbuf(e2, "Vst", [128, 4, 129], BF16)
            MSET('pool', Vst[:], 1.0)
            qT_P = sbuf(e2, "qT_P", [128, 4, OWN * 128], BF16)
            qT_S = sbuf(e2, "qT_S", [128, 4, NST * 128], BF16)

            def norm_rope(zsrc, g64, cstile, out_bf):
                z3 = zsrc[:].rearrange("p (h n) -> p h n", h=8)
                TT('pool', qn[:], zsrc[:], zsrc[:], ALU.mult)
                RED('dve', s8q[:], qn[:].rearrange("p (h n) -> p h n", h=8))
                rsqrt_small(s8q[:], s8q[:], 1.0 / 64, 1e-6)
                TT('dve', qn[:].rearrange("p (h n) -> p h n", h=8), z3, bc(s8q[:].unsqueeze(2), [128, 8, 64]), ALU.mult)
                TT('pool', qn[:].rearrange("p (h n) -> p h n", h=8), qn[:].rearrange("p (h n) -> p h n", h=8),
                   bc(g64[:].unsqueeze(1), [128, 8, 64]), ALU.mult)
                q4 = qn[:].rearrange("p (h c n) -> p h c n", h=8, c=2)
                o4 = out_bf[:].rearrange("p (h c n) -> p h c n", h=8, c=2)
                x1, x2 = q4[:, :, 0, :], q4[:, :, 1, :]
                cosb = bc(cstile[:, 0:32].unsqueeze(1), [128, 8, 32])
                sinb = bc(cstile[:, 32:64].unsqueeze(1), [128, 8, 32])
                a1 = u1[:].rearrange("p (h n) -> p h n", h=8)
                a2 = u2[:].rearrange("p (h n) -> p h n", h=8)
                TT('dve', a1, x1, cosb, ALU.mult)
                TT('pool', a2, x2, sinb, ALU.mult)
                TT('dve', o4[:, :, 0, :], a1, a2, ALU.subtract)
                TT('pool', a1, x2, cosb, ALU.mult)
                TT('dve', a2, x1, sinb, ALU.mult)
                TT('pool', o4[:, :, 1, :], a1, a2, ALU.add)

            ZK = [zk, sbuf(e2, "zk1", [128, 512])]
            ZV = [zv, sbuf(e2, "zv1", [128, 512])]
            CS = [cs, sbuf(e2, "cs1", [128, 64])]

            def kv_pass(xsrc, n_tiles, cs_src, kT_s, V_s):
                def stage_a(t):
                    S.dma(CS[t % 2][:], cs_src[t, :, :])

                    def evac(gi, pz, rs):
                        ACT([ZK, ZV][gi][t % 2][:], pz, AF.Copy, scale=rs[:, 0:1])
                    project_tile(xsrc[t, :, :], wda, [(512, 1024), (1024, 1536)], evac)

                def stage_b(t):
                    norm_rope(ZK[t % 2], gk, CS[t % 2], rb)
                    for p in range(4):
                        TR(K(pT[:, p * 128:(p + 1) * 128], "pT0"), rb[:, p * 128:(p + 1) * 128], idb[:])
                    CP('act', kTst[:], K(pT[:, 0:512].rearrange("p (a b) -> p a b", a=4), "pT0"))
                    S.dma(kT_s[:, :, t * 128:(t + 1) * 128].rearrange("h p t -> p h t"), kTst[:])
                    CP('act', Vst[:, :, 0:128], ZV[t % 2][:].rearrange("p (h n) -> p h n", h=4))
                    S.dma(V_s[:, :, t, :].rearrange("h p n -> p h n"), Vst[:])
                stage_a(0)
                for t in range(n_tiles):
                    S.rec_begin()
                    if t + 1 < n_tiles:
                        stage_a(t + 1)
                    A_ = S.rec_end()
                    S.rec_begin()
                    stage_b(t)
                    B_ = S.rec_end()
                    S.replay(S.merge(A_, B_))

            def q_pass(xsrc, tile0, n_tiles, cs_src, qT):
                def stage_a(t):
                    S.dma(CS[t % 2][:], cs_src[t, :, :])

                    def evac(gi, pz, rs):
                        ACT(ZK[t % 2][:], pz, AF.Copy, scale=rs[:, 0:1])
                    project_tile(xsrc[tile0 + t, :, :], wda, [(0, 512)], evac)

                def stage_b(t):
                    norm_rope(ZK[t % 2], gq, CS[t % 2], rb)
                    for p in range(4):
                        TR(K(pT[:, p * 128:(p + 1) * 128], "pT0"), rb[:, p * 128:(p + 1) * 128], idb[:])
                    CP('act', qT[:, :, t * 128:(t + 1) * 128], K(pT[:, 0:512].rearrange("p (a b) -> p a b", a=4), "pT0"))
                stage_a(0)
                for t in range(n_tiles):
                    S.rec_begin()
                    if t + 1 < n_tiles:
                        stage_a(t + 1)
                    A_ = S.rec_end()
                    S.rec_begin()
                    stage_b(t)
                    B_ = S.rec_end()
                    S.replay(S.merge(A_, B_))

            kv_pass(xP, NPT, csP, kT_sP, V_sP)
            kv_pass(xsmp[0, 1:NST + 1], NST, csP, kT_sS, V_sS)
            q_pass(xown[0], 1, OWN, csQ, qT_P)
            q_pass(xsmp[0], 1, NST, csP, qT_S)

            PTb = [sbuf(e2, "PTb%d" % i, [128, 512], BF16) for i in range(2)]
            o0 = sbuf(e2, "o0", [128, 128])
            o1 = sbuf(e2, "o1", [128, 128])
            rc = sbuf(e2, "rc", [128, 2])
            ssd = sbuf(e2, "ssd", [128, 1])
            ob = sbuf(e2, "ob", [128, 128], BF16)
            oT = sbuf(e2, "oT", [128, 128], BF16)
            pO = [[pA, pB], [pC, pD]]
            pS = [pL0, pL1]

            osv = [sbuf(e2, "osv%d" % j, [128, 128]) for j in range(4)]
            pOb = [pA, pB, pC, pD]

            def attention(qT, n_q_tiles, kT_s, V_s, n_kt, tok_base):
                kTh = sbuf(e2a, "kTh_%d" % tok_base, [128, n_kt * 128], BF16)
                Vh = sbuf(e2a, "Vh_%d" % tok_base, [128, n_kt, 129], BF16)
                cnt = 0
                for h in range(4):
                    S.dma(kTh[:], kT_s[h, :, :])
                    S.dma(Vh[:], V_s[h, :, :, :])
                    for qg in range(0, n_q_tiles, 4):
                        nq = min(4, n_q_tiles - qg)
                        for c in range(2):
                            rows = slice(c * 64, c * 64 + 64)
                            def qk(kt_, slot):
                                MM(pS[slot % 2][:, 0:nq * 128], kTh[rows, kt_ * 128:(kt_ + 1) * 128], qT[rows, h, qg * 128:(qg + nq) * 128])
                            qk(0, cnt)
                            for kt in range(n_kt):
                                ps_ = pS[cnt % 2]
                                pt_ = PTb[cnt % 2]
                                if kt + 1 < n_kt:
                                    qk(kt + 1, cnt + 1)
                                cnt += 1
                                ACT(pt_[:, 0:nq * 128], ps_[:, 0:nq * 128], AF.Exp, scale=0.125)
                                for j in range(nq):
                                    MM(pOb[j][:, 0:129], pt_[:, j * 128:(j + 1) * 128], Vh[:, kt, :], start=(kt == 0), stop=(kt == n_kt - 1))
                            for j in range(nq):
                                RECIP(rc[:, c:c + 1], pOb[j][:, 128:129])
                                if c == 0:
                                    ACT(osv[j][:], pOb[j][:, 0:128], AF.Copy, scale=rc[:, 0:1])
                                else:
                                    TT('dve', rc[:, 1:2], rc[:, 1:2], neglam[:], ALU.mult)
                                    STT('dve', o1[:], pOb[j][:, 0:128], rc[:, 1:2], osv[j][:], ALU.mult, ALU.add)
                                    ACT(o0[:], o1[:], AF.Square, accum=ssd[:])
                                    rsqrt_small(ssd[:], ssd[:], 1.0 / 128, 1e-6)
                                    STT('dve', ob[:], o1[:], ssd[:, 0:1], gsub[:], ALU.mult, ALU.mult)
                                    TR(pT[:, 0:128], ob[:], idb[:])
                                    CP('act', oT[:], pT[:, 0:128])
                                    tk = tok_base + (qg + j) * 128
                                    S.dma(mixT_s[h, :, tk:tk + 128], oT[:])

            S.barrier()
            wcast2 = [sbuf(e2, "wcast2_%d" % i, [128, WS], BF16) for i in range(2)]

            def precast_w1():
                wc_ = 0
                for kc in range(8):
                    for s0 in range(0, DFF, WS):
                        ws = wstage[xcount[0] % 2]
                        xcount[0] += 1
                        wcb = wcast2[wc_ % 2]
                        wc_ += 1
                        S.dma(ws[:, 0:WS], w_ff1[kc * 128:(kc + 1) * 128, s0:s0 + WS])
                        TS('pool', wcb[:], ws[:, 0:WS], g2T[:, kc:kc + 1], None, ALU.mult)
                        S.dma(w1b_s[kc, :, s0:s0 + WS], wcb[:])
            with ExitStack() as e2a:
                S.rec_begin()
                precast_w1()
                A_ = S.rec_end()
                S.rec_begin()
                attention(qT_P, OWN, kT_sP, V_sP, NPT, 0)
                B_ = S.rec_end()
                S.replay(S.merge(A_, B_))
            S.barrier()
            with ExitStack() as e2a:
                attention(qT_S, NST, kT_sS, V_sS, NST, OWN * 128)

        S.barrier()
        with ExitStack() as e3:
            wo = sbuf(e3, "wo", [128, 8, D], BF16)
            w2 = sbuf(e3, "w2", [128, 32, D], BF16)
            w1blk = [sbuf(e3, "w1blk%d" % i, [128, 8, 512], BF16) for i in range(2)]
            wcast = [sbuf(e3, "wcast%d" % i, [128, WS], BF16) for i in range(2)]
            wstage[0], wstage[1] = [sbuf(e3, "wstg3_%d" % i, [128, WS]) for i in range(2)]
            wc = 0
            for kc in range(8):
                for s0 in range(0, D, WS):
                    ws = wstage[xcount[0] % 2]
                    xcount[0] += 1
                    S.dma(ws[:, 0:WS], w_out[kc * 128:(kc + 1) * 128, s0:s0 + WS])
                    CP('pool', wo[:, kc, s0:s0 + WS], ws[:, 0:WS])
            for fc in range(32):
                for s0 in range(0, D, WS):
                    ws = wstage[xcount[0] % 2]
                    xcount[0] += 1
                    S.dma(ws[:, 0:WS], w_ff2[fc * 128:(fc + 1) * 128, s0:s0 + WS])
                    CP('pool', w2[:, fc, s0:s0 + WS], ws[:, 0:WS])
            GT = 4
            mixg = sbuf(e3, "mixg", [128, 8, GT * 128], BF16)
            xmid = [sbuf(e3, "xmid%d" % j, [128, D]) for j in range(GT)]
            rs2 = sbuf(e3, "rs2", [128, GT])
            xmb = sbuf(e3, "xmb", [128, D], BF16)
            xmT = sbuf(e3, "xmT", [128, 8, GT * 128], BF16)
            uT = sbuf(e3, "uT", [128, 32, GT * 128], BF16)
            rl = [sbuf(e3, "rl%d" % i, [128, GT * 128]) for i in range(2)]
            yo = sbuf(e3, "yo", [128, D])
            own_src = [(xown[0], 1 + t, out_p, t) for t in range(OWN)] + [(xsmp[0], 1 + t, out_s, t) for t in range(NST)]
            wbc = 0
            for g0 in range(0, NOWN, GT):
                ng = min(GT, NOWN - g0)
                nt = ng * 128
                S.dma(mixg[:, :, 0:nt], mixT_s[:, :, g0 * 128:g0 * 128 + nt].rearrange("c p t -> p c t"))
                for j in range(ng):
                    src, sidx, _, _ = own_src[g0 + j]
                    xr = xt[xcount[0] % 2]
                    xcount[0] += 1
                    S.dma(xr[:], src[sidx, :, :])
                    for hf in range(2):
                        pz = [pZ, pL0][hf]
                        for ch in range(8):
                            MM(pz[:], mixg[:, ch, j * 128:(j + 1) * 128], wo[:, ch, hf * 512:(hf + 1) * 512], start=(ch == 0), stop=(ch == 7))
                        TT('dve', xmid[j][:, hf * 512:(hf + 1) * 512], pz[:], xr[:, hf * 512:(hf + 1) * 512], ALU.add)
                    ACT(junk[:], xmid[j][:], AF.Square, accum=ssq[:])
                    rsqrt_small(rs2[:, j:j + 1], ssq[:], 1.0 / D, 1e-6)
                    CP('pool', xmb[:], xmid[j][:])
                    for hlf in range(2):
                        for k4 in range(4):
                            kc = hlf * 4 + k4
                            TR(K(pT[:, hlf * 512 + k4 * 128: hlf * 512 + (k4 + 1) * 128], "pT%d" % hlf), xmb[:, kc * 128:(kc + 1) * 128], idb[:])
                        CP('act' if hlf == 0 else 'dve', xmT[:, hlf * 4:(hlf + 1) * 4, j * 128:(j + 1) * 128],
                           K(pT[:, hlf * 512:(hlf + 1) * 512].rearrange("p (a b) -> p a b", a=4), "pT%d" % hlf))
                    TT('dve', rs2[:, j:j + 1], rs2[:, j:j + 1], rs2[:, j:j + 1], ALU.mult)
                for fb in range(8):
                    wb_ = w1blk[wbc % 2]
                    wbc += 1
                    S.dma(wb_[:], w1b_s[:, :, fb * 512:(fb + 1) * 512].rearrange("k p f -> p k f"))
                    for f4 in range(4):
                        fc = fb * 4 + f4
                        pf = [pA, pB, pC, pD][fc % 4]
                        for kc in range(8):
                            MM(pf[:, 0:nt], wb_[:, kc, f4 * 128:(f4 + 1) * 128], xmT[:, kc, 0:nt], start=(kc == 0), stop=(kc == 7))
                        r_ = rl[fc % 2]
                        ACT(r_[:, 0:nt], pf[:, 0:nt], AF.Relu)
                        TT('pool' if fc % 2 else 'dve', uT[:, fc, 0:nt], r_[:, 0:nt], r_[:, 0:nt], ALU.mult)
                for j in range(ng):
                    _, _, dst, didx = own_src[g0 + j]
                    for hf in range(2):
                        pz = [pL1, pZ][hf]
                        for fc in range(32):
                            MM(pz[:], uT[:, fc, j * 128:(j + 1) * 128], w2[:, fc, hf * 512:(hf + 1) * 512], start=(fc == 0), stop=(fc == 31))
                        STT('dve', yo[:, hf * 512:(hf + 1) * 512], pz[:], rs2[:, j:j + 1], xmid[j][:, hf * 512:(hf + 1) * 512], ALU.mult, ALU.add)
                    S.dma(dst[didx, :, :], yo[:])
        S.finish('sp')
        S.finish('pool')
        build_nc.ninst = S.ninst
    return nc


def _rope_tables(n_pos):
    inv_freq = (1.0 / (10000.0 ** (np.arange(0, 64, 2, dtype=np.float32) / np.float32(64)))).astype(np.float32)
    ang = np.arange(n_pos, dtype=np.float32)[:, None] * inv_freq[None, :]
    return np.concatenate([np.cos(ang), np.sin(ang)], axis=-1).astype(np.float32)


def _consts():
    s = np.arange(128)[:, None]
    t = np.arange(128)[None, :]
    maskA = np.zeros((2, 128, 1536), np.float32)
    ltri = np.zeros((2, 128, 256), np.float32)
    for d in range(2):
        strict = (s < t) if d == 0 else (s > t)
        incl = (s <= t) if d == 0 else (s >= t)
        maskA[d] = np.concatenate([strict] * 4 + [incl] * 4 + [strict.T] * 4, axis=1).astype(np.float32)
        ltri[d, :, 0:128] = -CDEC * incl
        ltri[d, :, 128:256] = -CDEC * strict
    return maskA, ltri


_NC_CACHE = {}


def kernel(**inputs):
    f32 = lambda a: np.ascontiguousarray(np.asarray(a, dtype=np.float32))
    x_prompt = f32(inputs["x_prompt"])
    x_sample = f32(inputs["x_sample"])
    SEQ = x_prompt.shape[1]
    DSEQ = x_sample.shape[1]
    NPT, NST = SEQ // 128, DSEQ // 128
    OWN = NPT // NCORES
    key = (NPT, NST)
    if key not in _NC_CACHE:
        _NC_CACHE[key] = build_nc(NPT, NST)
    nc = _NC_CACHE[key]
    xp = x_prompt[0].reshape(NPT, 128, D)
    ztile = np.zeros((1, 128, D), np.float32)
    cs_all = _rope_tables(max(SEQ, DSEQ)).reshape(-1, 128, 64)
    maskA, ltri = _consts()
    shared = {
        "xP": xp, "csP": cs_all[:NPT],
        "w_in": f32(inputs["w_in"][0]), "w_out": f32(inputs["w_out"][0]),
        "w_ff1": f32(inputs["w_ff1"][0]), "w_ff2": f32(inputs["w_ff2"][0]),
        "norm1_g": f32(f32(inputs["norm1_g"][0]).reshape(8, 128).T), "norm2_g": f32(f32(inputs["norm2_g"][0]).reshape(8, 128).T),
        "w0": f32(inputs["w0"][0]), "a0": f32(inputs["a0"][0]),
        "w_up": f32(inputs["w_up"][0]), "a_up": f32(inputs["a_up"][0]), "g_up": f32(inputs["g_up"][0]),
        "ident_d": np.eye(128, dtype=np.float32), "maskA_d": maskA, "ltri_d": ltri,
    }
    for nm in ["q_norm_g", "k_norm_g", "lam_q1", "lam_k1", "lam_q2", "lam_k2", "subln_g", "mu_prev",
               "mu_next", "k_k", "k_a", "r_k", "ln_x_g", "ln_x_b"]:
        shared[nm] = f32(inputs[nm][0]).reshape(1, -1)
    xtok = x_prompt[0]
    SEGT = OWN * 128
    NG = NCORES - 1

    def seg_halo(tok, s, segt):
        n = tok.shape[0]
        out = np.zeros((segt + 256, D), np.float32)
        lo, hi = s * segt - 128, (s + 1) * segt + 128
        a, b = max(lo, 0), min(hi, n)
        out[a - lo:b - lo] = tok[a:b]
        return out

    mu_p = f32(inputs["mu_prev"][0]); mu_n = f32(inputs["mu_next"][0])
    dir_par = []
    for d in range(2):
        dir_par.append(dict(mu=np.stack([mu_p, mu_n] if d == 0 else [mu_n, mu_p]),
                            w0=f32(inputs["w0"][0, d]), a0=f32(inputs["a0"][0, d]),
                            wup=f32(inputs["w_up"][0, d]), aup=f32(inputs["a_up"][0, d])))
    Jm = np.ascontiguousarray(np.eye(128, dtype=np.float32)[::-1])
    in_maps = []
    for c in range(NCORES):
        m = dict(shared)
        groups, dirs = [], []
        for g in range(NG):
            if g < c:
                groups.append(seg_halo(xtok, g, SEGT)); dirs.append(0)
            else:
                groups.append(seg_halo(xtok, NG + c - g, SEGT)[::-1]); dirs.append(1)
        m["xctx"] = np.ascontiguousarray(np.stack(groups)).reshape(NG, OWN + 2, 128, D)
        own = seg_halo(xtok, c, SEGT)
        m["xown"] = np.ascontiguousarray(np.stack([own, own[::-1]])).reshape(2, OWN + 2, 128, D)
        smp = seg_halo(x_sample[c], 0, DSEQ)
        m["xsmp"] = np.ascontiguousarray(np.stack([smp, smp[::-1]])).reshape(2, NST + 2, 128, D)
        dirs = dirs + [0, 1]
        m["grp_mu"] = np.ascontiguousarray(np.stack([dir_par[d]["mu"] for d in dirs]))
        m["grp_w0"] = np.ascontiguousarray(np.stack([dir_par[d]["w0"] for d in dirs]))
        m["grp_a0"] = np.ascontiguousarray(np.stack([dir_par[d]["a0"] for d in dirs]))
        m["grp_wup"] = np.ascontiguousarray(np.stack([dir_par[d]["wup"] for d in dirs]))
        m["grp_aup"] = np.ascontiguousarray(np.stack([dir_par[d]["aup"] for d in dirs]))
        fl = np.zeros((1, 8), np.float32); fl[0, c] = 1.0
        m["flags_d"] = fl
        m["J_d"] = Jm
        m["csQ"] = np.ascontiguousarray(cs_all[OWN * c:OWN * (c + 1)])
        in_maps.append(m)
    if inputs.get("_maps_only"):
        return nc, in_maps
    res = run_bass_kernel_spmd(nc, in_maps, core_ids=list(range(NCORES)))
    y_prompt = np.concatenate([res.results[c]["out_p"].reshape(OWN * 128, D) for c in range(NCORES)], axis=0)[None]
    y_sample = np.stack([res.results[c]["out_s"].reshape(NST * 128, D) for c in range(NCORES)], axis=0)
    return (y_prompt.astype(np.float32), y_sample.astype(np.float32))
```
